# Optimizing a Trainium2 kernel written in Bass

```python
import math
import jax, jax.numpy as jnp
from jax import lax
import numpy as np

D_MODEL = 2048
BATCH = 16
SEQ = 2048
DEPTH = 2

HEAD_DIM = 64
BLOCK = 128
LRU_WIDTH = 1024
LRU_BLOCKS = 16
LRU_CONV = 4
LRU_C = 8.0
SWA_HEADS = 16
SWA_KV_HEADS = 4
SWA_WINDOW = 128
S5_WIDTH = 768
S5_GROUP = 16
S5_GROUPS = S5_WIDTH // S5_GROUP
S5_STATE = 64
DIL_CONFIGS = ((128, 1), (512, 4), (2048, 16))
DIL_HEADS = 8
DIL_KV_HEADS = 4
D_FF = 5632
FFN_CONV = 3
ALPHA = (2 * DEPTH) ** 0.25
BETA = (8 * DEPTH) ** -0.25
LN_EPS = 1e-5

SWA_Q = SWA_HEADS * HEAD_DIM
SWA_KV = SWA_KV_HEADS * HEAD_DIM
DIL_Q = len(DIL_CONFIGS) * DIL_HEADS * HEAD_DIM
DIL_KV = DIL_KV_HEADS * HEAD_DIM
EVEN_IN = 2 * LRU_WIDTH + SWA_Q + 2 * SWA_KV
EVEN_OUT = LRU_WIDTH + SWA_Q
ODD_IN = S5_WIDTH + DIL_Q + 2 * DIL_KV
ODD_OUT = S5_WIDTH + DIL_HEADS * HEAD_DIM

kernel_name = 'hybrid_rglru_swa_s5_dilated_deepnorm'


def layer_norm(x, g, b):
    xf = x.astype(jnp.float32)
    mu = xf.mean(-1, keepdims=True)
    var = jnp.square(xf - mu).mean(-1, keepdims=True)
    y = (xf - mu) * lax.rsqrt(var + LN_EPS) * g.astype(jnp.float32) + b.astype(jnp.float32)
    return y.astype(x.dtype)


def causal_depthwise_conv(x, w, b):
    k, c = w.shape
    y = lax.conv_general_dilated(x, w[:, None, :].astype(x.dtype), window_strides=(1,),
                                 padding=((k - 1, 0),), dimension_numbers=('NWC', 'WIO', 'NWC'),
                                 feature_group_count=c)
    return y + b


def linear_scan(a, b):
    def combine(e1, e2):
        a1, b1 = e1
        a2, b2 = e2
        return a1 * a2, a2 * b1 + b2
    return lax.associative_scan(combine, (a, b), axis=1)[1]


def rg_lru(x, gx_w, gx_b, ga_w, ga_b, lru_L):
    bsz, seq, width = x.shape
    xf = x.astype(jnp.float32)
    xb = xf.reshape(bsz, seq, LRU_BLOCKS, width // LRU_BLOCKS)

    def block_diag(w, b):
        return jnp.einsum('bsnc,ncd->bsnd', xb, w.astype(jnp.float32)).reshape(bsz, seq, width) + b.astype(jnp.float32)

    i_gate = jax.nn.sigmoid(block_diag(gx_w, gx_b))
    r_gate = jax.nn.sigmoid(block_diag(ga_w, ga_b))
    log_a = -LRU_C * r_gate * jax.nn.softplus(-lru_L.astype(jnp.float32))
    a = jnp.exp(log_a)
    b = jnp.sqrt(-jnp.expm1(2.0 * log_a)) * (i_gate * xf)
    return linear_scan(a, b).astype(x.dtype)


def banded_attention(q, k, v, max_dist, sinks=None):
    bt, n, hk, g, e = q.shape
    nb = n // BLOCK
    qb = q.reshape(bt, nb, BLOCK, hk, g, e)

    def with_prev(t):
        tb = t.reshape(bt, nb, BLOCK, hk, e)
        prev = jnp.pad(tb[:, :-1], ((0, 0), (1, 0), (0, 0), (0, 0), (0, 0)))
        return jnp.concatenate([prev, tb], axis=2)

    kb, vb = with_prev(k), with_prev(v)
    s = jnp.einsum('bnqhge,bnkhe->bnhgqk', qb, kb).astype(jnp.float32) * (e ** -0.5)
    qi = jnp.arange(BLOCK)[:, None]
    kj = jnp.arange(2 * BLOCK)[None, :]
    dist = qi + BLOCK - kj
    key_pos = jnp.arange(nb)[:, None, None] * BLOCK + kj - BLOCK
    mask = (dist >= 0) & (dist <= max_dist) & (key_pos >= 0)
    s = jnp.where(mask[None, :, None, None], s, -jnp.inf)
    m = s.max(-1)
    if sinks is not None:
        sk = sinks.astype(jnp.float32)[None, None, :, :, None]
        m = jnp.maximum(m, sk)
    p = jnp.exp(s - m[..., None])
    denom = p.sum(-1)
    if sinks is not None:
        denom = denom + jnp.exp(sk - m)
    o = jnp.einsum('bnhgqk,bnkhe->bnqhge', p, vb.astype(jnp.float32))
    o = o / jnp.moveaxis(denom, -1, 2)[..., None]
    lse = jnp.moveaxis(m + jnp.log(denom), -1, 2).reshape(bt, n, hk, g)
    return o.reshape(bt, n, hk, g, e).astype(q.dtype), lse


def dilated_attention(q, k, v, window, dilation):
    bsz, seq = q.shape[:2]
    span = dilation * BLOCK
    seq_p = -(-seq // span) * span
    m = seq_p // dilation

    def to_sub(t):
        rest = t.shape[2:]
        t = jnp.pad(t, ((0, 0), (0, seq_p - seq)) + ((0, 0),) * len(rest))
        t = t.reshape((bsz, m, dilation) + rest)
        return jnp.moveaxis(t, 2, 1).reshape((bsz * dilation, m) + rest)

    def from_sub(t):
        rest = t.shape[2:]
        t = jnp.moveaxis(t.reshape((bsz, dilation, m) + rest), 1, 2)
        return t.reshape((bsz, seq_p) + rest)[:, :seq]

    o, lse = banded_attention(to_sub(q), to_sub(k), to_sub(v), window // dilation)
    return from_sub(o), from_sub(lse)


def s5_mixer(u, A_re, A_im, log_dt, B_re, B_im, C_re, C_im, D_skip, glu_w, glu_b):
    bsz, seq, _ = u.shape
    f32 = jnp.float32
    lam = lax.complex(A_re.astype(f32), A_im.astype(f32))
    dt = jnp.exp(log_dt.astype(f32))[:, None]
    lam_bar = jnp.exp(lam * dt)
    b_bar = ((lam_bar - 1.0) / lam)[:, :, None] * lax.complex(B_re.astype(f32), B_im.astype(f32))
    uf = u.astype(f32)
    ug = uf.reshape(bsz, seq, S5_GROUPS, S5_GROUP)
    bu = lax.complex(jnp.einsum('bsgc,gpc->bsgp', ug, b_bar.real),
                     jnp.einsum('bsgc,gpc->bsgp', ug, b_bar.imag))
    a = jnp.broadcast_to(lam_bar, (1, seq) + lam_bar.shape)
    state = linear_scan(a, bu)
    y = (jnp.einsum('bsgp,gcp->bsgc', state.real, C_re.astype(f32))
         - jnp.einsum('bsgp,gcp->bsgc', state.imag, C_im.astype(f32)))
    y = y.reshape(bsz, seq, S5_WIDTH) + D_skip.astype(f32) * uf
    z = jax.nn.gelu(y)
    return (z * jax.nn.sigmoid(z @ glu_w.astype(f32) + glu_b.astype(f32))).astype(u.dtype)


def conv_ffn(x, w_up, conv_w, conv_b, w_down):
    h = causal_depthwise_conv(x @ w_up, conv_w, conv_b)
    gate, val = h[..., :D_FF], h[..., D_FF:]
    return (jax.nn.silu(gate) * val) @ w_down


def even_layer(x, w_in, conv_w, conv_b, gx_w, gx_b, ga_w, ga_b, lru_L, sinks, w_out,
               ln1_g, ln1_b, ffn_up, ffn_conv_w, ffn_conv_b, ffn_down, ln2_g, ln2_b):
    bsz, seq, _ = x.shape
    h = x @ w_in
    o1 = LRU_WIDTH
    o2 = 2 * LRU_WIDTH
    o3 = o2 + SWA_Q
    o4 = o3 + SWA_KV
    xa, ga, q, k, v = h[..., :o1], h[..., o1:o2], h[..., o2:o3], h[..., o3:o4], h[..., o4:]
    ya = rg_lru(causal_depthwise_conv(xa, conv_w, conv_b), gx_w, gx_b, ga_w, ga_b, lru_L) * jax.nn.gelu(ga)
    g = SWA_HEADS // SWA_KV_HEADS
    ob, _ = banded_attention(q.reshape(bsz, seq, SWA_KV_HEADS, g, HEAD_DIM),
                             k.reshape(bsz, seq, SWA_KV_HEADS, HEAD_DIM),
                             v.reshape(bsz, seq, SWA_KV_HEADS, HEAD_DIM),
                             SWA_WINDOW - 1, sinks.reshape(SWA_KV_HEADS, g))
    mix = jnp.concatenate([ya, ob.reshape(bsz, seq, SWA_Q)], axis=-1) @ w_out
    x = layer_norm(ALPHA * x + mix, ln1_g, ln1_b)
    return layer_norm(ALPHA * x + conv_ffn(x, ffn_up, ffn_conv_w, ffn_conv_b, ffn_down), ln2_g, ln2_b)


def odd_layer(x, w_in, A_re, A_im, log_dt, B_re, B_im, C_re, C_im, D_skip, glu_w, glu_b, w_out,
              ln1_g, ln1_b, ffn_up, ffn_conv_w, ffn_conv_b, ffn_down, ln2_g, ln2_b):
    bsz, seq, _ = x.shape
    h = x @ w_in
    o1 = S5_WIDTH
    o2 = o1 + DIL_Q
    o3 = o2 + DIL_KV
    u, q, k, v = h[..., :o1], h[..., o1:o2], h[..., o2:o3], h[..., o3:]
    yc = s5_mixer(u, A_re, A_im, log_dt, B_re, B_im, C_re, C_im, D_skip, glu_w, glu_b)
    g = DIL_HEADS // DIL_KV_HEADS
    q = q.reshape(bsz, seq, len(DIL_CONFIGS), DIL_KV_HEADS, g, HEAD_DIM)
    k = k.reshape(bsz, seq, DIL_KV_HEADS, HEAD_DIM)
    v = v.reshape(bsz, seq, DIL_KV_HEADS, HEAD_DIM)
    outs, lses = [], []
    for r, (window, dilation) in enumerate(DIL_CONFIGS):
        o, l = dilated_attention(q[:, :, r], k, v, window, dilation)
        outs.append(o)
        lses.append(l)
    wts = jax.nn.softmax(jnp.stack(lses), axis=0)
    yd = (wts[..., None] * jnp.stack(outs).astype(jnp.float32)).sum(0)
    yd = yd.astype(x.dtype).reshape(bsz, seq, DIL_HEADS * HEAD_DIM)
    mix = jnp.concatenate([yc, yd], axis=-1) @ w_out
    x = layer_norm(ALPHA * x + mix, ln1_g, ln1_b)
    return layer_norm(ALPHA * x + conv_ffn(x, ffn_up, ffn_conv_w, ffn_conv_b, ffn_down), ln2_g, ln2_b)


def setup_inputs(seed: int = 0) -> dict:
    key = jax.random.key(seed)
    ks = iter(jax.random.split(key, 64))
    f32 = jnp.float32

    def nrm(shape, scale):
        return scale * jax.random.normal(next(ks), shape, f32)

    def gain(n):
        return 1.0 + nrm((n,), 0.01)

    def bias(n):
        return nrm((n,), 0.01)

    blk = LRU_WIDTH // LRU_BLOCKS
    a_c = jax.random.uniform(next(ks), (LRU_WIDTH,), f32, 0.9, 0.999)
    s_L = a_c ** (1.0 / LRU_C)
    log_dt = jax.random.uniform(next(ks), (S5_GROUPS,), f32, math.log(1e-3), math.log(1e-1))
    inp = {}
    inp['x'] = nrm((BATCH, SEQ, D_MODEL), 1.0)
    inp['l0_w_in'] = nrm((D_MODEL, EVEN_IN), D_MODEL ** -0.5)
    inp['l0_lru_conv_w'] = nrm((LRU_CONV, LRU_WIDTH), LRU_CONV ** -0.5)
    inp['l0_lru_conv_b'] = bias(LRU_WIDTH)
    inp['l0_lru_gx_w'] = nrm((LRU_BLOCKS, blk, blk), blk ** -0.5)
    inp['l0_lru_gx_b'] = bias(LRU_WIDTH)
    inp['l0_lru_ga_w'] = nrm((LRU_BLOCKS, blk, blk), blk ** -0.5)
    inp['l0_lru_ga_b'] = bias(LRU_WIDTH)
    inp['l0_lru_L'] = jnp.log(s_L) - jnp.log1p(-s_L)
    inp['l0_sinks'] = nrm((SWA_HEADS,), 0.5)
    inp['l0_w_out'] = nrm((EVEN_OUT, D_MODEL), BETA * EVEN_OUT ** -0.5)
    inp['l0_ln1_g'] = gain(D_MODEL)
    inp['l0_ln1_b'] = bias(D_MODEL)
    inp['l0_ffn_up'] = nrm((D_MODEL, 2 * D_FF), D_MODEL ** -0.5)
    inp['l0_ffn_conv_w'] = nrm((FFN_CONV, 2 * D_FF), FFN_CONV ** -0.5)
    inp['l0_ffn_conv_b'] = bias(2 * D_FF)
    inp['l0_ffn_down'] = nrm((D_FF, D_MODEL), BETA * D_FF ** -0.5)
    inp['l0_ln2_g'] = gain(D_MODEL)
    inp['l0_ln2_b'] = bias(D_MODEL)
    inp['l1_w_in'] = nrm((D_MODEL, ODD_IN), D_MODEL ** -0.5)
    inp['l1_s5_A_re'] = -0.5 + nrm((S5_GROUPS, S5_STATE), 0.01)
    inp['l1_s5_A_im'] = jnp.tile(math.pi * jnp.arange(S5_STATE, dtype=f32), (S5_GROUPS, 1))
    inp['l1_s5_log_dt'] = log_dt
    inp['l1_s5_B_re'] = nrm((S5_GROUPS, S5_STATE, S5_GROUP), (2.0 * S5_GROUP) ** -0.5)
    inp['l1_s5_B_im'] = nrm((S5_GROUPS, S5_STATE, S5_GROUP), (2.0 * S5_GROUP) ** -0.5)
    inp['l1_s5_C_re'] = nrm((S5_GROUPS, S5_GROUP, S5_STATE), (2.0 * S5_STATE) ** -0.5)
    inp['l1_s5_C_im'] = nrm((S5_GROUPS, S5_GROUP, S5_STATE), (2.0 * S5_STATE) ** -0.5)
    inp['l1_s5_D'] = nrm((S5_WIDTH,), 0.5)
    inp['l1_glu_w'] = nrm((S5_WIDTH, S5_WIDTH), S5_WIDTH ** -0.5)
    inp['l1_glu_b'] = bias(S5_WIDTH)
    inp['l1_w_out'] = nrm((ODD_OUT, D_MODEL), BETA * ODD_OUT ** -0.5)
    inp['l1_ln1_g'] = gain(D_MODEL)
    inp['l1_ln1_b'] = bias(D_MODEL)
    inp['l1_ffn_up'] = nrm((D_MODEL, 2 * D_FF), D_MODEL ** -0.5)
    inp['l1_ffn_conv_w'] = nrm((FFN_CONV, 2 * D_FF), FFN_CONV ** -0.5)
    inp['l1_ffn_conv_b'] = bias(2 * D_FF)
    inp['l1_ffn_down'] = nrm((D_FF, D_MODEL), BETA * D_FF ** -0.5)
    inp['l1_ln2_g'] = gain(D_MODEL)
    inp['l1_ln2_b'] = bias(D_MODEL)
    return inp


def reference(x,
              l0_w_in, l0_lru_conv_w, l0_lru_conv_b, l0_lru_gx_w, l0_lru_gx_b, l0_lru_ga_w, l0_lru_ga_b,
              l0_lru_L, l0_sinks, l0_w_out, l0_ln1_g, l0_ln1_b, l0_ffn_up, l0_ffn_conv_w, l0_ffn_conv_b,
              l0_ffn_down, l0_ln2_g, l0_ln2_b,
              l1_w_in, l1_s5_A_re, l1_s5_A_im, l1_s5_log_dt, l1_s5_B_re, l1_s5_B_im, l1_s5_C_re, l1_s5_C_im,
              l1_s5_D, l1_glu_w, l1_glu_b, l1_w_out, l1_ln1_g, l1_ln1_b, l1_ffn_up, l1_ffn_conv_w,
              l1_ffn_conv_b, l1_ffn_down, l1_ln2_g, l1_ln2_b):
    even_params = (l0_w_in, l0_lru_conv_w, l0_lru_conv_b, l0_lru_gx_w, l0_lru_gx_b, l0_lru_ga_w, l0_lru_ga_b,
                   l0_lru_L, l0_sinks, l0_w_out, l0_ln1_g, l0_ln1_b, l0_ffn_up, l0_ffn_conv_w, l0_ffn_conv_b,
                   l0_ffn_down, l0_ln2_g, l0_ln2_b)
    odd_params = (l1_w_in, l1_s5_A_re, l1_s5_A_im, l1_s5_log_dt, l1_s5_B_re, l1_s5_B_im, l1_s5_C_re, l1_s5_C_im,
                  l1_s5_D, l1_glu_w, l1_glu_b, l1_w_out, l1_ln1_g, l1_ln1_b, l1_ffn_up, l1_ffn_conv_w,
                  l1_ffn_conv_b, l1_ffn_down, l1_ln2_g, l1_ln2_b)
    for layer in range(DEPTH):
        if layer % 2 == 0:
            x = even_layer(x, *even_params)
        else:
            x = odd_layer(x, *odd_params)
    return x
```

```python
import contextlib
import math
import numpy as np
import concourse.bass as bass
import concourse.mybir as mybir
from concourse.bass_utils import run_bass_kernel_spmd

F32 = mybir.dt.float32
BF16 = mybir.dt.bfloat16
I32 = mybir.dt.int32
AF = mybir.ActivationFunctionType
ALU = mybir.AluOpType
AX = mybir.AxisListType

D = 2048
SEQ = 2048
T = 512
DFF = 5632
ALPHA = 4.0 ** 0.25
LN_EPS = 1e-5
NEG = -30000.0
TWO_PI = 2.0 * math.pi
PI_SAFE = 3.1415925
SIN_SCALE = 1.0 - 2e-6

ENGS = ["tensor", "vector", "scalar", "gpsimd", "sync"]
EPOCH = 24000


class Op:
    __slots__ = ("eng", "fn", "deps", "idx", "signal", "ndma", "lane", "lane_cum", "cum")


class Sched:
    def __init__(self, nc):
        self.nc = nc
        self.ops = {e: [] for e in ENGS}
        self.last_w = {}
        self.readers = {}
        self.lanes = {}

    def add(self, eng, fn, reads=(), writes=(), ndma=0, lane=None):
        op = Op()
        op.eng = eng
        op.fn = fn
        op.signal = False
        op.ndma = ndma
        op.lane = lane
        op.cum = 0
        op.lane_cum = 0
        dd = {}
        for r in reads:
            w = self.last_w.get(r)
            if w is not None:
                dd[id(w)] = w
        for r in writes:
            w = self.last_w.get(r)
            if w is not None:
                dd[id(w)] = w
            for rd in self.readers.get(r, ()):
                dd[id(rd)] = rd
        for r in reads:
            self.readers.setdefault(r, []).append(op)
        for r in writes:
            self.last_w[r] = op
            self.readers[r] = []
        op.idx = len(self.ops[eng])
        deps = []
        for d in dd.values():
            if d is op:
                continue
            if d.eng == "tensor" and eng == "tensor":
                continue
            deps.append(d)
        op.deps = deps
        for d in deps:
            d.signal = True
        if ndma:
            L = self.lanes.setdefault(lane, [])
            op.lane_cum = (L[-1].lane_cum if L else 0) + 16 * ndma
            L.append(op)
        self.ops[eng].append(op)
        return op

    def emit(self, final_waits=()):
        nc = self.nc
        for e in ENGS:
            c = 0
            for op in self.ops[e]:
                if op.ndma == 0 and op.signal:
                    c += 1
                op.cum = c
        with contextlib.ExitStack() as es:
            sems = {}

            def get_sem(key):
                if key not in sems:
                    sems[key] = es.enter_context(nc.semaphore("s%d" % len(sems)))
                return sems[key]

            def sem_for(op):
                if op.ndma:
                    if op.lane in ("once", "once2"):
                        return get_sem(("lane", op.lane)), self.lanes[op.lane][-1].lane_cum
                    return get_sem(("lane", op.lane)), op.lane_cum
                ep = (op.cum - 1) // EPOCH
                return get_sem((op.eng, ep)), op.cum - ep * EPOCH

            for e in ENGS:
                for op in self.ops[e]:
                    if op.signal or op.ndma:
                        sem_for(op)
            block = es.enter_context(nc.Block())

            def run(engname):
                def body(eng):
                    seen = {}
                    for op in self.ops[engname]:
                        need = {}
                        for d in op.deps:
                            s, v = sem_for(d)
                            k = id(s)
                            if seen.get(k, -1) >= v:
                                continue
                            if k not in need or need[k][1] < v:
                                need[k] = (s, v)
                        for k, (s, v) in need.items():
                            seen[k] = v
                            eng.wait_ge(s, v)
                        ins = op.fn(eng)
                        if op.ndma:
                            s, _ = sem_for(op)
                            if isinstance(ins, (list, tuple)):
                                for i in ins:
                                    i.then_inc(s, 16)
                            else:
                                ins.then_inc(s, 16)
                        elif op.signal:
                            s, _ = sem_for(op)
                            ins.then_inc(s, 1)
                    if engname == "sync":
                        fin = {}
                        for d in final_waits:
                            s, v = sem_for(d)
                            if id(s) not in fin or fin[id(s)][1] < v:
                                fin[id(s)] = (s, v)
                        for s, v in fin.values():
                            eng.wait_ge(s, v)
                return body

            block.tensor(run("tensor"))
            block.vector(run("vector"))
            block.scalar(run("scalar"))
            block.gpsimd(run("gpsimd"))
            block.sync(run("sync"))
        return len(sems)


def _cols(v):
    v = np.asarray(v, np.float32)
    return np.ascontiguousarray(v.reshape(-1, 128).T)


PP_LAYOUT = [
    ("l0_conv_w", 32), ("l0_conv_b", 8), ("l0_gx_b", 8), ("l0_ga_b", 8), ("l0_L", 8),
    ("l0_ln1_g", 16), ("l0_ln1_b", 16), ("l0_ln2_g", 16), ("l0_ln2_b", 16),
    ("l0_fcw", 264), ("l0_fcb", 88), ("l0_sinks", 16),
    ("l1_ln1_g", 16), ("l1_ln1_b", 16), ("l1_ln2_g", 16), ("l1_ln2_b", 16),
    ("l1_fcw", 264), ("l1_fcb", 88), ("l1_D", 6), ("l1_glu_b", 6),
    ("s5_are", 48), ("s5_aim", 48), ("s5_ldt", 48),
    ("rowmask", 8), ("sgn1", 1), ("negone", 1),
]
PP_OFF = {}
_o = 0
for _n, _w in PP_LAYOUT:
    PP_OFF[_n] = (_o, _w)
    _o += _w
PP_W = _o


def host_prepare(inp):
    f = np.float32
    out = {}
    pp = np.zeros((128, PP_W), f)

    def put(name, arr):
        o, w = PP_OFF[name]
        arr = np.asarray(arr, f).reshape(128, w)
        pp[:, o:o + w] = arr

    put("l0_conv_w", np.asarray(inp["l0_lru_conv_w"], f).reshape(4, 8, 128).transpose(2, 0, 1))
    put("l0_conv_b", _cols(inp["l0_lru_conv_b"]))
    put("l0_gx_b", _cols(inp["l0_lru_gx_b"]))
    put("l0_ga_b", _cols(inp["l0_lru_ga_b"]))
    put("l0_L", _cols(inp["l0_lru_L"]))
    for l in (0, 1):
        for nm in ("ln1_g", "ln1_b", "ln2_g", "ln2_b"):
            put("l%d_%s" % (l, nm), _cols(inp["l%d_%s" % (l, nm)]))
        fcw_src = (inp["l0_ffn_conv_w"], inp["l1_ffn_conv_w"])[l]
        fcb_src = (inp["l0_ffn_conv_b"], inp["l1_ffn_conv_b"])[l]
        put("l%d_fcw" % l, np.asarray(fcw_src, f).reshape(3, 88, 128).transpose(2, 0, 1))
        put("l%d_fcb" % l, _cols(fcb_src))
    put("l0_sinks", np.broadcast_to(np.asarray(inp["l0_sinks"], f)[None, :], (128, 16)))
    put("l1_D", _cols(inp["l1_s5_D"]))
    put("l1_glu_b", _cols(inp["l1_glu_b"]))
    are = np.asarray(inp["l1_s5_A_re"], f)
    aim = np.asarray(inp["l1_s5_A_im"], f)
    ldt = np.asarray(inp["l1_s5_log_dt"], f)
    put("s5_are", np.concatenate([are.T, are.T], 0))
    put("s5_aim", np.concatenate([aim.T, aim.T], 0))
    put("s5_ldt", np.broadcast_to(ldt[None, :], (128, 48)))
    rm = np.zeros((128, 8), f)
    for j in range(8):
        rm[16 * j:16 * j + 16, j] = 1.0
    put("rowmask", rm)
    sg = np.ones((128, 1), f)
    sg[64:] = -1.0
    put("sgn1", sg)
    put("negone", -np.ones((128, 1), f))
    out["pp"] = pp
    for nm, key in (("gxw", "l0_lru_gx_w"), ("gaw", "l0_lru_ga_w")):
        w = np.asarray(inp[key], f)
        bd = np.zeros((128, 8, 128), f)
        for j in range(8):
            bd[0:64, j, 0:64] = w[2 * j]
            bd[64:128, j, 64:128] = w[2 * j + 1]
        out[nm] = bd
    Bre = np.asarray(inp["l1_s5_B_re"], f).reshape(6, 8, 64, 16)
    Bim = np.asarray(inp["l1_s5_B_im"], f).reshape(6, 8, 64, 16)
    out["s5_Bre"] = np.ascontiguousarray(Bre.transpose(1, 3, 0, 2).reshape(128, 6, 64))
    out["s5_Bim"] = np.ascontiguousarray(Bim.transpose(1, 3, 0, 2).reshape(128, 6, 64))
    def kl(a):
        a = a.reshape(6, 8, 64)
        return np.ascontiguousarray(np.broadcast_to(a.transpose(1, 0, 2)[:, None], (8, 16, 6, 64)).reshape(128, 6, 64))
    out["s5_kare"] = kl(are)
    out["s5_kaim"] = kl(aim)
    out["s5_kldt"] = kl(np.broadcast_to(ldt[:, None], (48, 64)).copy())
    Cre = np.asarray(inp["l1_s5_C_re"], f).reshape(6, 8, 16, 64)
    Cim = np.asarray(inp["l1_s5_C_im"], f).reshape(6, 8, 16, 64)
    cre_l = Cre.transpose(3, 0, 1, 2).reshape(64, 6, 128)
    cim_l = Cim.transpose(3, 0, 1, 2).reshape(64, 6, 128)
    out["s5_C1"] = np.ascontiguousarray(np.concatenate([cre_l, cim_l], 0))
    out["s5_C2"] = np.ascontiguousarray(np.concatenate([cim_l, cre_l], 0))
    cm = np.zeros((128, 8, 128), f)
    for j in range(8):
        cm[:, j, 16 * j:16 * j + 16] = 1.0
    out["colmask"] = cm
    out["ident"] = np.eye(128, dtype=f)
    qi = np.arange(128)[:, None]
    kj = np.arange(256)[None, :]
    msk = np.zeros((128, 4, 256), f)
    msk[:, 0] = np.where((kj >= qi + 1) & (kj <= qi + 128), 0.0, NEG)
    msk[:, 1] = np.where((kj >= qi) & (kj <= qi + 128), 0.0, NEG)
    q32 = (np.arange(128) % 32)[:, None]
    ki = np.arange(128)[None, :]
    for c in range(4):
        msk[:, 2 + c // 2, (c % 2) * 128:(c % 2) * 128 + 128] = np.where(ki <= 32 * c + q32, 0.0, NEG)
    out["masks"] = msk
    tt = np.arange(SEQ)
    ab = np.zeros((128, 2, SEQ), f)
    ab[:, 0] = (tt // 32)[None, :]
    ab[:, 1] = (tt % 32)[None, :]
    out["iota_ab"] = ab
    out["jrow"] = np.ascontiguousarray(np.broadcast_to(np.arange(T, dtype=f)[None, :], (128, T)))
    return out


BIGW = [
    ("l0_w_in", 2048, 3584), ("l0_w_out", 2048, 2048), ("l0_ffn_up", 2048, 11264), ("l0_ffn_down", 5632, 2048),
    ("l1_w_in", 2048, 2816), ("l1_glu_w", 768, 768), ("l1_w_out", 1280, 2048), ("l1_ffn_up", 2048, 11264),
    ("l1_ffn_down", 5632, 2048),
]
SMALL_IN = [
    ("pp", (128, PP_W)), ("gxw", (128, 8, 128)), ("gaw", (128, 8, 128)),
    ("s5_Bre", (128, 6, 64)), ("s5_Bim", (128, 6, 64)), ("s5_kare", (128, 6, 64)), ("s5_kaim", (128, 6, 64)),
    ("s5_kldt", (128, 6, 64)), ("s5_C1", (128, 6, 128)), ("s5_C2", (128, 6, 128)), ("colmask", (128, 8, 128)),
    ("ident", (128, 128)), ("masks", (128, 4, 256)), ("iota_ab", (128, 2, SEQ)), ("jrow", (128, T)),
]


class Builder:
    def __init__(self, nseq=2, nt=4, layers=2, dbg=False, stop=None):
        self.nseq, self.nt, self.layers, self.dbg = nseq, nt, layers, dbg
        self.stop = stop
        nc = self.nc = bass.Bass("TRN2", target_bir_lowering=False)
        self.S = Sched(nc)
        self.din = {}
        self.din["xT"] = nc.dram_tensor("xT", [nseq, D, SEQ], F32, kind="ExternalInput").ap()
        for nm, r, c in BIGW:
            self.din[nm] = nc.dram_tensor(nm, [r, c], F32, kind="ExternalInput").ap()
        for nm, shp in SMALL_IN:
            self.din[nm] = nc.dram_tensor(nm, list(shp), F32, kind="ExternalInput").ap()
        self.outT = nc.dram_tensor("outT", [nseq, D, SEQ], F32, kind="ExternalOutput").ap()
        self.wb = {}
        for nm, r, c in BIGW:
            self.wb[nm] = nc.dram_tensor(nm + "_bf", [r, c], BF16, kind="Internal").ap()
        self.s5tab = nc.dram_tensor("s5tab", [48, 128, 2, SEQ], F32, kind="Internal").ap()
        self.s5mat = nc.dram_tensor("s5mat", [48, 128, 512], BF16, kind="Internal").ap()
        self.dbg_out = {}
        self.finals = []
        self.uid = 0
        self.ps_rr = 0
        self.ws_rr = 0
        self.lps_rr = 0
        self.ps_nrot = 6
        self.att_ctr = 0

    def sb(self, name, shape, dt=F32):
        return self.nc.alloc_sbuf_tensor("sb_" + name, list(shape), dt).ap()

    def add(self, eng, fn, reads=(), writes=(), **kw):
        if eng != "sync" and "regC_tok" not in writes:
            reads = list(reads) + ["regC_tok"]
        return self.S.add(eng, fn, reads=reads, writes=writes, **kw)

    def next_ps(self):
        i = self.ps_rr % self.ps_nrot
        self.ps_rr += 1
        return self.ps[i], "ps%d" % i

    def long_ps(self):
        i = 6 + (self.lps_rr % 2)
        self.lps_rr += 1
        return self.ps[i], "ps%d" % i

    def debug_store(self, name, src_ap, keys):
        if not self.dbg:
            return
        o = self.nc.dram_tensor(name, list(src_ap.shape), src_ap.dtype, kind="ExternalOutput").ap()
        self.dbg_out[name] = o
        self.finals.append(self.add("gpsimd", lambda e: e.dma_start(out=o, in_=src_ap), reads=keys, writes=[name],
                                    ndma=1, lane=name))

    def wload(self, wname, r0, kc, c0, nb, prows=128):
        i = self.ws_rr % self.nws
        self.ws_rr += 1
        slot = self.ws[i]
        assert kc * nb <= self.ws_elems
        view = slot[0:prows, 0:kc * nb].rearrange("p (k n) -> p k n", k=kc)
        src = self.wb[wname][r0:r0 + prows * kc, c0:c0 + nb].rearrange("(k p) n -> p k n", p=prows)
        key = "ws%d" % i
        self.add("sync", lambda e: e.dma_start(out=view, in_=src), reads=["wb_" + wname], writes=[key], ndma=1,
                 lane=key)
        return view, key

    def setup(self):
        nc = self.nc
        add = self.add
        self.ps = [nc.alloc_psum_tensor("psb%d" % i, [128, 512], F32).ap() for i in range(8)]
        self.pp = self.sb("pp", [128, PP_W])
        add("gpsimd", lambda e: e.dma_start(out=self.pp, in_=self.din["pp"]), writes=["pp"], ndma=1, lane="once")
        self.identb = self.sb("identb", [128, 128], BF16)
        add("gpsimd", lambda e: e.dma_start(out=self.identb, in_=self.din["ident"]), writes=["identb"], ndma=1,
            lane="once")
        self.identf = self.sb("identf", [128, 128])
        add("gpsimd", lambda e: e.dma_start(out=self.identf, in_=self.din["ident"]), writes=["identf"], ndma=1,
            lane="once")
        self.masks = self.sb("masks", [128, 4, 256])
        add("gpsimd", lambda e: e.dma_start(out=self.masks, in_=self.din["masks"]), writes=["masks"], ndma=1,
            lane="once")
        self.onesD = self.sb("onesD", [128, 128])
        add("vector", lambda e: e.memset(self.onesD, 1.0 / D), writes=["onesD"])
        self.gxw = self.sb("gxw", [128, 8, 128], BF16)
        self.gaw = self.sb("gaw", [128, 8, 128], BF16)
        add("gpsimd", lambda e: e.dma_start(out=self.gxw, in_=self.din["gxw"]), writes=["gxw"], ndma=1, lane="once")
        add("gpsimd", lambda e: e.dma_start(out=self.gaw, in_=self.din["gaw"]), writes=["gaw"], ndma=1, lane="once")
        if self.layers == 1:
            for nm, _, _ in BIGW:
                if not nm.startswith("l1"):
                    self.convert(nm)
        self.xres = self.sb("xres", [128, 16, T])
        self.xbf = self.sb("xbf", [128, 16, T], BF16)
        self.mix = self.sb("mix", [128, 16, T], BF16)
        self.regC = self.sb("regC", [128, 11264])
        self.g = self.regC.bitcast(BF16).rearrange("p (k n) -> p k n", k=44)
        self.fhalo = [self.sb("fhalo%d" % l, [128, 88, 2]) for l in range(2)]
        self.xa_halo = self.sb("xa_halo", [128, 8, 3])
        self.lru_h = self.sb("lru_h", [128, 8])
        self.kbuf0 = self.sb("kbuf0", [128, 2, 2 * T], BF16)
        self.vtok0 = self.sb("vtok0", [128, 8, 256], BF16)
        self.small = self.sb("small", [128, 64])
        self.cpcol = self.sb("cpcol", [128, 8])
        if self.layers == 2:
            self.Kc = self.sb("Kc", [128, 2, SEQ], BF16)
            self.V1 = self.sb("V1", [128, 8, 256], BF16)
            self.V4 = self.sb("V4", [128, 8, 256], BF16)
            self.V16 = self.sb("V16", [128, 16, 256], BF16)
            self.s5_rp = self.sb("s5_r", [128, 48])
            self.s5_state = self.sb("s5_state", [128, 48])
            self.s5_th = self.sb("s5_th", [128, 48])
            self.s5_phi0 = self.sb("s5_phi0", [128, 4, 48])
            self.jrow = self.sb("jrow", [128, T])
            self.add("gpsimd", lambda e: e.dma_start(out=self.jrow, in_=self.din["jrow"]), writes=["jrow"], ndma=1,
                     lane="once")
        self.ffn_tmp = {"hb": self.sb("f_hb", [128, 4, T + 2]), "acc": self.sb("f_acc", [128, 4, T])}
        self.lnscr = self.ffn_tmp["acc"]
        self.ws_elems = 4096
        rem = nc.sbuf_bytes_remaining
        self.nws = max(2, min(6, (rem - 2048) // (self.ws_elems * 2)))
        self.ws = [self.sb("ws%d" % i, [128, self.ws_elems], BF16) for i in range(self.nws)]
        o, w = PP_OFF["l0_L"]
        tmp = self.small[:, 0:8]
        add("scalar", lambda e: e.activation(out=tmp, in_=self.pp[:, o:o + 8], func=AF.Exp, scale=-1.0),
            reads=["pp"], writes=["small"])
        add("scalar", lambda e: e.activation(out=tmp, in_=tmp, func=AF.Ln, bias=1.0), reads=["small"],
            writes=["small"])
        add("vector", lambda e: e.tensor_scalar(out=self.cpcol, in0=tmp, scalar1=-8.0, scalar2=None, op0=ALU.mult),
            reads=["small"], writes=["cpcol"])

    def convert(self, nm):
        r, c = [(rr, cc) for (n_, rr, cc) in BIGW if n_ == nm][0]
        nsplit = max(1, r // 512)
        rows = r // nsplit

        def fn(e):
            return [e.dma_start(out=self.wb[nm][i * rows:(i + 1) * rows, :],
                                in_=self.din[nm][i * rows:(i + 1) * rows, :]) for i in range(nsplit)]
        self.S.add("gpsimd", fn, writes=["wb_" + nm], ndma=nsplit, lane="wb_" + nm)

    def ppc(self, name, i=0, n=1):
        o, w = PP_OFF[name]
        return self.pp[:, o + i:o + i + n]

    def barrier(self):
        self.add("gpsimd", lambda e: e.memset(self.small[:, 63:64], 0.0), writes=["regC_tok", "small63"])

    def proj(self, wname, kc, cols, act, act_keys, consume, nb=256):
        cur = None
        for gi, col in enumerate(cols):
            c0, wd = col[0], col[1]
            ob = col[2] if len(col) > 2 else 0
            p0 = (c0 // nb) * nb
            assert c0 + wd <= p0 + nb
            if cur is None or cur[0] != p0:
                view, key = self.wload(wname, 0, kc, p0, nb)
                cur = (p0, view, key)
            _, view, key = cur
            ps, pk = self.next_ps()
            for k in range(kc):
                self.add("tensor", lambda e, k=k, ps=ps, view=view, c0=c0, p0=p0, wd=wd, ob=ob: e.matmul(
                    ps[ob:ob + wd, :], lhsT=view[:, k, c0 - p0:c0 - p0 + wd], rhs=act[:, k, :], start=(k == 0),
                    stop=(k == kc - 1)), reads=[key] + act_keys, writes=[pk])
            consume(gi, ps, pk)

    def layer_norm(self, gname, bname, store_out=None):
        add = self.add
        mean_ps, mk = self.next_ps()
        msq_ps, qk = self.next_ps()
        for n in range(16):
            sq = self.lnscr[:, 2 + (n % 2), :]
            sk = "f_acc%d" % (2 + n % 2)
            add("scalar", lambda e, n=n, sq=sq: e.activation(out=sq, in_=self.xres[:, n, :], func=AF.Square),
                reads=[("xres", n)], writes=[sk])
            add("tensor", lambda e, n=n: e.matmul(mean_ps, lhsT=self.onesD, rhs=self.xres[:, n, :], start=(n == 0),
                                                  stop=(n == 15)), reads=["onesD", ("xres", n)], writes=[mk])
            add("tensor", lambda e, n=n, sq=sq: e.matmul(msq_ps, lhsT=self.onesD, rhs=sq, start=(n == 0),
                                                         stop=(n == 15)), reads=["onesD", sk], writes=[qk])
        mean = self.lnscr[:, 0, :]
        rstd = self.lnscr[:, 1, :]
        t2 = self.lnscr[:, 2, :]
        add("scalar", lambda e: e.activation(out=mean, in_=mean_ps, func=AF.Identity), reads=[mk], writes=["f_acc0"])
        add("scalar", lambda e: e.activation(out=t2, in_=mean_ps, func=AF.Square), reads=[mk], writes=["f_acc2"])
        add("vector", lambda e: e.tensor_tensor(out=rstd, in0=msq_ps, in1=t2, op=ALU.subtract), reads=[qk, "f_acc2"],
            writes=["f_acc1"])
        add("vector", lambda e: e.tensor_scalar(out=rstd, in0=rstd, scalar1=LN_EPS, scalar2=None, op0=ALU.add),
            reads=["f_acc1"], writes=["f_acc1"])
        add("scalar", lambda e: e.activation(out=rstd, in_=rstd, func=AF.Sqrt), reads=["f_acc1"], writes=["f_acc1"])
        add("vector", lambda e: e.reciprocal(out=rstd, in_=rstd), reads=["f_acc1"], writes=["f_acc1"])
        for n in range(16):
            xr = self.xres[:, n, :]
            add("vector", lambda e, xr=xr: e.tensor_tensor(out=xr, in0=xr, in1=mean, op=ALU.subtract),
                reads=[("xres", n), "f_acc0"], writes=[("xres", n)])
            add("vector", lambda e, xr=xr: e.tensor_tensor(out=xr, in0=xr, in1=rstd, op=ALU.mult),
                reads=[("xres", n), "f_acc1"], writes=[("xres", n)])
            if store_out is None:
                add("scalar", lambda e, xr=xr, n=n: e.activation(out=self.xbf[:, n, :], in_=xr, func=AF.Identity,
                                                                 scale=self.ppc(gname, n), bias=self.ppc(bname, n)),
                    reads=[("xres", n), "pp"], writes=[("xbf", n)])
            add("scalar", lambda e, xr=xr, n=n: e.activation(out=xr, in_=xr, func=AF.Identity,
                                                             scale=self.ppc(gname, n), bias=self.ppc(bname, n)),
                reads=[("xres", n), "pp"], writes=[("xres", n)])
            if store_out is not None:
                store_out(n)

    def ffn(self, l, hook=None):
        add = self.add
        up, down = "l%d_ffn_up" % l, "l%d_ffn_down" % l
        fcw, fcb = "l%d_fcw" % l, "l%d_fcb" % l
        halo = self.fhalo[l]
        xkeys = [("xbf", n) for n in range(16)]
        tmp = self.ffn_tmp

        def conv(ps, pk, c, slot):
            hb = tmp["hb"][:, slot, :]
            acc = tmp["acc"][:, slot, :]
            hk, ak = "f_hb%d" % slot, "f_acc%d" % slot
            add("gpsimd", lambda e: e.tensor_copy(out=hb[:, 0:2], in_=halo[:, c, :]), reads=[("fhalo", l, c)],
                writes=[hk + "h"])
            add("scalar", lambda e: e.activation(out=hb[:, 2:2 + T], in_=ps, func=AF.Identity), reads=[pk],
                writes=[hk])
            add("scalar", lambda e: e.activation(out=acc, in_=ps, func=AF.Identity, scale=self.ppc(fcw, 2 * 88 + c),
                                                 bias=self.ppc(fcb, c)), reads=[pk, "pp"], writes=[ak])
            add("gpsimd", lambda e: e.tensor_copy(out=halo[:, c, :], in_=hb[:, T:T + 2]), reads=[hk],
                writes=[("fhalo", l, c)])
            add("vector", lambda e: e.scalar_tensor_tensor(out=acc, in0=hb[:, 1:1 + T], scalar=self.ppc(fcw, 88 + c),
                                                           in1=acc, op0=ALU.mult, op1=ALU.add),
                reads=[hk, hk + "h", ak, "pp"], writes=[ak])
            add("vector", lambda e: e.scalar_tensor_tensor(out=acc, in0=hb[:, 0:T], scalar=self.ppc(fcw, c),
                                                           in1=acc, op0=ALU.mult, op1=ALU.add),
                reads=[hk, hk + "h", ak, "pp"], writes=[ak])
            return acc, ak

        for i in range(22):
            gv, gk = self.wload(up, 0, 16, 256 * i, 256)
            vv, vk = self.wload(up, 0, 16, DFF + 256 * i, 256)
            for j in range(2):
                c = 2 * i + j
                psg, pkg = self.next_ps()
                for k in range(16):
                    add("tensor", lambda e, k=k, psg=psg, gv=gv, j=j: e.matmul(
                        psg, lhsT=gv[:, k, 128 * j:128 * j + 128], rhs=self.xbf[:, k, :], start=(k == 0),
                        stop=(k == 15)), reads=[gk] + xkeys, writes=[pkg])
                psv, pkv = self.next_ps()
                for k in range(16):
                    add("tensor", lambda e, k=k, psv=psv, vv=vv, j=j: e.matmul(
                        psv, lhsT=vv[:, k, 128 * j:128 * j + 128], rhs=self.xbf[:, k, :], start=(k == 0),
                        stop=(k == 15)), reads=[vk] + xkeys, writes=[pkv])
                sl = c % 2
                accg, akg = conv(psg, pkg, c, 2 * sl)
                accv, akv = conv(psv, pkv, 44 + c, 2 * sl + 1)
                add("scalar", lambda e, accg=accg: e.activation(out=accg, in_=accg, func=AF.Silu), reads=[akg],
                    writes=[akg])
                add("gpsimd", lambda e, accg=accg, accv=accv, c=c: e.tensor_tensor(out=self.g[:, c, :], in0=accg,
                                                                                   in1=accv, op=ALU.mult),
                    reads=[akg, akv], writes=[("g", c)])
                if hook is not None:
                    hook()
        gkeys = [("g", c) for c in range(44)]
        for nb2 in range(8):
            pieces = []
            for (k0, kn) in ((0, 16), (16, 16), (32, 12)):
                wv, wk = self.wload(down, k0 * 128, kn, 256 * nb2, 256)
                pieces.append((k0, kn, wv, wk))
            for j in range(2):
                n = 2 * nb2 + j
                ps, pk = self.next_ps()
                for (k0, kn, wv, wk) in pieces:
                    for kk in range(kn):
                        k = k0 + kk
                        add("tensor", lambda e, k=k, ps=ps, wv=wv, kk=kk, j=j: e.matmul(
                            ps, lhsT=wv[:, kk, 128 * j:128 * j + 128], rhs=self.g[:, k, :], start=(k == 0),
                            stop=(k == 43)), reads=[wk, ("g", k)], writes=[pk])
                add("vector", lambda e, ps=ps, n=n: e.scalar_tensor_tensor(
                    out=self.xres[:, n, :], in0=self.xres[:, n, :], scalar=ALPHA, in1=ps, op0=ALU.mult, op1=ALU.add),
                    reads=[pk, ("xres", n)], writes=[("xres", n)])

    NS_SM, NS_PN, NS_PT, NS_ST = 5, 3, 3, 8

    def att_carve(self, carve):
        self.att_tmp = {
            "Sm": carve(256 * self.NS_SM).rearrange("p (s n) -> p s n", s=self.NS_SM),
            "Pn": carve(128 * self.NS_PN).bitcast(BF16).rearrange("p (s n) -> p s n", s=self.NS_PN),
            "PT": carve(128 * self.NS_PT).bitcast(BF16).rearrange("p (s n) -> p s n", s=self.NS_PT),
            "stat": carve(8 * self.NS_ST).rearrange("p (s n) -> p s n", s=self.NS_ST),
        }

    def att_a1(self, u):
        add = self.add
        i = u["i"] = self.att_ctr
        self.att_ctr += 1
        qparts = u["q"] if isinstance(u["q"], list) else [(u["q"], 0)]
        ps, pk = self.next_ps()
        u["ps"], u["pk"] = ps, pk
        off = 0
        for (k_ap, _) in u["kvs"]:
            nk = k_ap.shape[-1]
            for (qp, ro) in qparts:
                nr = qp.shape[-1]
                add("tensor", lambda e, k_ap=k_ap, off=off, nk=nk, qp=qp, ro=ro, nr=nr: e.matmul(
                    ps[ro:ro + nr, off:off + nk], lhsT=qp, rhs=k_ap, start=True, stop=True),
                    reads=u["qkeys"] + u["kvkeys"], writes=[pk])
            off += nk
        u["ntot"] = off
        nq = u["nq"]
        u["Sm"] = self.att_tmp["Sm"][0:nq, i % self.NS_SM, 0:off]
        u["smk"] = "a_Sm%d" % (i % self.NS_SM)
        u["st"] = self.att_tmp["stat"][0:nq, i % self.NS_ST, :]
        u["stk"] = "a_st%d" % (i % self.NS_ST)
        u["Pn"] = self.att_tmp["Pn"][0:nq, i % self.NS_PN, 0:off]
        u["pnk"] = "a_Pn%d" % (i % self.NS_PN)

    def att_a2(self, u):
        add = self.add
        nq, ntot, ps, pk, sink, Sm, smk, st, stk = (u[k] for k in ("nq", "ntot", "ps", "pk", "sink", "Sm", "smk",
                                                                     "st", "stk"))
        if u.get("mask_all") is not None:
            add("vector", lambda e: e.tensor_tensor(out=Sm, in0=ps[0:nq, 0:ntot], in1=u["mask_all"], op=ALU.add),
                reads=[pk, "masks"], writes=[smk])
        else:
            off = 0
            for bi, (k_ap, _) in enumerate(u["kvs"]):
                nk = k_ap.shape[-1]
                add("vector", lambda e, off=off, nk=nk, bi=bi: e.tensor_tensor(out=Sm[:, off:off + nk],
                                                                              in0=ps[0:nq, off:off + nk],
                                                                              in1=u["masks"][bi], op=ALU.add),
                    reads=[pk, "masks"], writes=[smk])
                off += nk
        if sink is None:
            add("vector", lambda e: e.reduce_max(out=st[:, 1:2], in_=Sm, axis=AX.X, negate=True), reads=[smk],
                writes=[stk])
        else:
            add("vector", lambda e: e.reduce_max(out=st[:, 0:1], in_=Sm, axis=AX.X), reads=[smk], writes=[stk])
            add("vector", lambda e: e.tensor_scalar(out=st[:, 1:2], in0=st[:, 0:1], scalar1=sink[0:nq, :],
                                                    scalar2=-1.0, op0=ALU.max, op1=ALU.mult), reads=[stk, "pp"],
                writes=[stk])

    def att_a3(self, u):
        add = self.add
        nq, sink, Sm, smk, st, stk = (u[k] for k in ("nq", "sink", "Sm", "smk", "st", "stk"))
        add("scalar", lambda e: e.activation(out=Sm, in_=Sm, func=AF.Exp, bias=st[:, 1:2], accum_out=st[:, 2:3]),
            reads=[smk, stk], writes=[smk, stk])
        if sink is not None:
            add("scalar", lambda e: e.activation(out=st[:, 3:4], in_=sink[0:nq, :], func=AF.Exp, bias=st[:, 1:2]),
                reads=[stk, "pp"], writes=[stk])

    def att_a4(self, u):
        add = self.add
        sink, st, stk = (u[k] for k in ("sink", "st", "stk"))
        if sink is not None:
            add("vector", lambda e: e.tensor_tensor(out=st[:, 4:5], in0=st[:, 2:3], in1=st[:, 3:4], op=ALU.add),
                reads=[stk], writes=[stk])
            add("vector", lambda e: e.reciprocal(out=st[:, 5:6], in_=st[:, 4:5]), reads=[stk], writes=[stk])
        else:
            add("vector", lambda e: e.reciprocal(out=st[:, 5:6], in_=st[:, 2:3]), reads=[stk], writes=[stk])
            add("scalar", lambda e: e.activation(out=st[:, 6:7], in_=st[:, 2:3], func=AF.Ln), reads=[stk],
                writes=[stk])
            add("gpsimd", lambda e: e.tensor_tensor(out=st[:, 6:7], in0=st[:, 6:7], in1=st[:, 1:2], op=ALU.subtract),
                reads=[stk], writes=[stk])

    def att_a5(self, u):
        Sm, smk, st, stk, Pn, pnk = (u[k] for k in ("Sm", "smk", "st", "stk", "Pn", "pnk"))
        self.add("scalar", lambda e: e.activation(out=Pn, in_=Sm, func=AF.Identity, scale=st[:, 5:6]),
                 reads=[smk, stk], writes=[pnk])

    def att_a6(self, u):
        add = self.add
        nq, Pn, pnk = u["nq"], u["Pn"], u["pnk"]
        pst, ptk = self.next_ps()
        pstb = pst.bitcast(BF16)
        u["pstb"], u["ptk"] = pstb, ptk
        off = 0
        for bi, (k_ap, _) in enumerate(u["kvs"]):
            nk = k_ap.shape[-1]
            add("tensor", lambda e, off=off, nk=nk, bi=bi: e.transpose(out=pstb[0:nk, bi * 128:bi * 128 + nq],
                                                                        in_=Pn[:, off:off + nk],
                                                                        identity=self.identb[0:nq, 0:nq]),
                reads=[pnk, "identb"], writes=[ptk])
            off += nk

    def att_a7(self, u):
        add = self.add
        i, nq, pstb, ptk = u["i"], u["nq"], u["pstb"], u["ptk"]
        nks = [k.shape[-1] for k, _ in u["kvs"]]
        nb = len(nks)
        PT = self.att_tmp["PT"][:, i % self.NS_PT, 0:nb * 128]
        ptsk = "a_PT%d" % (i % self.NS_PT)
        if nq == 128 and all(n_ == nks[0] for n_ in nks):
            if i % 2 == 0:
                add("scalar", lambda e: e.activation(out=PT[0:nks[0], :], in_=pstb[0:nks[0], 0:nb * 128],
                                                     func=AF.Identity), reads=[ptk], writes=[ptsk])
            else:
                add("vector", lambda e: e.tensor_copy(out=PT[0:nks[0], :], in_=pstb[0:nks[0], 0:nb * 128]),
                    reads=[ptk], writes=[ptsk])
        else:
            for bi, n_ in enumerate(nks):
                add("vector", lambda e, bi=bi, n_=n_: e.tensor_copy(out=PT[0:n_, bi * 128:bi * 128 + nq],
                                                                    in_=pstb[0:n_, bi * 128:bi * 128 + nq]),
                    reads=[ptk], writes=[ptsk])
        u["res"] = [(PT[0:nk, bi * 128:bi * 128 + nq], v_ap) for bi, (nk, (_, v_ap)) in enumerate(zip(nks, u["kvs"]))]
        u["ptsk"] = ptsk

    def att_a8(self, u):
        u["pv"](u)
        if u.get("post") is not None:
            u["post"]()

    def attn_pipeline(self, units, hook=None):
        stages = [self.att_a1, self.att_a2, self.att_a3, self.att_a4, self.att_a5, self.att_a6, self.att_a7,
                  self.att_a8]
        n = len(units)
        ns = len(stages)
        for u in units:
            u.setdefault("sink", None)
        for i in range(n + ns - 1):
            for k in range(ns):
                j = i - k
                if 0 <= j < n:
                    stages[k](units[j])
            if hook is not None:
                hook(i)

    def layer0_mixer(self, c):
        add = self.add
        par = c % 2
        xkeys = [("xbf", n) for n in range(16)]
        R = self.regC
        tmp = {}
        o = [0]

        def carve(nf32):
            a = R[:, o[0]:o[0] + nf32]
            o[0] += nf32
            return a
        qbuf = carve(2048).bitcast(BF16).rearrange("p (h t) -> p h t", h=8)
        self.att_carve(carve)

        def qslot(h):
            return (h // 8) * 4 + h % 4
        lt = {nm: carve(512).rearrange("p (s n) -> p s n", s=1) for nm in
              ("gel", "u", "si", "sr", "a", "b", "h")}
        xab = carve(2 * 516).rearrange("p (s n) -> p s n", s=2)
        ubf = carve(512).bitcast(BF16).rearrange("p (s n) -> p s n", s=2)
        assert o[0] <= 11264, o[0]
        wv, wvk = self.wload("l0_w_in", 0, 16, 3328, 256)
        for qb in range(4):
            ps, pk = self.next_ps()
            for k in range(16):
                add("tensor", lambda e, k=k, ps=ps, qb=qb: e.matmul(ps[:, 0:256], lhsT=self.xbf[:, k, 128 * qb:128 * qb + 128],
                                                                    rhs=wv[:, k, :], start=(k == 0), stop=(k == 15)),
                    reads=[wvk] + xkeys, writes=[pk])
            add("scalar", lambda e, ps=ps, qb=qb: e.activation(out=self.vtok0[:, 4 * par + qb, :], in_=ps[:, 0:256],
                                                               func=AF.Identity), reads=[pk],
                writes=[("vtok0", 4 * par + qb)])
        def k_consume(gi, ps, pk):
            kb = (gi % 2) * 64
            add("scalar", lambda e: e.activation(out=self.kbuf0[kb:kb + 64, gi // 2, par * T:(par + 1) * T],
                                                 in_=ps[kb:kb + 64, :], func=AF.Identity), reads=[pk],
                writes=[("kbuf0", gi, par)])
        self.proj("l0_w_in", 16, [(3072 + 64 * h, 64, (h % 2) * 64) for h in range(4)], self.xbf, xkeys, k_consume)
        def q_consume(h, ps, pk):
            qb_ = ((h // 4) % 2) * 64
            add("scalar", lambda e: e.activation(out=qbuf[qb_:qb_ + 64, qslot(h), :], in_=ps[qb_:qb_ + 64, :],
                                                 func=AF.Identity, scale=0.125), reads=[pk], writes=[("qbuf", h)])
        self.proj("l0_w_in", 16, [(2048 + 64 * h, 64, ((h // 4) % 2) * 64) for h in range(16)], self.xbf, xkeys, q_consume)
        units = []
        for hp in range(8):
            pso, pok = self.long_ps()
            for h in (2 * hp, 2 * hp + 1):
                kv = h // 4
                kb = (kv % 2) * 64
                kvi = kv // 2
                base = (h % 2) * 64
                for qb in range(4):
                    kvs, kvkeys, masks = [], [], []
                    if not (c == 0 and qb == 0):
                        if qb > 0:
                            kp = self.kbuf0[kb:kb + 64, kvi, par * T + 128 * (qb - 1):par * T + 128 * qb]
                            vp = self.vtok0[:, 4 * par + qb - 1, 64 * kv:64 * kv + 64]
                            kvkeys += [("kbuf0", kv, par), ("vtok0", 4 * par + qb - 1)]
                        else:
                            kp = self.kbuf0[kb:kb + 64, kvi, (1 - par) * T + 384:(1 - par) * T + 512]
                            vp = self.vtok0[:, 4 * (1 - par) + 3, 64 * kv:64 * kv + 64]
                            kvkeys += [("kbuf0", kv, 1 - par), ("vtok0", 4 * (1 - par) + 3)]
                        kvs.append((kp, vp))
                        masks.append(self.masks[:, 0, 0:128])
                    kvs.append((self.kbuf0[kb:kb + 64, kvi, par * T + 128 * qb:par * T + 128 * qb + 128],
                                self.vtok0[:, 4 * par + qb, 64 * kv:64 * kv + 64]))
                    kvkeys += [("kbuf0", kv, par), ("vtok0", 4 * par + qb)]
                    masks.append(self.masks[:, 0, 128:256])

                    def pv(u, pso=pso, pok=pok, base=base, qb=qb):
                        res = u["res"]
                        for bi, (pt, v_ap) in enumerate(res):
                            add("tensor", lambda e, pt=pt, v_ap=v_ap, bi=bi, nb=len(res): e.matmul(
                                pso[base:base + 64, 128 * qb:128 * qb + 128], lhsT=v_ap, rhs=pt, start=(bi == 0),
                                stop=(bi == nb - 1)), reads=[u["ptsk"]] + u["kvkeys"], writes=[pok])
                    u = dict(q=qbuf[kb:kb + 64, qslot(h), 128 * qb:128 * qb + 128], qkeys=[("qbuf", h)], kvs=kvs,
                             kvkeys=kvkeys, masks=masks, nq=128, sink=self.ppc("l0_sinks", h), pv=pv,
                             mask_all=(self.masks[:, 0, :] if len(kvs) == 2 else None))
                    units.append(u)

            def post(pso=pso, pok=pok, hp=hp):
                add("scalar", lambda e: e.activation(out=self.mix[:, 8 + hp, :], in_=pso, func=AF.Identity),
                    reads=[pok], writes=[("mix", 8 + hp)])
            units[-1]["post"] = post
        att_units = units

        def lru_rest(j, psx, pkx, psg, pkg):
            s = 0
            xa = xab[:, j % 2, :]
            xk = "l_xa%d" % (j % 2)
            add("gpsimd", lambda e: e.tensor_copy(out=xa[:, 0:3], in_=self.xa_halo[:, j, :]),
                reads=[("xa_halo", j)], writes=[xk + "h"])
            add("scalar", lambda e: e.activation(out=xa[:, 3:3 + T], in_=psx, func=AF.Identity), reads=[pkx],
                writes=[xk])
            gel = lt["gel"][:, s, :]
            add("scalar", lambda e: e.activation(out=gel, in_=psg, func=AF.Gelu), reads=[pkg],
                writes=["l_gel%d" % s])
            u = lt["u"][:, s, :]
            uk = "l_u%d" % s
            add("scalar", lambda e: e.activation(out=u, in_=psx, func=AF.Identity, scale=self.ppc("l0_conv_w", 3 * 8 + j),
                                                 bias=self.ppc("l0_conv_b", j)), reads=[pkx, "pp"], writes=[uk])
            add("gpsimd", lambda e: e.tensor_copy(out=self.xa_halo[:, j, :], in_=xa[:, T:T + 3]), reads=[xk],
                writes=[("xa_halo", j)])
            for tap in range(3):
                add("vector", lambda e, tap=tap: e.scalar_tensor_tensor(
                    out=u, in0=xa[:, tap:tap + T], scalar=self.ppc("l0_conv_w", tap * 8 + j), in1=u, op0=ALU.mult,
                    op1=ALU.add), reads=[xk, xk + "h", uk, "pp"], writes=[uk])
            ub = ubf[:, s, :]
            ubk = "l_ub%d" % s
            add("scalar", lambda e: e.activation(out=ub, in_=u, func=AF.Identity), reads=[uk], writes=[ubk])
            ps1, pk1 = self.next_ps()
            add("tensor", lambda e: e.matmul(ps1, lhsT=self.gxw[:, j, :], rhs=ub, start=True, stop=True),
                reads=["gxw", ubk], writes=[pk1])
            ps2, pk2 = self.next_ps()
            add("tensor", lambda e: e.matmul(ps2, lhsT=self.gaw[:, j, :], rhs=ub, start=True, stop=True),
                reads=["gaw", ubk], writes=[pk2])
            si = lt["si"][:, s, :]
            sr = lt["sr"][:, s, :]
            a = lt["a"][:, s, :]
            b = lt["b"][:, s, :]
            hh = lt["h"][:, s, :]
            add("scalar", lambda e: e.activation(out=si, in_=ps1, func=AF.Sigmoid, bias=self.ppc("l0_gx_b", j)),
                reads=[pk1, "pp"], writes=["l_si%d" % s])
            add("scalar", lambda e: e.activation(out=sr, in_=ps2, func=AF.Sigmoid, bias=self.ppc("l0_ga_b", j)),
                reads=[pk2, "pp"], writes=["l_sr%d" % s])
            add("scalar", lambda e: e.activation(out=a, in_=sr, func=AF.Exp, scale=self.cpcol[:, j:j + 1]),
                reads=["l_sr%d" % s, "cpcol"], writes=["l_a%d" % s])
            add("scalar", lambda e: e.activation(out=sr, in_=a, func=AF.Square), reads=["l_a%d" % s],
                writes=["l_sr%d" % s])
            add("scalar", lambda e: e.activation(out=sr, in_=sr, func=AF.Sqrt, scale=-1.0, bias=1.0),
                reads=["l_sr%d" % s], writes=["l_sr%d" % s])
            add("gpsimd", lambda e: e.tensor_tensor(out=b, in0=si, in1=u, op=ALU.mult), reads=["l_si%d" % s, uk],
                writes=["l_b%d" % s])
            add("vector", lambda e: e.tensor_tensor(out=b, in0=b, in1=sr, op=ALU.mult),
                reads=["l_b%d" % s, "l_sr%d" % s], writes=["l_b%d" % s])
            add("vector", lambda e: e.tensor_tensor_scan(out=hh, data0=a, data1=b, initial=self.lru_h[:, j:j + 1],
                                                         op0=ALU.mult, op1=ALU.add),
                reads=["l_a%d" % s, "l_b%d" % s, ("lru_h", j)], writes=["l_h%d" % s])
            add("gpsimd", lambda e: e.tensor_copy(out=self.lru_h[:, j:j + 1], in_=hh[:, T - 1:T]), reads=["l_h%d" % s],
                writes=[("lru_h", j)])
            add("gpsimd", lambda e: e.tensor_tensor(out=self.mix[:, j, :], in0=hh, in1=gel, op=ALU.mult),
                reads=["l_h%d" % s, "l_gel%d" % s], writes=[("mix", j)])

        self.attn_pipeline(att_units)
        pend = []
        for j in range(8):
            p0 = (j // 2) * 256
            if j % 2 == 0:
                wx, wxk = self.wload("l0_w_in", 0, 16, p0, 256)
                wg, wgk = self.wload("l0_w_in", 0, 16, 1024 + p0, 256)
            psx, pkx = self.next_ps()
            for k in range(16):
                add("tensor", lambda e, k=k, psx=psx, wx=wx, j=j: e.matmul(
                    psx, lhsT=wx[:, k, 128 * (j % 2):128 * (j % 2) + 128], rhs=self.xbf[:, k, :], start=(k == 0),
                    stop=(k == 15)), reads=[wxk] + xkeys, writes=[pkx])
            psg, pkg = self.next_ps()
            for k in range(16):
                add("tensor", lambda e, k=k, psg=psg, wg=wg, j=j: e.matmul(
                    psg, lhsT=wg[:, k, 128 * (j % 2):128 * (j % 2) + 128], rhs=self.xbf[:, k, :], start=(k == 0),
                    stop=(k == 15)), reads=[wgk] + xkeys, writes=[pkg])
            pend.append((j, psx, pkx, psg, pkg))
            if len(pend) == 2:
                lru_rest(*pend.pop(0))
        while pend:
            lru_rest(*pend.pop(0))

    def reduce_angle(self, eng_ops, x, ki, kf, n, keys):
        add = self.add
        add("vector", lambda e: e.tensor_scalar(out=ki, in0=x, scalar1=1.0 / TWO_PI, scalar2=None, op0=ALU.mult),
            reads=keys, writes=[keys[0] + "_ki"])
        add("vector", lambda e: e.tensor_copy(out=kf, in_=ki), reads=[keys[0] + "_ki"], writes=[keys[0] + "_kf"])
        add("vector", lambda e: e.scalar_tensor_tensor(out=x, in0=kf, scalar=-TWO_PI, in1=x, op0=ALU.mult,
                                                       op1=ALU.add), reads=[keys[0] + "_kf"] + keys, writes=keys)
        add("vector", lambda e: e.tensor_scalar(out=x, in0=x, scalar1=-PI_SAFE, scalar2=PI_SAFE, op0=ALU.max,
                                                op1=ALU.min), reads=keys, writes=keys)

    def cos_from_reduced(self, red, tmp, out, keys_red, key_tmp, key_out):
        add = self.add
        add("vector", lambda e: e.tensor_scalar(out=tmp, in0=red, scalar1=1.5707962, scalar2=-6.2831833,
                                                op0=ALU.is_gt, op1=ALU.mult), reads=keys_red, writes=[key_tmp])
        add("vector", lambda e: e.scalar_tensor_tensor(out=tmp, in0=red, scalar=1.5707955, in1=tmp, op0=ALU.add,
                                                       op1=ALU.add), reads=keys_red + [key_tmp], writes=[key_tmp])
        add("scalar", lambda e: e.activation(out=out, in_=tmp, func=AF.Sin), reads=[key_tmp],
            writes=[key_out])

    def s5_prologue(self):
        add = self.add
        R = self.regC
        o = [0]

        def carve(n):
            a = R[:, o[0]:o[0] + n]
            o[0] += n
            return a
        names = ["s5_Bre", "s5_Bim", "s5_kare", "s5_kaim", "s5_kldt"]
        t = {}
        for nm in names:
            t[nm] = carve(384)
            add("gpsimd", lambda e, nm=nm: e.dma_start(out=t[nm].rearrange("p (a b) -> p a b", a=6), in_=self.din[nm]),
                writes=[nm], ndma=1, lane="once")
        C1 = carve(768)
        C2 = carve(768)
        cmk = carve(1024)
        add("gpsimd", lambda e: e.dma_start(out=C1.rearrange("p (a b) -> p a b", a=6), in_=self.din["s5_C1"]),
            writes=["C1"], ndma=1, lane="once")
        add("gpsimd", lambda e: e.dma_start(out=C2.rearrange("p (a b) -> p a b", a=6), in_=self.din["s5_C2"]),
            writes=["C2"], ndma=1, lane="once")
        add("gpsimd", lambda e: e.dma_start(out=cmk.rearrange("p (a b) -> p a b", a=8), in_=self.din["colmask"]),
            writes=["cmk"], ndma=1, lane="once")
        w = {nm: carve(384) for nm in ("dt", "r", "th", "sin", "cos", "t1", "t2", "kre", "kim", "kf")}
        ki = carve(384).bitcast(I32)
        kare, kaim = t["s5_kare"], t["s5_kaim"]

        def op2(eng, out, a, b, op, rk, wk):
            add(eng, lambda e: e.tensor_tensor(out=out, in0=a, in1=b, op=op), reads=rk, writes=wk)
        add("scalar", lambda e: e.activation(out=w["dt"], in_=t["s5_kldt"], func=AF.Exp), reads=["s5_kldt"],
            writes=["k_dt"])
        op2("vector", w["r"], kare, w["dt"], ALU.mult, ["s5_kare", "k_dt"], ["k_r"])
        add("scalar", lambda e: e.activation(out=w["r"], in_=w["r"], func=AF.Exp), reads=["k_r"], writes=["k_r"])
        op2("vector", w["th"], kaim, w["dt"], ALU.mult, ["s5_kaim", "k_dt"], ["k_th"])
        self.reduce_angle(None, w["th"], ki, w["kf"], 384, ["k_th"])
        add("scalar", lambda e: e.activation(out=w["sin"], in_=w["th"], func=AF.Sin), reads=["k_th"],
            writes=["k_sin"])
        self.cos_from_reduced(w["th"], w["t1"], w["cos"], ["k_th"], "k_t1", "k_cos")
        op2("vector", w["cos"], w["cos"], w["r"], ALU.mult, ["k_cos", "k_r"], ["k_cos"])
        add("vector", lambda e: e.tensor_scalar(out=w["cos"], in0=w["cos"], scalar1=-1.0, scalar2=None, op0=ALU.add),
            reads=["k_cos"], writes=["k_cos"])
        op2("vector", w["sin"], w["sin"], w["r"], ALU.mult, ["k_sin", "k_r"], ["k_sin"])
        op2("vector", w["t1"], kare, kare, ALU.mult, ["s5_kare", "k_t1"], ["k_t1"])
        op2("vector", w["t2"], kaim, kaim, ALU.mult, ["s5_kaim"], ["k_t2"])
        op2("vector", w["t1"], w["t1"], w["t2"], ALU.add, ["k_t1", "k_t2"], ["k_t1"])
        add("vector", lambda e: e.reciprocal(out=w["dt"], in_=w["t1"]), reads=["k_t1", "k_dt"], writes=["k_dt"])
        op2("vector", w["t1"], w["cos"], kare, ALU.mult, ["k_cos", "s5_kare", "k_t1"], ["k_t1"])
        op2("vector", w["t2"], w["sin"], kaim, ALU.mult, ["k_sin", "s5_kaim", "k_t2"], ["k_t2"])
        op2("vector", w["kre"], w["t1"], w["t2"], ALU.add, ["k_t1", "k_t2"], ["k_kre"])
        op2("vector", w["kre"], w["kre"], w["dt"], ALU.mult, ["k_kre", "k_dt"], ["k_kre"])
        op2("vector", w["t1"], w["sin"], kare, ALU.mult, ["k_sin", "s5_kare", "k_t1"], ["k_t1"])
        op2("vector", w["t2"], w["cos"], kaim, ALU.mult, ["k_cos", "s5_kaim", "k_t2"], ["k_t2"])
        op2("vector", w["kim"], w["t1"], w["t2"], ALU.subtract, ["k_t1", "k_t2"], ["k_kim"])
        op2("vector", w["kim"], w["kim"], w["dt"], ALU.mult, ["k_kim", "k_dt"], ["k_kim"])
        Bre, Bim = t["s5_Bre"], t["s5_Bim"]
        op2("vector", w["t1"], w["kre"], Bre, ALU.mult, ["k_kre", "s5_Bre", "k_t1"], ["k_t1"])
        op2("vector", w["t2"], w["kim"], Bim, ALU.mult, ["k_kim", "s5_Bim", "k_t2"], ["k_t2"])
        op2("vector", w["r"], w["t1"], w["t2"], ALU.subtract, ["k_t1", "k_t2", "k_r"], ["k_bre"])
        op2("vector", w["t1"], w["kre"], Bim, ALU.mult, ["k_kre", "s5_Bim", "k_t1"], ["k_t1"])
        op2("vector", w["t2"], w["kim"], Bre, ALU.mult, ["k_kim", "s5_Bre", "k_t2"], ["k_t2"])
        op2("vector", w["th"], w["t1"], w["t2"], ALU.add, ["k_t1", "k_t2", "k_th"], ["k_bim"])
        bre = w["r"].rearrange("p (a b) -> p a b", a=6)
        bim = w["th"].rearrange("p (a b) -> p a b", a=6)
        add("vector", lambda e: e.tensor_scalar(out=C1, in0=C1, scalar1=self.ppc("sgn1"), scalar2=None, op0=ALU.mult),
            reads=["C1", "pp"], writes=["C1"])
        add("vector", lambda e: e.tensor_scalar(out=C2, in0=C2, scalar1=-1.0, scalar2=None, op0=ALU.mult),
            reads=["C2"], writes=["C2"])
        C1v = C1.rearrange("p (a b) -> p a b", a=6)
        C2v = C2.rearrange("p (a b) -> p a b", a=6)
        cmv = cmk.rearrange("p (a b) -> p a b", a=8)
        rmo = PP_OFF["rowmask"][0]
        rmask = self.pp[:, rmo:rmo + 8]
        xrf = self.xres.rearrange("p a b -> p (a b)")
        SM = [xrf[:, 2048 * i:2048 * i + 2048].bitcast(BF16).rearrange("p (j n) -> p j n", j=8) for i in range(2)]
        for ch in range(6):
            sm = SM[ch % 2]
            smk = "SM%d" % (ch % 2)
            rm_b = rmask.unsqueeze(2).to_broadcast([128, 8, 64])
            for (c0, src, key, neg) in ((0, bre, "k_bre", False), (64, bim, "k_bim", False), (128, bim, "k_bim", False),
                                        (192, bre, "k_bre", True)):
                srcb = src[:, ch, :].unsqueeze(1).to_broadcast([128, 8, 64])
                add("vector", lambda e, sm=sm, c0=c0, srcb=srcb: e.tensor_tensor(out=sm[:, :, c0:c0 + 64], in0=srcb,
                                                                                in1=rm_b, op=ALU.mult),
                    reads=[key, "pp"], writes=[smk + "_%d" % c0])
                if neg:
                    add("vector", lambda e, sm=sm, c0=c0: e.tensor_scalar(out=sm[:, :, c0:c0 + 64],
                                                                          in0=sm[:, :, c0:c0 + 64], scalar1=-1.0,
                                                                          scalar2=None, op0=ALU.mult),
                        reads=[smk + "_%d" % c0], writes=[smk + "_%d" % c0])
            for (c0, Cv, key) in ((256, C1v, "C1"), (384, C2v, "C2")):
                cb = Cv[:, ch, :].unsqueeze(1).to_broadcast([128, 8, 128])
                add("vector", lambda e, sm=sm, c0=c0, cb=cb: e.tensor_tensor(out=sm[:, :, c0:c0 + 128], in0=cb, in1=cmv,
                                                                            op=ALU.mult),
                    reads=[key, "cmk"], writes=[smk + "_%d" % c0])
            dst = self.s5mat[8 * ch:8 * ch + 8].rearrange("g p n -> p g n")
            add("sync", lambda e, sm=sm, dst=dst: e.dma_start(out=dst, in_=sm),
                reads=[smk + "_%d" % c0 for c0 in (0, 64, 128, 192, 256, 384)] + ["regC_tok"], writes=["s5mat"],
                ndma=1, lane=smk)
        self.barrier()
        o[0] = 0
        self.s5_r = self.s5_rp
        are, aim, ldt = (self.pp[:, PP_OFF[n][0]:PP_OFF[n][0] + 48] for n in ("s5_are", "s5_aim", "s5_ldt"))
        dt = carve(48)
        th = self.s5_th
        ph1 = carve(48)
        tmp48 = carve(48)
        kf48 = carve(48)
        ki48 = carve(48).bitcast(I32)
        add("scalar", lambda e: e.activation(out=dt, in_=ldt, func=AF.Exp), reads=["pp"], writes=["a_dt"])
        op2("vector", self.s5_r, are, dt, ALU.mult, ["pp", "a_dt"], ["s5_r"])
        add("scalar", lambda e: e.activation(out=self.s5_r, in_=self.s5_r, func=AF.Exp), reads=["s5_r"],
            writes=["s5_r"])
        op2("vector", th, aim, dt, ALU.mult, ["pp", "a_dt"], ["a_th"])
        self.reduce_angle(None, th, ki48, kf48, 48, ["a_th"])
        add("vector", lambda e: e.tensor_scalar(out=ph1, in0=th, scalar1=32.0, scalar2=None, op0=ALU.mult),
            reads=["a_th"], writes=["a_ph1"])
        self.reduce_angle(None, ph1, ki48, kf48, 48, ["a_ph1"])
        for c in range(4):
            add("vector", lambda e, c=c: e.tensor_scalar(out=tmp48, in0=ph1, scalar1=16.0 * c, scalar2=None,
                                                         op0=ALU.mult), reads=["a_ph1"], writes=["a_tmp48"])
            self.reduce_angle(None, tmp48, ki48, kf48, 48, ["a_tmp48"])
            add("vector", lambda e, c=c: e.tensor_copy(out=self.s5_phi0[:, c, :], in_=tmp48), reads=["a_tmp48"],
                writes=["s5_phi0"])
        for nm, _, _ in BIGW:
            if not nm.startswith("l1"):
                self.convert(nm)
        assert o[0] <= 11264, o[0]

    def table_gen(self, c, t0):
        add = self.add
        mixf = self.mix.rearrange("p a b -> p (a b)").bitcast(F32)
        angb = [mixf[:, 512 * i:512 * i + 512] for i in range(2)]
        kib = mixf[:, 1024:1536].bitcast(I32)
        kfb = mixf[:, 1536:2048]
        tabs = [mixf[:, 2048 + 1024 * i:3072 + 1024 * i].rearrange("p (a n) -> p a n", a=2) for i in range(2)]

        def stage1(g):
            ang, ak = angb[g % 2], "tg_ang%d" % (g % 2)
            tb, tk = tabs[g % 2], "tg_tab%d" % (g % 2)
            add("vector", lambda e: e.tensor_scalar(out=ang, in0=self.jrow, scalar1=self.s5_th[:, g:g + 1],
                                                    scalar2=self.s5_phi0[:, c, g:g + 1], op0=ALU.mult, op1=ALU.add),
                reads=["jrow", "a_th", "s5_phi0"], writes=[ak])
            add("vector", lambda e: e.tensor_scalar(out=kib, in0=ang, scalar1=1.0 / TWO_PI, scalar2=None,
                                                    op0=ALU.mult), reads=[ak], writes=["tg_ki"])
            add("vector", lambda e: e.tensor_copy(out=kfb, in_=kib), reads=["tg_ki"], writes=["tg_kf"])
            add("vector", lambda e: e.scalar_tensor_tensor(out=ang, in0=kfb, scalar=-TWO_PI, in1=ang, op0=ALU.mult,
                                                           op1=ALU.add), reads=["tg_kf", ak], writes=[ak])
            add("vector", lambda e: e.tensor_scalar(out=ang, in0=ang, scalar1=-PI_SAFE, scalar2=PI_SAFE, op0=ALU.max,
                                                    op1=ALU.min), reads=[ak], writes=[ak])
            add("vector", lambda e: e.tensor_scalar(out=tb[:, 0, :], in0=ang, scalar1=1.5707962, scalar2=-6.2831833,
                                                    op0=ALU.is_gt, op1=ALU.mult), reads=[ak], writes=[tk + "c"])
            add("vector", lambda e: e.scalar_tensor_tensor(out=tb[:, 0, :], in0=ang, scalar=1.5707955,
                                                           in1=tb[:, 0, :], op0=ALU.add, op1=ALU.add),
                reads=[ak, tk + "c"], writes=[tk + "c"])

        def stage2(g):
            ang, ak = angb[g % 2], "tg_ang%d" % (g % 2)
            tb, tk = tabs[g % 2], "tg_tab%d" % (g % 2)
            add("scalar", lambda e: e.activation(out=tb[:, 1, :], in_=ang, func=AF.Sin), reads=[ak], writes=[tk + "s"])
            add("scalar", lambda e: e.activation(out=tb[:, 0, :], in_=tb[:, 0, :], func=AF.Sin), reads=[tk + "c"],
                writes=[tk + "c"])
            dst = self.s5tab[g, :, :, t0:t0 + T]
            add("sync", lambda e: e.dma_start(out=dst, in_=tb), reads=[tk + "s", tk + "c", "regC_tok"],
                writes=["s5tab"], ndma=1, lane=tk)

        for g in range(49):
            if g < 48:
                stage1(g)
            if g >= 1:
                stage2(g - 1)
            yield

    def layer1_s5(self, c, t0):
        add = self.add
        xkeys = [("xbf", n) for n in range(16)]
        R = self.regC
        o = [0]

        def carve(n):
            a = R[:, o[0]:o[0] + n]
            o[0] += n
            return a
        uf = carve(512).rearrange("p (s n) -> p s n", s=1)
        ub = carve(512).bitcast(BF16).rearrange("p (s n) -> p s n", s=2)
        zbf = carve(1536).bitcast(BF16).rearrange("p (k n) -> p k n", k=6)
        NSM = 6
        NTB = 5
        mixf = self.mix.rearrange("p a b -> p (a b)").bitcast(F32)
        smb = [carve(256).bitcast(BF16) for _ in range(4)]
        smb += [mixf[:, 3584 + 256 * i:3584 + 256 * (i + 1)].bitcast(BF16) for i in range(2)]
        tbb = [carve(1024).rearrange("p (a n) -> p a n", a=2) for _ in range(3)]
        tbb += [mixf[:, 1536 + 1024 * i:1536 + 1024 * (i + 1)].rearrange("p (a n) -> p a n", a=2) for i in range(2)]
        cb = [carve(512) for _ in range(2)]
        c2b = [carve(512) for _ in range(2)]
        zb = [carve(512) for _ in range(2)]
        pzb = [carve(256).bitcast(BF16) for _ in range(2)]
        qzb = [carve(256).bitcast(BF16) for _ in range(2)]
        ytmp = carve(512)
        self.s5_o = o[0]
        assert o[0] <= 11264, o[0]
        gi = 0

        def stage_a(ch, j, s):
            nonlocal gi
            g = 8 * ch + j
            sm, smk = smb[gi % NSM], "s_sm%d" % (gi % NSM)
            tb, tk = tbb[gi % NTB], "s_tb%d" % (gi % NTB)
            cc, ck = cb[gi % 2], "s_c%d" % (gi % 2)
            c2, c2k = c2b[gi % 2], "s_c2%d" % (gi % 2)
            zz, zk = zb[gi % 2], "s_z%d" % (gi % 2)
            st = dict(g=g, j=j, sm=sm, smk=smk, tb=tb, tk=tk, zz=zz, zk=zk, b2=gi % 2, cc=cc, ck=ck, c2=c2, c2k=c2k)
            gi += 1
            add("sync", lambda e: e.dma_start(out=sm, in_=self.s5mat[g]), reads=["s5mat", "regC_tok"], writes=[smk],
                ndma=1, lane=smk)
            add("sync", lambda e: e.dma_start(out=tb, in_=self.s5tab[g, :, :, t0:t0 + T]),
                reads=["s5tab", "regC_tok"], writes=[tk], ndma=1, lane=tk)
            psa, pka = self.next_ps()
            add("tensor", lambda e: e.matmul(psa, lhsT=sm[:, 0:128], rhs=ub[:, s, :], start=True, stop=True),
                reads=[smk, "s_ub%d" % s], writes=[pka])
            psb, pkb = self.next_ps()
            add("tensor", lambda e: e.matmul(psb, lhsT=sm[:, 128:256], rhs=ub[:, s, :], start=True, stop=True),
                reads=[smk, "s_ub%d" % s], writes=[pkb])
            add("vector", lambda e: e.tensor_tensor(out=cc, in0=psa, in1=tb[:, 0, :], op=ALU.mult), reads=[pka, tk],
                writes=[ck])
            add("vector", lambda e: e.tensor_tensor(out=c2, in0=psb, in1=tb[:, 1, :], op=ALU.mult), reads=[pkb, tk],
                writes=[c2k])
            return st

        def stage_a2(st):
            g, cc, ck, c2, c2k, zz, zk = (st[k] for k in ("g", "cc", "ck", "c2", "c2k", "zz", "zk"))
            add("vector", lambda e: e.tensor_tensor(out=cc, in0=cc, in1=c2, op=ALU.add), reads=[ck, c2k], writes=[ck])
            add("vector", lambda e: e.tensor_tensor_scan(
                out=zz, data0=self.s5_r[:, g:g + 1].to_broadcast([128, T]), data1=cc,
                initial=self.s5_state[:, g:g + 1], op0=ALU.mult, op1=ALU.add),
                reads=[ck, "s5_r", ("s5_state", g)], writes=[zk])
            add("scalar", lambda e: e.activation(out=self.s5_state[:, g:g + 1], in_=zz[:, T - 1:T], func=AF.Identity),
                reads=[zk], writes=[("s5_state", g)])

        def stage_b(st, yps, yk):
            j, sm, smk, tb, tk, zz, zk, b2 = (st[k] for k in ("j", "sm", "smk", "tb", "tk", "zz", "zk", "b2"))
            pz, qz = pzb[b2], qzb[b2]
            pzk, qzk = "s_pz%d" % b2, "s_qz%d" % b2
            add("gpsimd", lambda e: e.tensor_tensor(out=pz, in0=zz, in1=tb[:, 0, :], op=ALU.mult), reads=[zk, tk],
                writes=[pzk])
            add("gpsimd", lambda e: e.tensor_tensor(out=qz, in0=zz, in1=tb[:, 1, :], op=ALU.mult), reads=[zk, tk],
                writes=[qzk])
            add("tensor", lambda e: e.matmul(yps, lhsT=sm[:, 256:384], rhs=pz, start=(j == 0), stop=False),
                reads=[smk, pzk], writes=[yk])
            add("tensor", lambda e: e.matmul(yps, lhsT=sm[:, 384:512], rhs=qz, start=False, stop=(j == 7)),
                reads=[smk, qzk], writes=[yk])

        def u_consume(ch, ps, pk):
            s = ch % 2
            add("scalar", lambda e: e.activation(out=uf[:, 0, :], in_=ps, func=AF.Identity), reads=[pk],
                writes=["s_uf"])
            add("vector", lambda e: e.tensor_copy(out=ub[:, s, :], in_=uf[:, 0, :]), reads=["s_uf"],
                writes=["s_ub%d" % s])
            yps, yk = self.long_ps()
            sts = []
            for j in range(8 + 2):
                if j < 8:
                    sts.append(stage_a(ch, j, s))
                if 0 <= j - 1 < 8:
                    stage_a2(sts[j - 1])
                if 0 <= j - 2 < 8:
                    stage_b(sts[j - 2], yps, yk)
            add("vector", lambda e: e.scalar_tensor_tensor(out=ytmp, in0=uf[:, 0, :], scalar=self.ppc("l1_D", ch),
                                                           in1=yps, op0=ALU.mult, op1=ALU.add),
                reads=[yk, "s_uf", "pp"], writes=["s_y"])
            add("scalar", lambda e: e.activation(out=zbf[:, ch, :], in_=ytmp, func=AF.Gelu), reads=["s_y"],
                writes=[("s_zbf", ch)])

        self.proj("l1_w_in", 16, [(128 * ch, 128) for ch in range(6)], self.xbf, xkeys, u_consume)
        if self.stop in ("s5a", "s5b"):
            return
        zkeys = [("s_zbf", k) for k in range(6)]

        def glu_consume(n, ps, pk):
            add("scalar", lambda e: e.activation(out=ytmp, in_=ps, func=AF.Sigmoid, bias=self.ppc("l1_glu_b", n)),
                reads=[pk, "pp"], writes=["s_y"])
            add("vector", lambda e: e.tensor_tensor(out=self.mix[:, n, :], in0=zbf[:, n, :], in1=ytmp, op=ALU.mult),
                reads=["s_y", ("s_zbf", n)], writes=[("mix", n)])
        self.proj("l1_glu_w", 6, [(128 * n, 128) for n in range(6)], zbf, zkeys, glu_consume)

    def layer1_attn(self, c, t0):
        add = self.add
        par = c % 2
        xkeys = [("xbf", n) for n in range(16)]
        R = self.regC
        o = [0]

        def carve(n):
            a = R[:, o[0]:o[0] + n]
            o[0] += n
            return a
        q1 = carve(3072).bitcast(BF16).rearrange("p (h t) -> p h t", h=12)
        self.att_carve(carve)
        v16l = carve(2048).bitcast(BF16).rearrange("p (r n) -> p r n", r=16)
        ob = [carve(512) for _ in range(3)]
        lb = [carve(512) for _ in range(3)]
        tA = carve(512)
        lbcs = [carve(64) for _ in range(2)]
        assert o[0] <= 11264, o[0]
        wv, wvk = self.wload("l1_w_in", 0, 16, 2560, 256)
        for qb in range(4):
            ps, pk = self.next_ps()
            for k in range(16):
                add("tensor", lambda e, k=k, ps=ps, qb=qb: e.matmul(ps[:, 0:256], lhsT=self.xbf[:, k, 128 * qb:128 * qb + 128],
                                                                    rhs=wv[:, k, :], start=(k == 0), stop=(k == 15)),
                    reads=[wvk] + xkeys, writes=[pk])
            add("scalar", lambda e, ps=ps, qb=qb: e.activation(out=self.V1[:, 4 * par + qb, :], in_=ps[:, 0:256],
                                                               func=AF.Identity), reads=[pk], writes=[("V1", 4 * par + qb)])
        for rho in range(4):
            ps, pk = self.next_ps()
            for k in range(16):
                add("tensor", lambda e, k=k, ps=ps, rho=rho: e.matmul(ps[:, 0:256], lhsT=self.xbf[:, k, rho:T:4],
                                                                      rhs=wv[:, k, :], start=(k == 0), stop=(k == 15)),
                    reads=[wvk] + xkeys, writes=[pk])
            add("vector", lambda e, ps=ps, rho=rho: e.tensor_copy(out=self.V4[:, 4 * par + rho, :], in_=ps[:, 0:256]),
                reads=[pk], writes=[("V4", 4 * par + rho)])
        vb = 32 * c if c < 3 else 0
        vdst = self.V16 if c < 3 else v16l
        for r2 in range(8):
            ps, pk = self.next_ps()
            for h2 in range(2):
                rho = 2 * r2 + h2
                for k in range(16):
                    add("tensor", lambda e, k=k, ps=ps, rho=rho, h2=h2: e.matmul(
                        ps[vb:vb + 32, 256 * h2:256 * h2 + 256], lhsT=self.xbf[:, k, rho:T:16], rhs=wv[:, k, :],
                        start=(k == 0), stop=(k == 15)), reads=[wvk] + xkeys, writes=[pk])
            add("scalar", lambda e, ps=ps, r2=r2: e.activation(
                out=vdst[vb:vb + 32, 2 * r2:2 * r2 + 2, :].rearrange("p a b -> p (a b)"), in_=ps[vb:vb + 32, :],
                func=AF.Identity), reads=[pk], writes=[("V16", c, r2)])
        v16keys = [("V16", cc, r2) for cc in range(c + 1) for r2 in range(8)]
        def k_consume(gi, ps, pk):
            kb = (gi % 2) * 64
            add("scalar", lambda e: e.activation(out=self.Kc[kb:kb + 64, gi // 2, t0:t0 + T], in_=ps[kb:kb + 64, :],
                                                 func=AF.Identity), reads=[pk], writes=[("Kc", gi, c)])
        self.proj("l1_w_in", 16, [(2304 + 64 * h, 64, (h % 2) * 64) for h in range(4)], self.xbf, xkeys, k_consume)

        lctr = [0]

        def lse_rows(st, stk, nq, dst_ps, dst_key, targets):
            li = lctr[0] % 2
            lctr[0] += 1
            lbc = lbcs[li]
            lk = "a_lbc%d" % li
            add("gpsimd", lambda e: e.tensor_copy(out=lbc[0:nq, 0:64], in_=st[:, 6:7].to_broadcast([nq, 64])),
                reads=[stk], writes=[lk])
            for (pb, c0, ncol, r0) in targets:
                add("tensor", lambda e, pb=pb, c0=c0, ncol=ncol, r0=r0: e.matmul(
                    dst_ps[pb:pb + 64, c0:c0 + ncol], lhsT=lbc[0:nq, 0:64], rhs=self.identf[0:nq, r0:r0 + ncol],
                    start=True, stop=True), reads=[lk, "identf"], writes=[dst_key])

        for kvp in range(2):
            cols = []
            for r in range(3):
                for kvl in range(2):
                    for g in range(2):
                        cols.append((768 + 256 * (2 * r + kvp) + 128 * kvl + 64 * g, 64, kvl * 64))

            def q_consume(gi, ps, pk):
                kvl = (gi // 2) % 2
                qb_ = kvl * 64
                add("scalar", lambda e: e.activation(out=q1[qb_:qb_ + 64, gi, :], in_=ps[qb_:qb_ + 64, :],
                                                     func=AF.Identity, scale=0.125), reads=[pk], writes=[("q1", gi)])
            self.proj("l1_w_in", 16, cols, self.xbf, xkeys, q_consume)
            for kvl in range(2):
                kv = 2 * kvp + kvl
                kb = kvl * 64
                kvi = kv // 2
                kkeys = [("Kc", kv, cc) for cc in range(c + 1)]
                units = []

                def qslot(r, g, kvl=kvl):
                    return (r * 2 + kvl) * 2 + g

                def evac_post(po, pok, pl, plk, i):
                    def post():
                        add("scalar", lambda e: e.activation(out=ob[i], in_=po, func=AF.Identity), reads=[pok],
                            writes=["a_o%d" % i])
                        add("vector", lambda e: e.tensor_copy(out=lb[i], in_=pl), reads=[plk], writes=["a_l%d" % i])
                    return post
                po, pok = self.long_ps()
                pl, plk = self.long_ps()
                N = 32 * (c + 1)
                for rho in range(16):
                    qparts = [(q1[kb:kb + 64, qslot(2, g), rho:T:16], 32 * g) for g in range(2)]
                    qk = [("q1", qslot(2, 0)), ("q1", qslot(2, 1))]
                    mrow = self.masks[0:64, 2 + c // 2, (c % 2) * 128:(c % 2) * 128 + 128]
                    if c < 3:
                        kvs = [(self.Kc[kb:kb + 64, kvi, rho:T * (c + 1):16], self.V16[0:N, rho, 64 * kv:64 * kv + 64])]
                        masks = [mrow[:, 0:N]]
                    else:
                        kvs = [(self.Kc[kb:kb + 64, kvi, rho:1536:16], self.V16[0:96, rho, 64 * kv:64 * kv + 64]),
                               (self.Kc[kb:kb + 64, kvi, 1536 + rho:2048:16], v16l[0:32, rho, 64 * kv:64 * kv + 64])]
                        masks = [mrow[:, 0:96], mrow[:, 96:128]]

                    def pv(u, po=po, pok=pok, pl=pl, plk=plk, rho=rho):
                        res = u["res"]
                        for g in range(2):
                            for bi, (pt, v_ap) in enumerate(res):
                                add("tensor", lambda e, pt=pt, v_ap=v_ap, bi=bi, nb=len(res), g=g: e.matmul(
                                    po[64 * g:64 * g + 64, 32 * rho:32 * rho + 32], lhsT=v_ap,
                                    rhs=pt[:, 32 * g:32 * g + 32], start=(bi == 0), stop=(bi == nb - 1)),
                                    reads=[u["ptsk"]] + v16keys, writes=[pok])
                        lse_rows(u["st"], u["stk"], 64, pl, plk, [(64 * g, 32 * rho, 32, 32 * g) for g in range(2)])
                    units.append(dict(q=qparts, qkeys=qk, kvs=kvs, kvkeys=kkeys + v16keys, masks=masks, nq=64, pv=pv))
                units[-1]["post"] = evac_post(po, pok, pl, plk, 2)
                po, pok = self.long_ps()
                pl, plk = self.long_ps()
                for g in range(2):
                    for qb in range(4):
                        kvs, vkeys, masks = [], [], []
                        tq = t0 + 128 * qb
                        if not (c == 0 and qb == 0):
                            if qb > 0:
                                vp = self.V1[:, 4 * par + qb - 1, 64 * kv:64 * kv + 64]
                                vkeys.append(("V1", 4 * par + qb - 1))
                            else:
                                vp = self.V1[:, 4 * (1 - par) + 3, 64 * kv:64 * kv + 64]
                                vkeys.append(("V1", 4 * (1 - par) + 3))
                            kvs.append((self.Kc[kb:kb + 64, kvi, tq - 128:tq], vp))
                            masks.append(self.masks[:, 1, 0:128])
                        kvs.append((self.Kc[kb:kb + 64, kvi, tq:tq + 128], self.V1[:, 4 * par + qb, 64 * kv:64 * kv + 64]))
                        vkeys.append(("V1", 4 * par + qb))
                        masks.append(self.masks[:, 1, 128:256])

                        def pv(u, po=po, pok=pok, pl=pl, plk=plk, g=g, qb=qb, vkeys=vkeys):
                            res = u["res"]
                            for bi, (pt, v_ap) in enumerate(res):
                                add("tensor", lambda e, pt=pt, v_ap=v_ap, bi=bi, nb=len(res): e.matmul(
                                    po[64 * g:64 * g + 64, 128 * qb:128 * qb + 128], lhsT=v_ap, rhs=pt, start=(bi == 0),
                                    stop=(bi == nb - 1)), reads=[u["ptsk"]] + vkeys, writes=[pok])
                            lse_rows(u["st"], u["stk"], 128, pl, plk, [(64 * g, 128 * qb, 128, 0)])
                        units.append(dict(q=q1[kb:kb + 64, qslot(0, g), 128 * qb:128 * qb + 128],
                                          qkeys=[("q1", qslot(0, g))], kvs=kvs, kvkeys=kkeys + vkeys, masks=masks,
                                          nq=128, pv=pv, mask_all=(self.masks[:, 1, :] if len(kvs) == 2 else None)))
                units[-1]["post"] = evac_post(po, pok, pl, plk, 0)
                po, pok = self.long_ps()
                pl, plk = self.long_ps()
                for g in range(2):
                    for rho in range(4):
                        kvs, vkeys, masks = [], [], []
                        if c > 0:
                            kvs.append((self.Kc[kb:kb + 64, kvi, t0 - T + rho:t0:4],
                                        self.V4[:, 4 * (1 - par) + rho, 64 * kv:64 * kv + 64]))
                            vkeys.append(("V4", 4 * (1 - par) + rho))
                            masks.append(self.masks[:, 1, 0:128])
                        kvs.append((self.Kc[kb:kb + 64, kvi, t0 + rho:t0 + T:4],
                                    self.V4[:, 4 * par + rho, 64 * kv:64 * kv + 64]))
                        vkeys.append(("V4", 4 * par + rho))
                        masks.append(self.masks[:, 1, 128:256])

                        def pv(u, po=po, pok=pok, pl=pl, plk=plk, g=g, rho=rho, vkeys=vkeys):
                            res = u["res"]
                            for bi, (pt, v_ap) in enumerate(res):
                                add("tensor", lambda e, pt=pt, v_ap=v_ap, bi=bi, nb=len(res): e.matmul(
                                    po[64 * g:64 * g + 64, 128 * rho:128 * rho + 128], lhsT=v_ap, rhs=pt,
                                    start=(bi == 0), stop=(bi == nb - 1)), reads=[u["ptsk"]] + vkeys, writes=[pok])
                            lse_rows(u["st"], u["stk"], 128, pl, plk, [(64 * g, 128 * rho, 128, 0)])
                        units.append(dict(q=q1[kb:kb + 64, qslot(1, g), rho:T:4], qkeys=[("q1", qslot(1, g))], kvs=kvs,
                                          kvkeys=kkeys + vkeys, masks=masks, nq=128, pv=pv,
                                          mask_all=(self.masks[:, 1, :] if len(kvs) == 2 else None)))
                units[-1]["post"] = evac_post(po, pok, pl, plk, 1)
                self.attn_pipeline(units)
                def v4(a):
                    return a.rearrange("p (i r) -> p i r", r=4)

                def s4(a):
                    return a.rearrange("p (r i) -> p i r", r=4)

                def v16(a):
                    return a.rearrange("p (q r) -> p q r", r=16)

                def s16(a):
                    return a.rearrange("p (r q) -> p q r", r=16)
                add("vector", lambda e: e.tensor_tensor(out=v4(tA), in0=v4(lb[0]), in1=s4(lb[1]), op=ALU.max),
                    reads=["a_l0", "a_l1"], writes=["a_tA"])
                add("vector", lambda e: e.tensor_tensor(out=v16(tA), in0=v16(tA), in1=s16(lb[2]), op=ALU.max),
                    reads=["a_tA", "a_l2"], writes=["a_tA"])
                add("gpsimd", lambda e: e.tensor_tensor(out=lb[0], in0=lb[0], in1=tA, op=ALU.subtract),
                    reads=["a_l0", "a_tA"], writes=["a_l0"])
                add("gpsimd", lambda e: e.tensor_tensor(out=s4(lb[1]), in0=s4(lb[1]), in1=v4(tA), op=ALU.subtract),
                    reads=["a_l1", "a_tA"], writes=["a_l1"])
                add("vector", lambda e: e.tensor_tensor(out=s16(lb[2]), in0=s16(lb[2]), in1=v16(tA), op=ALU.subtract),
                    reads=["a_l2", "a_tA"], writes=["a_l2"])
                for i in range(3):
                    add("scalar", lambda e, i=i: e.activation(out=lb[i], in_=lb[i], func=AF.Exp), reads=["a_l%d" % i],
                        writes=["a_l%d" % i])
                    add("gpsimd" if i < 2 else "vector", lambda e, i=i: e.tensor_tensor(out=ob[i], in0=ob[i], in1=lb[i],
                                                                                      op=ALU.mult),
                        reads=["a_l%d" % i, "a_o%d" % i], writes=["a_o%d" % i])
                add("vector", lambda e: e.tensor_tensor(out=v4(tA), in0=v4(lb[0]), in1=s4(lb[1]), op=ALU.add),
                    reads=["a_l0", "a_l1", "a_tA"], writes=["a_tA"])
                add("vector", lambda e: e.tensor_tensor(out=v16(tA), in0=v16(tA), in1=s16(lb[2]), op=ALU.add),
                    reads=["a_tA", "a_l2"], writes=["a_tA"])
                add("vector", lambda e: e.reciprocal(out=tA, in_=tA), reads=["a_tA"], writes=["a_tA"])
                add("gpsimd", lambda e: e.tensor_tensor(out=v4(ob[0]), in0=v4(ob[0]), in1=s4(ob[1]), op=ALU.add),
                    reads=["a_o0", "a_o1"], writes=["a_o0"])
                add("gpsimd", lambda e: e.tensor_tensor(out=v16(ob[0]), in0=v16(ob[0]), in1=s16(ob[2]), op=ALU.add),
                    reads=["a_o0", "a_o2"], writes=["a_o0"])
                add("vector", lambda e, kv=kv: e.tensor_tensor(out=self.mix[:, 6 + kv, :], in0=ob[0], in1=tA,
                                                               op=ALU.mult), reads=["a_o0", "a_tA"],
                    writes=[("mix", 6 + kv)])

    def w_out_ln(self, wname, kc, gname, bname):
        add = self.add
        mkeys = [("mix", n) for n in range(kc)]

        def consume(n, ps, pk):
            add("vector", lambda e: e.scalar_tensor_tensor(out=self.xres[:, n, :], in0=self.xres[:, n, :], scalar=ALPHA,
                                                           in1=ps, op0=ALU.mult, op1=ALU.add),
                reads=[pk, ("xres", n)], writes=[("xres", n)])
        self.proj(wname, kc, [(128 * n, 128) for n in range(16)], self.mix, mkeys, consume)
        self.layer_norm(gname, bname)

    def ffn_carve(self):
        if not hasattr(self, "ffn_tmp"):
            pass

    def tile(self, s, c):
        add = self.add
        t0 = c * T
        src = self.din["xT"][s, :, t0:t0 + T].rearrange("(k p) t -> p k t", p=128)
        add("scalar", lambda e: e.dma_start(out=self.xres, in_=src), writes=[("xres", n) for n in range(16)], ndma=1,
            lane="xres")
        for n in range(16):
            add("vector" if n % 2 else "scalar",
                (lambda e, n=n: e.tensor_copy(out=self.xbf[:, n, :], in_=self.xres[:, n, :])) if n % 2 else
                (lambda e, n=n: e.activation(out=self.xbf[:, n, :], in_=self.xres[:, n, :], func=AF.Identity)),
                reads=[("xres", n)], writes=[("xbf", n)])
        if s == 0 and c == 0 and self.layers == 2:
            for nm, _, _ in BIGW:
                if nm.startswith("l1"):
                    self.convert(nm)
        if c == 0:
            add("gpsimd", lambda e: e.memset(self.xa_halo, 0.0), writes=[("xa_halo", j) for j in range(8)])
            add("gpsimd", lambda e: e.memset(self.lru_h, 0.0), writes=[("lru_h", j) for j in range(8)])
            for l in range(2):
                add("gpsimd", lambda e, l=l: e.memset(self.fhalo[l], 0.0),
                    writes=[("fhalo", l, cc) for cc in range(88)])
        self.barrier()
        self.layer0_mixer(c)
        if self.dbg and s == 0 and c == 0:
            self.debug_store("dbg_mix0", self.mix, [("mix", n) for n in range(16)])
        self.w_out_ln("l0_w_out", 16, "l0_ln1_g", "l0_ln1_b")
        if self.dbg and s == 0 and c == 0:
            self.debug_store("dbg_x1_0", self.xres, [("xres", n) for n in range(16)])
        self.barrier()
        if s == 0 and self.layers == 2:
            tg = self.table_gen(c, t0)

            def tg_hook():
                next(tg, None)
                next(tg, None)
            self.ffn(0, hook=tg_hook)
            for _ in tg:
                pass
        else:
            self.ffn(0)
        last = (self.layers == 1) or self.stop == "prologue"
        self.layer_norm("l0_ln2_g", "l0_ln2_b", store_out=(self.store_chunk(s, t0) if last else None))
        if last:
            return
        if self.dbg and s == 0 and c == 0:
            self.debug_store("dbg_x2_0", self.xres, [("xres", n) for n in range(16)])
        if c == 0:
            add("gpsimd", lambda e: e.memset(self.s5_state, 0.0), writes=[("s5_state", g) for g in range(48)])
        self.barrier()
        if self.stop != "s5z":
            self.layer1_s5(c, t0)
        if self.stop in ("s5", "s5a", "s5z", "s5b"):
            self.debug_store("dbg_mix1", self.mix[:, 0:6, :], [("mix", n) for n in range(6)])
            for n in range(16):
                self.store_chunk(s, t0)(n)
            return
        self.barrier()
        self.layer1_attn(c, t0)
        if self.dbg and s == 0 and c == 0:
            self.debug_store("dbg_mix1", self.mix[:, 0:10, :], [("mix", n) for n in range(10)])
        self.w_out_ln("l1_w_out", 10, "l1_ln1_g", "l1_ln1_b")
        if self.dbg and s == 0 and c == 0:
            self.debug_store("dbg_x1_1", self.xres, [("xres", n) for n in range(16)])
        self.barrier()
        self.ffn(1)
        self.layer_norm("l1_ln2_g", "l1_ln2_b", store_out=self.store_chunk(s, t0))

    def store_chunk(self, s, t0):
        def f(n):
            dst = self.outT[s, 128 * n:128 * n + 128, t0:t0 + T]
            self.finals.append(self.add("gpsimd", lambda e: e.dma_start(out=dst, in_=self.xres[:, n, :]),
                                        reads=[("xres", n)], writes=[("out", n)], ndma=1, lane="out"))
        return f

    def build(self):
        self.setup()
        if self.layers == 2:
            self.s5_prologue()
        self.barrier()
        for s in range(self.nseq):
            for c in range(self.nt):
                self.tile(s, c)
        nsem = self.S.emit(final_waits=self.finals)
        return nsem


_CACHE = {}


def kernel(**inputs):
    hp = host_prepare(inputs)
    x = np.asarray(inputs["x"], np.float32)
    ncores = 8
    nseq = x.shape[0] // ncores
    key = "full"
    if key not in _CACHE:
        b = Builder(nseq=nseq, nt=4, layers=2)
        b.build()
        _CACHE[key] = b
    b = _CACHE[key]
    shared = {nm: np.ascontiguousarray(np.asarray(inputs[nm], np.float32)) for nm, _, _ in BIGW}
    shared.update(hp)
    in_maps = []
    for i in range(ncores):
        m = dict(shared)
        m["xT"] = np.ascontiguousarray(x[i * nseq:(i + 1) * nseq].transpose(0, 2, 1))
        in_maps.append(m)
    res = run_bass_kernel_spmd(b.nc, in_maps, core_ids=list(range(ncores)))
    out = np.empty_like(x)
    for i in range(ncores):
        out[i * nseq:(i + 1) * nseq] = res.results[i]["outT"].transpose(0, 2, 1)
    return out
```

```python
import contextlib
import math
import numpy as np
import concourse.bass as bass
import concourse.mybir as mybir
from concourse.bass_utils import run_bass_kernel_spmd

F32 = mybir.dt.float32
BF16 = mybir.dt.bfloat16
I32 = mybir.dt.int32
AF = mybir.ActivationFunctionType
ALU = mybir.AluOpType
AX = mybir.AxisListType

D = 2048
SEQ = 2048
T = 512
DFF = 5632
ALPHA = 4.0 ** 0.25
LN_EPS = 1e-5
NEG = -30000.0
TWO_PI = 2.0 * math.pi
PI_SAFE = 3.1415925
SIN_SCALE = 1.0 - 2e-6

ENGS = ["tensor", "vector", "scalar", "gpsimd", "sync"]
EPOCH = 24000


class Op:
    __slots__ = ("eng", "fn", "deps", "idx", "signal", "ndma", "lane", "lane_cum", "cum")


class Sched:
    def __init__(self, nc):
        self.nc = nc
        self.ops = {e: [] for e in ENGS}
        self.last_w = {}
        self.readers = {}
        self.lanes = {}

    def add(self, eng, fn, reads=(), writes=(), ndma=0, lane=None):
        op = Op()
        op.eng = eng
        op.fn = fn
        op.signal = False
        op.ndma = ndma
        op.lane = lane
        op.cum = 0
        op.lane_cum = 0
        dd = {}
        for r in reads:
            w = self.last_w.get(r)
            if w is not None:
                dd[id(w)] = w
        for r in writes:
            w = self.last_w.get(r)
            if w is not None:
                dd[id(w)] = w
            for rd in self.readers.get(r, ()):
                dd[id(rd)] = rd
        for r in reads:
            self.readers.setdefault(r, []).append(op)
        for r in writes:
            self.last_w[r] = op
            self.readers[r] = []
        op.idx = len(self.ops[eng])
        deps = []
        for d in dd.values():
            if d is op:
                continue
            if d.eng == "tensor" and eng == "tensor":
                continue
            deps.append(d)
        op.deps = deps
        for d in deps:
            d.signal = True
        if ndma:
            L = self.lanes.setdefault(lane, [])
            op.lane_cum = (L[-1].lane_cum if L else 0) + 16 * ndma
            L.append(op)
        self.ops[eng].append(op)
        return op

    def emit(self, final_waits=()):
        nc = self.nc
        for e in ENGS:
            c = 0
            for op in self.ops[e]:
                if op.ndma == 0 and op.signal:
                    c += 1
                op.cum = c
        with contextlib.ExitStack() as es:
            sems = {}

            def get_sem(key):
                if key not in sems:
                    sems[key] = es.enter_context(nc.semaphore("s%d" % len(sems)))
                return sems[key]

            def sem_for(op):
                if op.ndma:
                    if op.lane in ("once", "once2"):
                        return get_sem(("lane", op.lane)), self.lanes[op.lane][-1].lane_cum
                    return get_sem(("lane", op.lane)), op.lane_cum
                ep = (op.cum - 1) // EPOCH
                return get_sem((op.eng, ep)), op.cum - ep * EPOCH

            for e in ENGS:
                for op in self.ops[e]:
                    if op.signal or op.ndma:
                        sem_for(op)
            block = es.enter_context(nc.Block())

            def run(engname):
                def body(eng):
                    seen = {}
                    for op in self.ops[engname]:
                        need = {}
                        for d in op.deps:
                            s, v = sem_for(d)
                            k = id(s)
                            if seen.get(k, -1) >= v:
                                continue
                            if k not in need or need[k][1] < v:
                                need[k] = (s, v)
                        for k, (s, v) in need.items():
                            seen[k] = v
                            eng.wait_ge(s, v)
                        ins = op.fn(eng)
                        if op.ndma:
                            s, _ = sem_for(op)
                            if isinstance(ins, (list, tuple)):
                                for i in ins:
                                    i.then_inc(s, 16)
                            else:
                                ins.then_inc(s, 16)
                        elif op.signal:
                            s, _ = sem_for(op)
                            ins.then_inc(s, 1)
                    if engname == "sync":
                        fin = {}
                        for d in final_waits:
                            s, v = sem_for(d)
                            if id(s) not in fin or fin[id(s)][1] < v:
                                fin[id(s)] = (s, v)
                        for s, v in fin.values():
                            eng.wait_ge(s, v)
                return body

            block.tensor(run("tensor"))
            block.vector(run("vector"))
            block.scalar(run("scalar"))
            block.gpsimd(run("gpsimd"))
            block.sync(run("sync"))
        return len(sems)


def _cols(v):
    v = np.asarray(v, np.float32)
    return np.ascontiguousarray(v.reshape(-1, 128).T)


PP_LAYOUT = [
    ("l0_conv_w", 32), ("l0_conv_b", 8), ("l0_gx_b", 8), ("l0_ga_b", 8), ("l0_L", 8),
    ("l0_ln1_g", 16), ("l0_ln1_b", 16), ("l0_ln2_g", 16), ("l0_ln2_b", 16),
    ("l0_fcw", 264), ("l0_fcb", 88), ("l0_sinks", 16),
    ("l1_ln1_g", 16), ("l1_ln1_b", 16), ("l1_ln2_g", 16), ("l1_ln2_b", 16),
    ("l1_fcw", 264), ("l1_fcb", 88), ("l1_D", 6), ("l1_glu_b", 6),
    ("s5_are", 48), ("s5_aim", 48), ("s5_ldt", 48),
    ("rowmask", 8), ("sgn1", 1), ("negone", 1),
]
PP_OFF = {}
_o = 0
for _n, _w in PP_LAYOUT:
    PP_OFF[_n] = (_o, _w)
    _o += _w
PP_W = _o


def host_prepare(inp):
    f = np.float32
    out = {}
    pp = np.zeros((128, PP_W), f)

    def put(name, arr):
        o, w = PP_OFF[name]
        arr = np.asarray(arr, f).reshape(128, w)
        pp[:, o:o + w] = arr

    put("l0_conv_w", np.asarray(inp["l0_lru_conv_w"], f).reshape(4, 8, 128).transpose(2, 0, 1))
    put("l0_conv_b", _cols(inp["l0_lru_conv_b"]))
    put("l0_gx_b", _cols(inp["l0_lru_gx_b"]))
    put("l0_ga_b", _cols(inp["l0_lru_ga_b"]))
    put("l0_L", _cols(inp["l0_lru_L"]))
    for l in (0, 1):
        for nm in ("ln1_g", "ln1_b", "ln2_g", "ln2_b"):
            put("l%d_%s" % (l, nm), _cols(inp["l%d_%s" % (l, nm)]))
        fcw_src = (inp["l0_ffn_conv_w"], inp["l1_ffn_conv_w"])[l]
        fcb_src = (inp["l0_ffn_conv_b"], inp["l1_ffn_conv_b"])[l]
        put("l%d_fcw" % l, np.asarray(fcw_src, f).reshape(3, 88, 128).transpose(2, 0, 1))
        put("l%d_fcb" % l, _cols(fcb_src))
    put("l0_sinks", np.broadcast_to(np.asarray(inp["l0_sinks"], f)[None, :], (128, 16)))
    put("l1_D", _cols(inp["l1_s5_D"]))
    put("l1_glu_b", _cols(inp["l1_glu_b"]))
    are = np.asarray(inp["l1_s5_A_re"], f)
    aim = np.asarray(inp["l1_s5_A_im"], f)
    ldt = np.asarray(inp["l1_s5_log_dt"], f)
    put("s5_are", np.concatenate([are.T, are.T], 0))
    put("s5_aim", np.concatenate([aim.T, aim.T], 0))
    put("s5_ldt", np.broadcast_to(ldt[None, :], (128, 48)))
    rm = np.zeros((128, 8), f)
    for j in range(8):
        rm[16 * j:16 * j + 16, j] = 1.0
    put("rowmask", rm)
    sg = np.ones((128, 1), f)
    sg[64:] = -1.0
    put("sgn1", sg)
    put("negone", -np.ones((128, 1), f))
    out["pp"] = pp
    for nm, key in (("gxw", "l0_lru_gx_w"), ("gaw", "l0_lru_ga_w")):
        w = np.asarray(inp[key], f)
        bd = np.zeros((128, 8, 128), f)
        for j in range(8):
            bd[0:64, j, 0:64] = w[2 * j]
            bd[64:128, j, 64:128] = w[2 * j + 1]
        out[nm] = bd
    Bre = np.asarray(inp["l1_s5_B_re"], f).reshape(6, 8, 64, 16)
    Bim = np.asarray(inp["l1_s5_B_im"], f).reshape(6, 8, 64, 16)
    out["s5_Bre"] = np.ascontiguousarray(Bre.transpose(1, 3, 0, 2).reshape(128, 6, 64))
    out["s5_Bim"] = np.ascontiguousarray(Bim.transpose(1, 3, 0, 2).reshape(128, 6, 64))
    def kl(a):
        a = a.reshape(6, 8, 64)
        return np.ascontiguousarray(np.broadcast_to(a.transpose(1, 0, 2)[:, None], (8, 16, 6, 64)).reshape(128, 6, 64))
    out["s5_kare"] = kl(are)
    out["s5_kaim"] = kl(aim)
    out["s5_kldt"] = kl(np.broadcast_to(ldt[:, None], (48, 64)).copy())
    Cre = np.asarray(inp["l1_s5_C_re"], f).reshape(6, 8, 16, 64)
    Cim = np.asarray(inp["l1_s5_C_im"], f).reshape(6, 8, 16, 64)
    cre_l = Cre.transpose(3, 0, 1, 2).reshape(64, 6, 128)
    cim_l = Cim.transpose(3, 0, 1, 2).reshape(64, 6, 128)
    out["s5_C1"] = np.ascontiguousarray(np.concatenate([cre_l, cim_l], 0))
    out["s5_C2"] = np.ascontiguousarray(np.concatenate([cim_l, cre_l], 0))
    cm = np.zeros((128, 8, 128), f)
    for j in range(8):
        cm[:, j, 16 * j:16 * j + 16] = 1.0
    out["colmask"] = cm
    out["ident"] = np.eye(128, dtype=f)
    qi = np.arange(128)[:, None]
    kj = np.arange(256)[None, :]
    msk = np.zeros((128, 4, 256), f)
    msk[:, 0] = np.where((kj >= qi + 1) & (kj <= qi + 128), 0.0, NEG)
    msk[:, 1] = np.where((kj >= qi) & (kj <= qi + 128), 0.0, NEG)
    q32 = (np.arange(128) % 32)[:, None]
    ki = np.arange(128)[None, :]
    for c in range(4):
        msk[:, 2 + c // 2, (c % 2) * 128:(c % 2) * 128 + 128] = np.where(ki <= 32 * c + q32, 0.0, NEG)
    out["masks"] = msk
    tt = np.arange(SEQ)
    ab = np.zeros((128, 2, SEQ), f)
    ab[:, 0] = (tt // 32)[None, :]
    ab[:, 1] = (tt % 32)[None, :]
    out["iota_ab"] = ab
    out["jrow"] = np.ascontiguousarray(np.broadcast_to(np.arange(T, dtype=f)[None, :], (128, T)))
    return out


BIGW = [
    ("l0_w_in", 2048, 3584), ("l0_w_out", 2048, 2048), ("l0_ffn_up", 2048, 11264), ("l0_ffn_down", 5632, 2048),
    ("l1_w_in", 2048, 2816), ("l1_glu_w", 768, 768), ("l1_w_out", 1280, 2048), ("l1_ffn_up", 2048, 11264),
    ("l1_ffn_down", 5632, 2048),
]
SMALL_IN = [
    ("pp", (128, PP_W)), ("gxw", (128, 8, 128)), ("gaw", (128, 8, 128)),
    ("s5_Bre", (128, 6, 64)), ("s5_Bim", (128, 6, 64)), ("s5_kare", (128, 6, 64)), ("s5_kaim", (128, 6, 64)),
    ("s5_kldt", (128, 6, 64)), ("s5_C1", (128, 6, 128)), ("s5_C2", (128, 6, 128)), ("colmask", (128, 8, 128)),
    ("ident", (128, 128)), ("masks", (128, 4, 256)), ("iota_ab", (128, 2, SEQ)), ("jrow", (128, T)),
]


class Builder:
    def __init__(self, nseq=2, nt=4, layers=2, dbg=False, stop=None):
        self.nseq, self.nt, self.layers, self.dbg = nseq, nt, layers, dbg
        self.stop = stop
        nc = self.nc = bass.Bass("TRN2", target_bir_lowering=False)
        self.S = Sched(nc)
        self.din = {}
        self.din["xT"] = nc.dram_tensor("xT", [nseq, D, SEQ], F32, kind="ExternalInput").ap()
        for nm, r, c in BIGW:
            self.din[nm] = nc.dram_tensor(nm, [r, c], F32, kind="ExternalInput").ap()
        for nm, shp in SMALL_IN:
            self.din[nm] = nc.dram_tensor(nm, list(shp), F32, kind="ExternalInput").ap()
        self.outT = nc.dram_tensor("outT", [nseq, D, SEQ], F32, kind="ExternalOutput").ap()
        self.wb = {}
        for nm, r, c in BIGW:
            self.wb[nm] = nc.dram_tensor(nm + "_bf", [r, c], BF16, kind="Internal").ap()
        self.s5tab = nc.dram_tensor("s5tab", [48, 128, 2, SEQ], F32, kind="Internal").ap()
        self.s5mat = nc.dram_tensor("s5mat", [48, 128, 512], BF16, kind="Internal").ap()
        self.dbg_out = {}
        self.finals = []
        self.uid = 0
        self.ps_rr = 0
        self.ws_rr = 0
        self.lps_rr = 0
        self.ps_nrot = 6
        self.att_ctr = 0

    def sb(self, name, shape, dt=F32):
        return self.nc.alloc_sbuf_tensor("sb_" + name, list(shape), dt).ap()

    def add(self, eng, fn, reads=(), writes=(), **kw):
        if eng != "sync" and "regC_tok" not in writes:
            reads = list(reads) + ["regC_tok"]
        return self.S.add(eng, fn, reads=reads, writes=writes, **kw)

    def next_ps(self):
        i = self.ps_rr % self.ps_nrot
        self.ps_rr += 1
        return self.ps[i], "ps%d" % i

    def long_ps(self):
        i = 6 + (self.lps_rr % 2)
        self.lps_rr += 1
        return self.ps[i], "ps%d" % i

    def debug_store(self, name, src_ap, keys):
        if not self.dbg:
            return
        o = self.nc.dram_tensor(name, list(src_ap.shape), src_ap.dtype, kind="ExternalOutput").ap()
        self.dbg_out[name] = o
        self.finals.append(self.add("gpsimd", lambda e: e.dma_start(out=o, in_=src_ap), reads=keys, writes=[name],
                                    ndma=1, lane=name))

    def wload(self, wname, r0, kc, c0, nb, prows=128):
        i = self.ws_rr % self.nws
        self.ws_rr += 1
        slot = self.ws[i]
        assert kc * nb <= self.ws_elems
        view = slot[0:prows, 0:kc * nb].rearrange("p (k n) -> p k n", k=kc)
        src = self.wb[wname][r0:r0 + prows * kc, c0:c0 + nb].rearrange("(k p) n -> p k n", p=prows)
        key = "ws%d" % i
        self.add("sync", lambda e: e.dma_start(out=view, in_=src), reads=["wb_" + wname], writes=[key], ndma=1,
                 lane=key)
        return view, key

    def setup(self):
        nc = self.nc
        add = self.add
        self.ps = [nc.alloc_psum_tensor("psb%d" % i, [128, 512], F32).ap() for i in range(8)]
        self.pp = self.sb("pp", [128, PP_W])
        add("gpsimd", lambda e: e.dma_start(out=self.pp, in_=self.din["pp"]), writes=["pp"], ndma=1, lane="once")
        self.identb = self.sb("identb", [128, 128], BF16)
        add("gpsimd", lambda e: e.dma_start(out=self.identb, in_=self.din["ident"]), writes=["identb"], ndma=1,
            lane="once")
        self.identf = self.sb("identf", [128, 128])
        add("gpsimd", lambda e: e.dma_start(out=self.identf, in_=self.din["ident"]), writes=["identf"], ndma=1,
            lane="once")
        self.masks = self.sb("masks", [128, 4, 256])
        add("gpsimd", lambda e: e.dma_start(out=self.masks, in_=self.din["masks"]), writes=["masks"], ndma=1,
            lane="once")
        self.onesD = self.sb("onesD", [128, 128])
        add("vector", lambda e: e.memset(self.onesD, 1.0 / D), writes=["onesD"])
        self.gxw = self.sb("gxw", [128, 8, 128], BF16)
        self.gaw = self.sb("gaw", [128, 8, 128], BF16)
        add("gpsimd", lambda e: e.dma_start(out=self.gxw, in_=self.din["gxw"]), writes=["gxw"], ndma=1, lane="once")
        add("gpsimd", lambda e: e.dma_start(out=self.gaw, in_=self.din["gaw"]), writes=["gaw"], ndma=1, lane="once")
        if self.layers == 1:
            for nm, _, _ in BIGW:
                if not nm.startswith("l1"):
                    self.convert(nm)
        self.xres = self.sb("xres", [128, 16, T])
        self.xbf = self.sb("xbf", [128, 16, T], BF16)
        self.mix = self.sb("mix", [128, 16, T], BF16)
        self.regC = self.sb("regC", [128, 11264])
        self.g = self.regC.bitcast(BF16).rearrange("p (k n) -> p k n", k=44)
        self.fhalo = [self.sb("fhalo%d" % l, [128, 88, 2]) for l in range(2)]
        self.xa_halo = self.sb("xa_halo", [128, 8, 3])
        self.lru_h = self.sb("lru_h", [128, 8])
        self.kbuf0 = self.sb("kbuf0", [128, 2, 2 * T], BF16)
        self.vtok0 = self.sb("vtok0", [128, 8, 256], BF16)
        self.small = self.sb("small", [128, 64])
        self.cpcol = self.sb("cpcol", [128, 8])
        if self.layers == 2:
            self.Kc = self.sb("Kc", [128, 2, SEQ], BF16)
            self.V1 = self.sb("V1", [128, 8, 256], BF16)
            self.V4 = self.sb("V4", [128, 8, 256], BF16)
            self.V16 = self.sb("V16", [128, 16, 256], BF16)
            self.s5_rp = self.sb("s5_r", [128, 48])
            self.s5_state = self.sb("s5_state", [128, 48])
            self.s5_th = self.sb("s5_th", [128, 48])
            self.s5_phi0 = self.sb("s5_phi0", [128, 4, 48])
            self.jrow = self.sb("jrow", [128, T])
            self.add("gpsimd", lambda e: e.dma_start(out=self.jrow, in_=self.din["jrow"]), writes=["jrow"], ndma=1,
                     lane="once")
        self.ffn_tmp = {"hb": self.sb("f_hb", [128, 4, T + 2]), "acc": self.sb("f_acc", [128, 4, T])}
        self.lnscr = self.ffn_tmp["acc"]
        self.ws_elems = 4096
        rem = nc.sbuf_bytes_remaining
        self.nws = max(2, min(6, (rem - 2048) // (self.ws_elems * 2)))
        self.ws = [self.sb("ws%d" % i, [128, self.ws_elems], BF16) for i in range(self.nws)]
        o, w = PP_OFF["l0_L"]
        tmp = self.small[:, 0:8]
        add("scalar", lambda e: e.activation(out=tmp, in_=self.pp[:, o:o + 8], func=AF.Exp, scale=-1.0),
            reads=["pp"], writes=["small"])
        add("scalar", lambda e: e.activation(out=tmp, in_=tmp, func=AF.Ln, bias=1.0), reads=["small"],
            writes=["small"])
        add("vector", lambda e: e.tensor_scalar(out=self.cpcol, in0=tmp, scalar1=-8.0, scalar2=None, op0=ALU.mult),
            reads=["small"], writes=["cpcol"])

    def convert(self, nm):
        r, c = [(rr, cc) for (n_, rr, cc) in BIGW if n_ == nm][0]
        nsplit = max(1, r // 512)
        rows = r // nsplit

        def fn(e):
            return [e.dma_start(out=self.wb[nm][i * rows:(i + 1) * rows, :],
                                in_=self.din[nm][i * rows:(i + 1) * rows, :]) for i in range(nsplit)]
        self.S.add("gpsimd", fn, writes=["wb_" + nm], ndma=nsplit, lane="wb_" + nm)

    def ppc(self, name, i=0, n=1):
        o, w = PP_OFF[name]
        return self.pp[:, o + i:o + i + n]

    def barrier(self):
        self.add("gpsimd", lambda e: e.memset(self.small[:, 63:64], 0.0), writes=["regC_tok", "small63"])

    def proj(self, wname, kc, cols, act, act_keys, consume, nb=256):
        cur = None
        for gi, col in enumerate(cols):
            c0, wd = col[0], col[1]
            ob = col[2] if len(col) > 2 else 0
            p0 = (c0 // nb) * nb
            assert c0 + wd <= p0 + nb
            if cur is None or cur[0] != p0:
                view, key = self.wload(wname, 0, kc, p0, nb)
                cur = (p0, view, key)
            _, view, key = cur
            ps, pk = self.next_ps()
            for k in range(kc):
                self.add("tensor", lambda e, k=k, ps=ps, view=view, c0=c0, p0=p0, wd=wd, ob=ob: e.matmul(
                    ps[ob:ob + wd, :], lhsT=view[:, k, c0 - p0:c0 - p0 + wd], rhs=act[:, k, :], start=(k == 0),
                    stop=(k == kc - 1)), reads=[key] + act_keys, writes=[pk])
            consume(gi, ps, pk)

    def layer_norm(self, gname, bname, store_out=None):
        add = self.add
        mean_ps, mk = self.next_ps()
        msq_ps, qk = self.next_ps()
        for n in range(16):
            sq = self.lnscr[:, 2 + (n % 2), :]
            sk = "f_acc%d" % (2 + n % 2)
            add("scalar", lambda e, n=n, sq=sq: e.activation(out=sq, in_=self.xres[:, n, :], func=AF.Square),
                reads=[("xres", n)], writes=[sk])
            add("tensor", lambda e, n=n: e.matmul(mean_ps, lhsT=self.onesD, rhs=self.xres[:, n, :], start=(n == 0),
                                                  stop=(n == 15)), reads=["onesD", ("xres", n)], writes=[mk])
            add("tensor", lambda e, n=n, sq=sq: e.matmul(msq_ps, lhsT=self.onesD, rhs=sq, start=(n == 0),
                                                         stop=(n == 15)), reads=["onesD", sk], writes=[qk])
        mean = self.lnscr[:, 0, :]
        rstd = self.lnscr[:, 1, :]
        t2 = self.lnscr[:, 2, :]
        add("scalar", lambda e: e.activation(out=mean, in_=mean_ps, func=AF.Identity), reads=[mk], writes=["f_acc0"])
        add("scalar", lambda e: e.activation(out=t2, in_=mean_ps, func=AF.Square), reads=[mk], writes=["f_acc2"])
        add("vector", lambda e: e.tensor_tensor(out=rstd, in0=msq_ps, in1=t2, op=ALU.subtract), reads=[qk, "f_acc2"],
            writes=["f_acc1"])
        add("vector", lambda e: e.tensor_scalar(out=rstd, in0=rstd, scalar1=LN_EPS, scalar2=None, op0=ALU.add),
            reads=["f_acc1"], writes=["f_acc1"])
        add("scalar", lambda e: e.activation(out=rstd, in_=rstd, func=AF.Sqrt), reads=["f_acc1"], writes=["f_acc1"])
        add("vector", lambda e: e.reciprocal(out=rstd, in_=rstd), reads=["f_acc1"], writes=["f_acc1"])
        for n in range(16):
            xr = self.xres[:, n, :]
            add("vector", lambda e, xr=xr: e.tensor_tensor(out=xr, in0=xr, in1=mean, op=ALU.subtract),
                reads=[("xres", n), "f_acc0"], writes=[("xres", n)])
            add("vector", lambda e, xr=xr: e.tensor_tensor(out=xr, in0=xr, in1=rstd, op=ALU.mult),
                reads=[("xres", n), "f_acc1"], writes=[("xres", n)])
            if store_out is None:
                add("scalar", lambda e, xr=xr, n=n: e.activation(out=self.xbf[:, n, :], in_=xr, func=AF.Identity,
                                                                 scale=self.ppc(gname, n), bias=self.ppc(bname, n)),
                    reads=[("xres", n), "pp"], writes=[("xbf", n)])
            add("scalar", lambda e, xr=xr, n=n: e.activation(out=xr, in_=xr, func=AF.Identity,
                                                             scale=self.ppc(gname, n), bias=self.ppc(bname, n)),
                reads=[("xres", n), "pp"], writes=[("xres", n)])
            if store_out is not None:
                store_out(n)

    def ffn(self, l, hook=None):
        add = self.add
        up, down = "l%d_ffn_up" % l, "l%d_ffn_down" % l
        fcw, fcb = "l%d_fcw" % l, "l%d_fcb" % l
        halo = self.fhalo[l]
        xkeys = [("xbf", n) for n in range(16)]
        tmp = self.ffn_tmp

        def conv(ps, pk, c, slot):
            hb = tmp["hb"][:, slot, :]
            acc = tmp["acc"][:, slot, :]
            hk, ak = "f_hb%d" % slot, "f_acc%d" % slot
            add("gpsimd", lambda e: e.tensor_copy(out=hb[:, 0:2], in_=halo[:, c, :]), reads=[("fhalo", l, c)],
                writes=[hk + "h"])
            add("scalar", lambda e: e.activation(out=hb[:, 2:2 + T], in_=ps, func=AF.Identity), reads=[pk],
                writes=[hk])
            add("scalar", lambda e: e.activation(out=acc, in_=ps, func=AF.Identity, scale=self.ppc(fcw, 2 * 88 + c),
                                                 bias=self.ppc(fcb, c)), reads=[pk, "pp"], writes=[ak])
            add("gpsimd", lambda e: e.tensor_copy(out=halo[:, c, :], in_=hb[:, T:T + 2]), reads=[hk],
                writes=[("fhalo", l, c)])
            add("vector", lambda e: e.scalar_tensor_tensor(out=acc, in0=hb[:, 1:1 + T], scalar=self.ppc(fcw, 88 + c),
                                                           in1=acc, op0=ALU.mult, op1=ALU.add),
                reads=[hk, hk + "h", ak, "pp"], writes=[ak])
            add("vector", lambda e: e.scalar_tensor_tensor(out=acc, in0=hb[:, 0:T], scalar=self.ppc(fcw, c),
                                                           in1=acc, op0=ALU.mult, op1=ALU.add),
                reads=[hk, hk + "h", ak, "pp"], writes=[ak])
            return acc, ak

        for i in range(22):
            gv, gk = self.wload(up, 0, 16, 256 * i, 256)
            vv, vk = self.wload(up, 0, 16, DFF + 256 * i, 256)
            for j in range(2):
                c = 2 * i + j
                psg, pkg = self.next_ps()
                for k in range(16):
                    add("tensor", lambda e, k=k, psg=psg, gv=gv, j=j: e.matmul(
                        psg, lhsT=gv[:, k, 128 * j:128 * j + 128], rhs=self.xbf[:, k, :], start=(k == 0),
                        stop=(k == 15)), reads=[gk] + xkeys, writes=[pkg])
                psv, pkv = self.next_ps()
                for k in range(16):
                    add("tensor", lambda e, k=k, psv=psv, vv=vv, j=j: e.matmul(
                        psv, lhsT=vv[:, k, 128 * j:128 * j + 128], rhs=self.xbf[:, k, :], start=(k == 0),
                        stop=(k == 15)), reads=[vk] + xkeys, writes=[pkv])
                sl = c % 2
                accg, akg = conv(psg, pkg, c, 2 * sl)
                accv, akv = conv(psv, pkv, 44 + c, 2 * sl + 1)
                add("scalar", lambda e, accg=accg: e.activation(out=accg, in_=accg, func=AF.Silu), reads=[akg],
                    writes=[akg])
                add("gpsimd", lambda e, accg=accg, accv=accv, c=c: e.tensor_tensor(out=self.g[:, c, :], in0=accg,
                                                                                   in1=accv, op=ALU.mult),
                    reads=[akg, akv], writes=[("g", c)])
                if hook is not None:
                    hook()
        gkeys = [("g", c) for c in range(44)]
        for nb2 in range(8):
            pieces = []
            for (k0, kn) in ((0, 16), (16, 16), (32, 12)):
                wv, wk = self.wload(down, k0 * 128, kn, 256 * nb2, 256)
                pieces.append((k0, kn, wv, wk))
            for j in range(2):
                n = 2 * nb2 + j
                ps, pk = self.next_ps()
                for (k0, kn, wv, wk) in pieces:
                    for kk in range(kn):
                        k = k0 + kk
                        add("tensor", lambda e, k=k, ps=ps, wv=wv, kk=kk, j=j: e.matmul(
                            ps, lhsT=wv[:, kk, 128 * j:128 * j + 128], rhs=self.g[:, k, :], start=(k == 0),
                            stop=(k == 43)), reads=[wk, ("g", k)], writes=[pk])
                add("vector", lambda e, ps=ps, n=n: e.scalar_tensor_tensor(
                    out=self.xres[:, n, :], in0=self.xres[:, n, :], scalar=ALPHA, in1=ps, op0=ALU.mult, op1=ALU.add),
                    reads=[pk, ("xres", n)], writes=[("xres", n)])

    NS_SM, NS_PN, NS_PT, NS_ST = 5, 3, 3, 8

    def att_carve(self, carve):
        self.att_tmp = {
            "Sm": carve(256 * self.NS_SM).rearrange("p (s n) -> p s n", s=self.NS_SM),
            "Pn": carve(128 * self.NS_PN).bitcast(BF16).rearrange("p (s n) -> p s n", s=self.NS_PN),
            "PT": carve(128 * self.NS_PT).bitcast(BF16).rearrange("p (s n) -> p s n", s=self.NS_PT),
            "stat": carve(8 * self.NS_ST).rearrange("p (s n) -> p s n", s=self.NS_ST),
        }

    def att_a1(self, u):
        add = self.add
        i = u["i"] = self.att_ctr
        self.att_ctr += 1
        qparts = u["q"] if isinstance(u["q"], list) else [(u["q"], 0)]
        ps, pk = self.next_ps()
        u["ps"], u["pk"] = ps, pk
        off = 0
        for (k_ap, _) in u["kvs"]:
            nk = k_ap.shape[-1]
            for (qp, ro) in qparts:
                nr = qp.shape[-1]
                add("tensor", lambda e, k_ap=k_ap, off=off, nk=nk, qp=qp, ro=ro, nr=nr: e.matmul(
                    ps[ro:ro + nr, off:off + nk], lhsT=qp, rhs=k_ap, start=True, stop=True),
                    reads=u["qkeys"] + u["kvkeys"], writes=[pk])
            off += nk
        u["ntot"] = off
        nq = u["nq"]
        u["Sm"] = self.att_tmp["Sm"][0:nq, i % self.NS_SM, 0:off]
        u["smk"] = "a_Sm%d" % (i % self.NS_SM)
        u["st"] = self.att_tmp["stat"][0:nq, i % self.NS_ST, :]
        u["stk"] = "a_st%d" % (i % self.NS_ST)
        u["Pn"] = self.att_tmp["Pn"][0:nq, i % self.NS_PN, 0:off]
        u["pnk"] = "a_Pn%d" % (i % self.NS_PN)

    def att_a2(self, u):
        add = self.add
        nq, ntot, ps, pk, sink, Sm, smk, st, stk = (u[k] for k in ("nq", "ntot", "ps", "pk", "sink", "Sm", "smk",
                                                                     "st", "stk"))
        if u.get("mask_all") is not None:
            add("vector", lambda e: e.tensor_tensor(out=Sm, in0=ps[0:nq, 0:ntot], in1=u["mask_all"], op=ALU.add),
                reads=[pk, "masks"], writes=[smk])
        else:
            off = 0
            for bi, (k_ap, _) in enumerate(u["kvs"]):
                nk = k_ap.shape[-1]
                add("vector", lambda e, off=off, nk=nk, bi=bi: e.tensor_tensor(out=Sm[:, off:off + nk],
                                                                              in0=ps[0:nq, off:off + nk],
                                                                              in1=u["masks"][bi], op=ALU.add),
                    reads=[pk, "masks"], writes=[smk])
                off += nk
        if sink is None:
            add("vector", lambda e: e.reduce_max(out=st[:, 1:2], in_=Sm, axis=AX.X, negate=True), reads=[smk],
                writes=[stk])
        else:
            add("vector", lambda e: e.reduce_max(out=st[:, 0:1], in_=Sm, axis=AX.X), reads=[smk], writes=[stk])
            add("vector", lambda e: e.tensor_scalar(out=st[:, 1:2], in0=st[:, 0:1], scalar1=sink[0:nq, :],
                                                    scalar2=-1.0, op0=ALU.max, op1=ALU.mult), reads=[stk, "pp"],
                writes=[stk])

    def att_a3(self, u):
        add = self.add
        nq, sink, Sm, smk, st, stk = (u[k] for k in ("nq", "sink", "Sm", "smk", "st", "stk"))
        add("scalar", lambda e: e.activation(out=Sm, in_=Sm, func=AF.Exp, bias=st[:, 1:2], accum_out=st[:, 2:3]),
            reads=[smk, stk], writes=[smk, stk])
        if sink is not None:
            add("scalar", lambda e: e.activation(out=st[:, 3:4], in_=sink[0:nq, :], func=AF.Exp, bias=st[:, 1:2]),
                reads=[stk, "pp"], writes=[stk])

    def att_a4(self, u):
        add = self.add
        sink, st, stk = (u[k] for k in ("sink", "st", "stk"))
        if sink is not None:
            add("vector", lambda e: e.tensor_tensor(out=st[:, 4:5], in0=st[:, 2:3], in1=st[:, 3:4], op=ALU.add),
                reads=[stk], writes=[stk])
            add("vector", lambda e: e.reciprocal(out=st[:, 5:6], in_=st[:, 4:5]), reads=[stk], writes=[stk])
        else:
            add("vector", lambda e: e.reciprocal(out=st[:, 5:6], in_=st[:, 2:3]), reads=[stk], writes=[stk])
            add("scalar", lambda e: e.activation(out=st[:, 6:7], in_=st[:, 2:3], func=AF.Ln), reads=[stk],
                writes=[stk])
            add("gpsimd", lambda e: e.tensor_tensor(out=st[:, 6:7], in0=st[:, 6:7], in1=st[:, 1:2], op=ALU.subtract),
                reads=[stk], writes=[stk])

    def att_a5(self, u):
        Sm, smk, st, stk, Pn, pnk = (u[k] for k in ("Sm", "smk", "st", "stk", "Pn", "pnk"))
        self.add("scalar", lambda e: e.activation(out=Pn, in_=Sm, func=AF.Identity, scale=st[:, 5:6]),
                 reads=[smk, stk], writes=[pnk])

    def att_a6(self, u):
        add = self.add
        nq, Pn, pnk = u["nq"], u["Pn"], u["pnk"]
        pst, ptk = self.next_ps()
        pstb = pst.bitcast(BF16)
        u["pstb"], u["ptk"] = pstb, ptk
        off = 0
        for bi, (k_ap, _) in enumerate(u["kvs"]):
            nk = k_ap.shape[-1]
            add("tensor", lambda e, off=off, nk=nk, bi=bi: e.transpose(out=pstb[0:nk, bi * 128:bi * 128 + nq],
                                                                        in_=Pn[:, off:off + nk],
                                                                        identity=self.identb[0:nq, 0:nq]),
                reads=[pnk, "identb"], writes=[ptk])
            off += nk

    def att_a7(self, u):
        add = self.add
        i, nq, pstb, ptk = u["i"], u["nq"], u["pstb"], u["ptk"]
        nks = [k.shape[-1] for k, _ in u["kvs"]]
        nb = len(nks)
        PT = self.att_tmp["PT"][:, i % self.NS_PT, 0:nb * 128]
        ptsk = "a_PT%d" % (i % self.NS_PT)
        if nq == 128 and all(n_ == nks[0] for n_ in nks):
            if i % 2 == 0:
                add("scalar", lambda e: e.activation(out=PT[0:nks[0], :], in_=pstb[0:nks[0], 0:nb * 128],
                                                     func=AF.Identity), reads=[ptk], writes=[ptsk])
            else:
                add("vector", lambda e: e.tensor_copy(out=PT[0:nks[0], :], in_=pstb[0:nks[0], 0:nb * 128]),
                    reads=[ptk], writes=[ptsk])
        else:
            for bi, n_ in enumerate(nks):
                add("vector", lambda e, bi=bi, n_=n_: e.tensor_copy(out=PT[0:n_, bi * 128:bi * 128 + nq],
                                                                    in_=pstb[0:n_, bi * 128:bi * 128 + nq]),
                    reads=[ptk], writes=[ptsk])
        u["res"] = [(PT[0:nk, bi * 128:bi * 128 + nq], v_ap) for bi, (nk, (_, v_ap)) in enumerate(zip(nks, u["kvs"]))]
        u["ptsk"] = ptsk

    def att_a8(self, u):
        u["pv"](u)
        if u.get("post") is not None:
            u["post"]()

    def attn_pipeline(self, units, hook=None):
        stages = [self.att_a1, self.att_a2, self.att_a3, self.att_a4, self.att_a5, self.att_a6, self.att_a7,
                  self.att_a8]
        n = len(units)
        ns = len(stages)
        for u in units:
            u.setdefault("sink", None)
        for i in range(n + ns - 1):
            for k in range(ns):
                j = i - k
                if 0 <= j < n:
                    stages[k](units[j])
            if hook is not None:
                hook(i)

    def layer0_mixer(self, c):
        add = self.add
        par = c % 2
        xkeys = [("xbf", n) for n in range(16)]
        R = self.regC
        tmp = {}
        o = [0]

        def carve(nf32):
            a = R[:, o[0]:o[0] + nf32]
            o[0] += nf32
            return a
        qbuf = carve(2048).bitcast(BF16).rearrange("p (h t) -> p h t", h=8)
        self.att_carve(carve)

        def qslot(h):
            return (h // 8) * 4 + h % 4
        lt = {nm: carve(512).rearrange("p (s n) -> p s n", s=1) for nm in
              ("gel", "u", "si", "sr", "a", "b", "h")}
        xab = carve(2 * 516).rearrange("p (s n) -> p s n", s=2)
        ubf = carve(512).bitcast(BF16).rearrange("p (s n) -> p s n", s=2)
        assert o[0] <= 11264, o[0]
        wv, wvk = self.wload("l0_w_in", 0, 16, 3328, 256)
        for qb in range(4):
            ps, pk = self.next_ps()
            for k in range(16):
                add("tensor", lambda e, k=k, ps=ps, qb=qb: e.matmul(ps[:, 0:256], lhsT=self.xbf[:, k, 128 * qb:128 * qb + 128],
                                                                    rhs=wv[:, k, :], start=(k == 0), stop=(k == 15)),
                    reads=[wvk] + xkeys, writes=[pk])
            add("scalar", lambda e, ps=ps, qb=qb: e.activation(out=self.vtok0[:, 4 * par + qb, :], in_=ps[:, 0:256],
                                                               func=AF.Identity), reads=[pk],
                writes=[("vtok0", 4 * par + qb)])
        def k_consume(gi, ps, pk):
            kb = (gi % 2) * 64
            add("scalar", lambda e: e.activation(out=self.kbuf0[kb:kb + 64, gi // 2, par * T:(par + 1) * T],
                                                 in_=ps[kb:kb + 64, :], func=AF.Identity), reads=[pk],
                writes=[("kbuf0", gi, par)])
        self.proj("l0_w_in", 16, [(3072 + 64 * h, 64, (h % 2) * 64) for h in range(4)], self.xbf, xkeys, k_consume)
        def q_consume(h, ps, pk):
            qb_ = ((h // 4) % 2) * 64
            add("scalar", lambda e: e.activation(out=qbuf[qb_:qb_ + 64, qslot(h), :], in_=ps[qb_:qb_ + 64, :],
                                                 func=AF.Identity, scale=0.125), reads=[pk], writes=[("qbuf", h)])
        self.proj("l0_w_in", 16, [(2048 + 64 * h, 64, ((h // 4) % 2) * 64) for h in range(16)], self.xbf, xkeys, q_consume)
        units = []
        for hp in range(8):
            pso, pok = self.long_ps()
            for h in (2 * hp, 2 * hp + 1):
                kv = h // 4
                kb = (kv % 2) * 64
                kvi = kv // 2
                base = (h % 2) * 64
                for qb in range(4):
                    kvs, kvkeys, masks = [], [], []
                    if not (c == 0 and qb == 0):
                        if qb > 0:
                            kp = self.kbuf0[kb:kb + 64, kvi, par * T + 128 * (qb - 1):par * T + 128 * qb]
                            vp = self.vtok0[:, 4 * par + qb - 1, 64 * kv:64 * kv + 64]
                            kvkeys += [("kbuf0", kv, par), ("vtok0", 4 * par + qb - 1)]
                        else:
                            kp = self.kbuf0[kb:kb + 64, kvi, (1 - par) * T + 384:(1 - par) * T + 512]
                            vp = self.vtok0[:, 4 * (1 - par) + 3, 64 * kv:64 * kv + 64]
                            kvkeys += [("kbuf0", kv, 1 - par), ("vtok0", 4 * (1 - par) + 3)]
                        kvs.append((kp, vp))
                        masks.append(self.masks[:, 0, 0:128])
                    kvs.append((self.kbuf0[kb:kb + 64, kvi, par * T + 128 * qb:par * T + 128 * qb + 128],
                                self.vtok0[:, 4 * par + qb, 64 * kv:64 * kv + 64]))
                    kvkeys += [("kbuf0", kv, par), ("vtok0", 4 * par + qb)]
                    masks.append(self.masks[:, 0, 128:256])

                    def pv(u, pso=pso, pok=pok, base=base, qb=qb):
                        res = u["res"]
                        for bi, (pt, v_ap) in enumerate(res):
                            add("tensor", lambda e, pt=pt, v_ap=v_ap, bi=bi, nb=len(res): e.matmul(
                                pso[base:base + 64, 128 * qb:128 * qb + 128], lhsT=v_ap, rhs=pt, start=(bi == 0),
                                stop=(bi == nb - 1)), reads=[u["ptsk"]] + u["kvkeys"], writes=[pok])
                    u = dict(q=qbuf[kb:kb + 64, qslot(h), 128 * qb:128 * qb + 128], qkeys=[("qbuf", h)], kvs=kvs,
                             kvkeys=kvkeys, masks=masks, nq=128, sink=self.ppc("l0_sinks", h), pv=pv,
                             mask_all=(self.masks[:, 0, :] if len(kvs) == 2 else None))
                    units.append(u)

            def post(pso=pso, pok=pok, hp=hp):
                add("scalar", lambda e: e.activation(out=self.mix[:, 8 + hp, :], in_=pso, func=AF.Identity),
                    reads=[pok], writes=[("mix", 8 + hp)])
            units[-1]["post"] = post
        att_units = units

        def lru_rest(j, psx, pkx, psg, pkg):
            s = 0
            xa = xab[:, j % 2, :]
            xk = "l_xa%d" % (j % 2)
            add("gpsimd", lambda e: e.tensor_copy(out=xa[:, 0:3], in_=self.xa_halo[:, j, :]),
                reads=[("xa_halo", j)], writes=[xk + "h"])
            add("scalar", lambda e: e.activation(out=xa[:, 3:3 + T], in_=psx, func=AF.Identity), reads=[pkx],
                writes=[xk])
            gel = lt["gel"][:, s, :]
            add("scalar", lambda e: e.activation(out=gel, in_=psg, func=AF.Gelu), reads=[pkg],
                writes=["l_gel%d" % s])
            u = lt["u"][:, s, :]
            uk = "l_u%d" % s
            add("scalar", lambda e: e.activation(out=u, in_=psx, func=AF.Identity, scale=self.ppc("l0_conv_w", 3 * 8 + j),
                                                 bias=self.ppc("l0_conv_b", j)), reads=[pkx, "pp"], writes=[uk])
            add("gpsimd", lambda e: e.tensor_copy(out=self.xa_halo[:, j, :], in_=xa[:, T:T + 3]), reads=[xk],
                writes=[("xa_halo", j)])
            for tap in range(3):
                add("vector", lambda e, tap=tap: e.scalar_tensor_tensor(
                    out=u, in0=xa[:, tap:tap + T], scalar=self.ppc("l0_conv_w", tap * 8 + j), in1=u, op0=ALU.mult,
                    op1=ALU.add), reads=[xk, xk + "h", uk, "pp"], writes=[uk])
            ub = ubf[:, s, :]
            ubk = "l_ub%d" % s
            add("scalar", lambda e: e.activation(out=ub, in_=u, func=AF.Identity), reads=[uk], writes=[ubk])
            ps1, pk1 = self.next_ps()
            add("tensor", lambda e: e.matmul(ps1, lhsT=self.gxw[:, j, :], rhs=ub, start=True, stop=True),
                reads=["gxw", ubk], writes=[pk1])
            ps2, pk2 = self.next_ps()
            add("tensor", lambda e: e.matmul(ps2, lhsT=self.gaw[:, j, :], rhs=ub, start=True, stop=True),
                reads=["gaw", ubk], writes=[pk2])
            si = lt["si"][:, s, :]
            sr = lt["sr"][:, s, :]
            a = lt["a"][:, s, :]
            b = lt["b"][:, s, :]
            hh = lt["h"][:, s, :]
            add("scalar", lambda e: e.activation(out=si, in_=ps1, func=AF.Sigmoid, bias=self.ppc("l0_gx_b", j)),
                reads=[pk1, "pp"], writes=["l_si%d" % s])
            add("scalar", lambda e: e.activation(out=sr, in_=ps2, func=AF.Sigmoid, bias=self.ppc("l0_ga_b", j)),
                reads=[pk2, "pp"], writes=["l_sr%d" % s])
            add("scalar", lambda e: e.activation(out=a, in_=sr, func=AF.Exp, scale=self.cpcol[:, j:j + 1]),
                reads=["l_sr%d" % s, "cpcol"], writes=["l_a%d" % s])
            add("scalar", lambda e: e.activation(out=sr, in_=a, func=AF.Square), reads=["l_a%d" % s],
                writes=["l_sr%d" % s])
            add("scalar", lambda e: e.activation(out=sr, in_=sr, func=AF.Sqrt, scale=-1.0, bias=1.0),
                reads=["l_sr%d" % s], writes=["l_sr%d" % s])
            add("gpsimd", lambda e: e.tensor_tensor(out=b, in0=si, in1=u, op=ALU.mult), reads=["l_si%d" % s, uk],
                writes=["l_b%d" % s])
            add("vector", lambda e: e.tensor_tensor(out=b, in0=b, in1=sr, op=ALU.mult),
                reads=["l_b%d" % s, "l_sr%d" % s], writes=["l_b%d" % s])
            add("vector", lambda e: e.tensor_tensor_scan(out=hh, data0=a, data1=b, initial=self.lru_h[:, j:j + 1],
                                                         op0=ALU.mult, op1=ALU.add),
                reads=["l_a%d" % s, "l_b%d" % s, ("lru_h", j)], writes=["l_h%d" % s])
            add("gpsimd", lambda e: e.tensor_copy(out=self.lru_h[:, j:j + 1], in_=hh[:, T - 1:T]), reads=["l_h%d" % s],
                writes=[("lru_h", j)])
            add("gpsimd", lambda e: e.tensor_tensor(out=self.mix[:, j, :], in0=hh, in1=gel, op=ALU.mult),
                reads=["l_h%d" % s, "l_gel%d" % s], writes=[("mix", j)])

        self.attn_pipeline(att_units)
        pend = []
        for j in range(8):
            p0 = (j // 2) * 256
            if j % 2 == 0:
                wx, wxk = self.wload("l0_w_in", 0, 16, p0, 256)
                wg, wgk = self.wload("l0_w_in", 0, 16, 1024 + p0, 256)
            psx, pkx = self.next_ps()
            for k in range(16):
                add("tensor", lambda e, k=k, psx=psx, wx=wx, j=j: e.matmul(
                    psx, lhsT=wx[:, k, 128 * (j % 2):128 * (j % 2) + 128], rhs=self.xbf[:, k, :], start=(k == 0),
                    stop=(k == 15)), reads=[wxk] + xkeys, writes=[pkx])
            psg, pkg = self.next_ps()
            for k in range(16):
                add("tensor", lambda e, k=k, psg=psg, wg=wg, j=j: e.matmul(
                    psg, lhsT=wg[:, k, 128 * (j % 2):128 * (j % 2) + 128], rhs=self.xbf[:, k, :], start=(k == 0),
                    stop=(k == 15)), reads=[wgk] + xkeys, writes=[pkg])
            pend.append((j, psx, pkx, psg, pkg))
            if len(pend) == 2:
                lru_rest(*pend.pop(0))
        while pend:
            lru_rest(*pend.pop(0))

    def reduce_angle(self, eng_ops, x, ki, kf, n, keys):
        add = self.add
        add("vector", lambda e: e.tensor_scalar(out=ki, in0=x, scalar1=1.0 / TWO_PI, scalar2=None, op0=ALU.mult),
            reads=keys, writes=[keys[0] + "_ki"])
        add("vector", lambda e: e.tensor_copy(out=kf, in_=ki), reads=[keys[0] + "_ki"], writes=[keys[0] + "_kf"])
        add("vector", lambda e: e.scalar_tensor_tensor(out=x, in0=kf, scalar=-TWO_PI, in1=x, op0=ALU.mult,
                                                       op1=ALU.add), reads=[keys[0] + "_kf"] + keys, writes=keys)
        add("vector", lambda e: e.tensor_scalar(out=x, in0=x, scalar1=-PI_SAFE, scalar2=PI_SAFE, op0=ALU.max,
                                                op1=ALU.min), reads=keys, writes=keys)

    def cos_from_reduced(self, red, tmp, out, keys_red, key_tmp, key_out):
        add = self.add
        add("vector", lambda e: e.tensor_scalar(out=tmp, in0=red, scalar1=1.5707962, scalar2=-6.2831833,
                                                op0=ALU.is_gt, op1=ALU.mult), reads=keys_red, writes=[key_tmp])
        add("vector", lambda e: e.scalar_tensor_tensor(out=tmp, in0=red, scalar=1.5707955, in1=tmp, op0=ALU.add,
                                                       op1=ALU.add), reads=keys_red + [key_tmp], writes=[key_tmp])
        add("scalar", lambda e: e.activation(out=out, in_=tmp, func=AF.Sin), reads=[key_tmp],
            writes=[key_out])

    def s5_prologue(self):
        add = self.add
        R = self.regC
        o = [0]

        def carve(n):
            a = R[:, o[0]:o[0] + n]
            o[0] += n
            return a
        names = ["s5_Bre", "s5_Bim", "s5_kare", "s5_kaim", "s5_kldt"]
        t = {}
        for nm in names:
            t[nm] = carve(384)
            add("gpsimd", lambda e, nm=nm: e.dma_start(out=t[nm].rearrange("p (a b) -> p a b", a=6), in_=self.din[nm]),
                writes=[nm], ndma=1, lane="once")
        C1 = carve(768)
        C2 = carve(768)
        cmk = carve(1024)
        add("gpsimd", lambda e: e.dma_start(out=C1.rearrange("p (a b) -> p a b", a=6), in_=self.din["s5_C1"]),
            writes=["C1"], ndma=1, lane="once")
        add("gpsimd", lambda e: e.dma_start(out=C2.rearrange("p (a b) -> p a b", a=6), in_=self.din["s5_C2"]),
            writes=["C2"], ndma=1, lane="once")
        add("gpsimd", lambda e: e.dma_start(out=cmk.rearrange("p (a b) -> p a b", a=8), in_=self.din["colmask"]),
            writes=["cmk"], ndma=1, lane="once")
        w = {nm: carve(384) for nm in ("dt", "r", "th", "sin", "cos", "t1", "t2", "kre", "kim", "kf")}
        ki = carve(384).bitcast(I32)
        kare, kaim = t["s5_kare"], t["s5_kaim"]

        def op2(eng, out, a, b, op, rk, wk):
            add(eng, lambda e: e.tensor_tensor(out=out, in0=a, in1=b, op=op), reads=rk, writes=wk)
        add("scalar", lambda e: e.activation(out=w["dt"], in_=t["s5_kldt"], func=AF.Exp), reads=["s5_kldt"],
            writes=["k_dt"])
        op2("vector", w["r"], kare, w["dt"], ALU.mult, ["s5_kare", "k_dt"], ["k_r"])
        add("scalar", lambda e: e.activation(out=w["r"], in_=w["r"], func=AF.Exp), reads=["k_r"], writes=["k_r"])
        op2("vector", w["th"], kaim, w["dt"], ALU.mult, ["s5_kaim", "k_dt"], ["k_th"])
        self.reduce_angle(None, w["th"], ki, w["kf"], 384, ["k_th"])
        add("scalar", lambda e: e.activation(out=w["sin"], in_=w["th"], func=AF.Sin), reads=["k_th"],
            writes=["k_sin"])
        self.cos_from_reduced(w["th"], w["t1"], w["cos"], ["k_th"], "k_t1", "k_cos")
        op2("vector", w["cos"], w["cos"], w["r"], ALU.mult, ["k_cos", "k_r"], ["k_cos"])
        add("vector", lambda e: e.tensor_scalar(out=w["cos"], in0=w["cos"], scalar1=-1.0, scalar2=None, op0=ALU.add),
            reads=["k_cos"], writes=["k_cos"])
        op2("vector", w["sin"], w["sin"], w["r"], ALU.mult, ["k_sin", "k_r"], ["k_sin"])
        op2("vector", w["t1"], kare, kare, ALU.mult, ["s5_kare", "k_t1"], ["k_t1"])
        op2("vector", w["t2"], kaim, kaim, ALU.mult, ["s5_kaim"], ["k_t2"])
        op2("vector", w["t1"], w["t1"], w["t2"], ALU.add, ["k_t1", "k_t2"], ["k_t1"])
        add("vector", lambda e: e.reciprocal(out=w["dt"], in_=w["t1"]), reads=["k_t1", "k_dt"], writes=["k_dt"])
        op2("vector", w["t1"], w["cos"], kare, ALU.mult, ["k_cos", "s5_kare", "k_t1"], ["k_t1"])
        op2("vector", w["t2"], w["sin"], kaim, ALU.mult, ["k_sin", "s5_kaim", "k_t2"], ["k_t2"])
        op2("vector", w["kre"], w["t1"], w["t2"], ALU.add, ["k_t1", "k_t2"], ["k_kre"])
        op2("vector", w["kre"], w["kre"], w["dt"], ALU.mult, ["k_kre", "k_dt"], ["k_kre"])
        op2("vector", w["t1"], w["sin"], kare, ALU.mult, ["k_sin", "s5_kare", "k_t1"], ["k_t1"])
        op2("vector", w["t2"], w["cos"], kaim, ALU.mult, ["k_cos", "s5_kaim", "k_t2"], ["k_t2"])
        op2("vector", w["kim"], w["t1"], w["t2"], ALU.subtract, ["k_t1", "k_t2"], ["k_kim"])
        op2("vector", w["kim"], w["kim"], w["dt"], ALU.mult, ["k_kim", "k_dt"], ["k_kim"])
        Bre, Bim = t["s5_Bre"], t["s5_Bim"]
        op2("vector", w["t1"], w["kre"], Bre, ALU.mult, ["k_kre", "s5_Bre", "k_t1"], ["k_t1"])
        op2("vector", w["t2"], w["kim"], Bim, ALU.mult, ["k_kim", "s5_Bim", "k_t2"], ["k_t2"])
        op2("vector", w["r"], w["t1"], w["t2"], ALU.subtract, ["k_t1", "k_t2", "k_r"], ["k_bre"])
        op2("vector", w["t1"], w["kre"], Bim, ALU.mult, ["k_kre", "s5_Bim", "k_t1"], ["k_t1"])
        op2("vector", w["t2"], w["kim"], Bre, ALU.mult, ["k_kim", "s5_Bre", "k_t2"], ["k_t2"])
        op2("vector", w["th"], w["t1"], w["t2"], ALU.add, ["k_t1", "k_t2", "k_th"], ["k_bim"])
        bre = w["r"].rearrange("p (a b) -> p a b", a=6)
        bim = w["th"].rearrange("p (a b) -> p a b", a=6)
        add("vector", lambda e: e.tensor_scalar(out=C1, in0=C1, scalar1=self.ppc("sgn1"), scalar2=None, op0=ALU.mult),
            reads=["C1", "pp"], writes=["C1"])
        add("vector", lambda e: e.tensor_scalar(out=C2, in0=C2, scalar1=-1.0, scalar2=None, op0=ALU.mult),
            reads=["C2"], writes=["C2"])
        C1v = C1.rearrange("p (a b) -> p a b", a=6)
        C2v = C2.rearrange("p (a b) -> p a b", a=6)
        cmv = cmk.rearrange("p (a b) -> p a b", a=8)
        rmo = PP_OFF["rowmask"][0]
        rmask = self.pp[:, rmo:rmo + 8]
        xrf = self.xres.rearrange("p a b -> p (a b)")
        SM = [xrf[:, 2048 * i:2048 * i + 2048].bitcast(BF16).rearrange("p (j n) -> p j n", j=8) for i in range(2)]
        for ch in range(6):
            sm = SM[ch % 2]
            smk = "SM%d" % (ch % 2)
            rm_b = rmask.unsqueeze(2).to_broadcast([128, 8, 64])
            for (c0, src, key, neg) in ((0, bre, "k_bre", False), (64, bim, "k_bim", False), (128, bim, "k_bim", False),
                                        (192, bre, "k_bre", True)):
                srcb = src[:, ch, :].unsqueeze(1).to_broadcast([128, 8, 64])
                add("vector", lambda e, sm=sm, c0=c0, srcb=srcb: e.tensor_tensor(out=sm[:, :, c0:c0 + 64], in0=srcb,
                                                                                in1=rm_b, op=ALU.mult),
                    reads=[key, "pp"], writes=[smk + "_%d" % c0])
                if neg:
                    add("vector", lambda e, sm=sm, c0=c0: e.tensor_scalar(out=sm[:, :, c0:c0 + 64],
                                                                          in0=sm[:, :, c0:c0 + 64], scalar1=-1.0,
                                                                          scalar2=None, op0=ALU.mult),
                        reads=[smk + "_%d" % c0], writes=[smk + "_%d" % c0])
            for (c0, Cv, key) in ((256, C1v, "C1"), (384, C2v, "C2")):
                cb = Cv[:, ch, :].unsqueeze(1).to_broadcast([128, 8, 128])
                add("vector", lambda e, sm=sm, c0=c0, cb=cb: e.tensor_tensor(out=sm[:, :, c0:c0 + 128], in0=cb, in1=cmv,
                                                                            op=ALU.mult),
                    reads=[key, "cmk"], writes=[smk + "_%d" % c0])
            dst = self.s5mat[8 * ch:8 * ch + 8].rearrange("g p n -> p g n")
            add("sync", lambda e, sm=sm, dst=dst: e.dma_start(out=dst, in_=sm),
                reads=[smk + "_%d" % c0 for c0 in (0, 64, 128, 192, 256, 384)] + ["regC_tok"], writes=["s5mat"],
                ndma=1, lane=smk)
        self.barrier()
        o[0] = 0
        self.s5_r = self.s5_rp
        are, aim, ldt = (self.pp[:, PP_OFF[n][0]:PP_OFF[n][0] + 48] for n in ("s5_are", "s5_aim", "s5_ldt"))
        dt = carve(48)
        th = self.s5_th
        ph1 = carve(48)
        tmp48 = carve(48)
        kf48 = carve(48)
        ki48 = carve(48).bitcast(I32)
        add("scalar", lambda e: e.activation(out=dt, in_=ldt, func=AF.Exp), reads=["pp"], writes=["a_dt"])
        op2("vector", self.s5_r, are, dt, ALU.mult, ["pp", "a_dt"], ["s5_r"])
        add("scalar", lambda e: e.activation(out=self.s5_r, in_=self.s5_r, func=AF.Exp), reads=["s5_r"],
            writes=["s5_r"])
        op2("vector", th, aim, dt, ALU.mult, ["pp", "a_dt"], ["a_th"])
        self.reduce_angle(None, th, ki48, kf48, 48, ["a_th"])
        add("vector", lambda e: e.tensor_scalar(out=ph1, in0=th, scalar1=32.0, scalar2=None, op0=ALU.mult),
            reads=["a_th"], writes=["a_ph1"])
        self.reduce_angle(None, ph1, ki48, kf48, 48, ["a_ph1"])
        for c in range(4):
            add("vector", lambda e, c=c: e.tensor_scalar(out=tmp48, in0=ph1, scalar1=16.0 * c, scalar2=None,
                                                         op0=ALU.mult), reads=["a_ph1"], writes=["a_tmp48"])
            self.reduce_angle(None, tmp48, ki48, kf48, 48, ["a_tmp48"])
            add("vector", lambda e, c=c: e.tensor_copy(out=self.s5_phi0[:, c, :], in_=tmp48), reads=["a_tmp48"],
                writes=["s5_phi0"])
        for nm, _, _ in BIGW:
            if not nm.startswith("l1"):
                self.convert(nm)
        assert o[0] <= 11264, o[0]

    def table_gen(self, c, t0):
        add = self.add
        mixf = self.mix.rearrange("p a b -> p (a b)").bitcast(F32)
        angb = [mixf[:, 512 * i:512 * i + 512] for i in range(2)]
        kib = mixf[:, 1024:1536].bitcast(I32)
        kfb = mixf[:, 1536:2048]
        tabs = [mixf[:, 2048 + 1024 * i:3072 + 1024 * i].rearrange("p (a n) -> p a n", a=2) for i in range(2)]

        def stage1(g):
            ang, ak = angb[g % 2], "tg_ang%d" % (g % 2)
            tb, tk = tabs[g % 2], "tg_tab%d" % (g % 2)
            add("vector", lambda e: e.tensor_scalar(out=ang, in0=self.jrow, scalar1=self.s5_th[:, g:g + 1],
                                                    scalar2=self.s5_phi0[:, c, g:g + 1], op0=ALU.mult, op1=ALU.add),
                reads=["jrow", "a_th", "s5_phi0"], writes=[ak])
            add("vector", lambda e: e.tensor_scalar(out=kib, in0=ang, scalar1=1.0 / TWO_PI, scalar2=None,
                                                    op0=ALU.mult), reads=[ak], writes=["tg_ki"])
            add("vector", lambda e: e.tensor_copy(out=kfb, in_=kib), reads=["tg_ki"], writes=["tg_kf"])
            add("vector", lambda e: e.scalar_tensor_tensor(out=ang, in0=kfb, scalar=-TWO_PI, in1=ang, op0=ALU.mult,
                                                           op1=ALU.add), reads=["tg_kf", ak], writes=[ak])
            add("vector", lambda e: e.tensor_scalar(out=ang, in0=ang, scalar1=-PI_SAFE, scalar2=PI_SAFE, op0=ALU.max,
                                                    op1=ALU.min), reads=[ak], writes=[ak])
            add("vector", lambda e: e.tensor_scalar(out=tb[:, 0, :], in0=ang, scalar1=1.5707962, scalar2=-6.2831833,
                                                    op0=ALU.is_gt, op1=ALU.mult), reads=[ak], writes=[tk + "c"])
            add("vector", lambda e: e.scalar_tensor_tensor(out=tb[:, 0, :], in0=ang, scalar=1.5707955,
                                                           in1=tb[:, 0, :], op0=ALU.add, op1=ALU.add),
                reads=[ak, tk + "c"], writes=[tk + "c"])

        def stage2(g):
            ang, ak = angb[g % 2], "tg_ang%d" % (g % 2)
            tb, tk = tabs[g % 2], "tg_tab%d" % (g % 2)
            add("scalar", lambda e: e.activation(out=tb[:, 1, :], in_=ang, func=AF.Sin), reads=[ak], writes=[tk + "s"])
            add("scalar", lambda e: e.activation(out=tb[:, 0, :], in_=tb[:, 0, :], func=AF.Sin), reads=[tk + "c"],
                writes=[tk + "c"])
            dst = self.s5tab[g, :, :, t0:t0 + T]
            add("gpsimd", lambda e: e.dma_start(out=dst, in_=tb), reads=[tk + "s", tk + "c"],
                writes=["s5tab"], ndma=1, lane=tk)

        for g in range(49):
            if g < 48:
                stage1(g)
            if g >= 1:
                stage2(g - 1)
            yield

    def layer1_s5(self, c, t0):
        add = self.add
        xkeys = [("xbf", n) for n in range(16)]
        R = self.regC
        o = [0]

        def carve(n):
            a = R[:, o[0]:o[0] + n]
            o[0] += n
            return a
        uf = carve(512).rearrange("p (s n) -> p s n", s=1)
        ub = carve(512).bitcast(BF16).rearrange("p (s n) -> p s n", s=2)
        zbf = carve(1536).bitcast(BF16).rearrange("p (k n) -> p k n", k=6)
        NSM = 6
        NTB = 5
        mixf = self.mix.rearrange("p a b -> p (a b)").bitcast(F32)
        smb = [carve(256).bitcast(BF16) for _ in range(4)]
        smb += [mixf[:, 3584 + 256 * i:3584 + 256 * (i + 1)].bitcast(BF16) for i in range(2)]
        tbb = [carve(1024).rearrange("p (a n) -> p a n", a=2) for _ in range(3)]
        tbb += [mixf[:, 1536 + 1024 * i:1536 + 1024 * (i + 1)].rearrange("p (a n) -> p a n", a=2) for i in range(2)]
        cb = [carve(512) for _ in range(2)]
        c2b = [carve(512) for _ in range(2)]
        zb = [carve(512) for _ in range(2)]
        pzb = [carve(256).bitcast(BF16) for _ in range(2)]
        qzb = [carve(256).bitcast(BF16) for _ in range(2)]
        ytmp = carve(512)
        self.s5_o = o[0]
        assert o[0] <= 11264, o[0]
        gi = 0

        def stage_a(ch, j, s):
            nonlocal gi
            g = 8 * ch + j
            sm, smk = smb[gi % NSM], "s_sm%d" % (gi % NSM)
            tb, tk = tbb[gi % NTB], "s_tb%d" % (gi % NTB)
            cc, ck = cb[gi % 2], "s_c%d" % (gi % 2)
            c2, c2k = c2b[gi % 2], "s_c2%d" % (gi % 2)
            zz, zk = zb[gi % 2], "s_z%d" % (gi % 2)
            st = dict(g=g, j=j, sm=sm, smk=smk, tb=tb, tk=tk, zz=zz, zk=zk, b2=gi % 2, cc=cc, ck=ck, c2=c2, c2k=c2k)
            gi += 1
            add("sync", lambda e: e.dma_start(out=sm, in_=self.s5mat[g]), reads=["s5mat", "regC_tok"], writes=[smk],
                ndma=1, lane=smk)
            add("sync", lambda e: e.dma_start(out=tb, in_=self.s5tab[g, :, :, t0:t0 + T]),
                reads=["s5tab", "regC_tok"], writes=[tk], ndma=1, lane=tk)
            psa, pka = self.next_ps()
            add("tensor", lambda e: e.matmul(psa, lhsT=sm[:, 0:128], rhs=ub[:, s, :], start=True, stop=True),
                reads=[smk, "s_ub%d" % s], writes=[pka])
            psb, pkb = self.next_ps()
            add("tensor", lambda e: e.matmul(psb, lhsT=sm[:, 128:256], rhs=ub[:, s, :], start=True, stop=True),
                reads=[smk, "s_ub%d" % s], writes=[pkb])
            add("vector", lambda e: e.tensor_tensor(out=cc, in0=psa, in1=tb[:, 0, :], op=ALU.mult), reads=[pka, tk],
                writes=[ck])
            add("vector", lambda e: e.tensor_tensor(out=c2, in0=psb, in1=tb[:, 1, :], op=ALU.mult), reads=[pkb, tk],
                writes=[c2k])
            return st

        def stage_a2(st):
            g, cc, ck, c2, c2k, zz, zk = (st[k] for k in ("g", "cc", "ck", "c2", "c2k", "zz", "zk"))
            add("vector", lambda e: e.tensor_tensor(out=cc, in0=cc, in1=c2, op=ALU.add), reads=[ck, c2k], writes=[ck])
            add("vector", lambda e: e.tensor_tensor_scan(
                out=zz, data0=self.s5_r[:, g:g + 1].to_broadcast([128, T]), data1=cc,
                initial=self.s5_state[:, g:g + 1], op0=ALU.mult, op1=ALU.add),
                reads=[ck, "s5_r", ("s5_state", g)], writes=[zk])
            add("scalar", lambda e: e.activation(out=self.s5_state[:, g:g + 1], in_=zz[:, T - 1:T], func=AF.Identity),
                reads=[zk], writes=[("s5_state", g)])

        def stage_b(st, yps, yk):
            j, sm, smk, tb, tk, zz, zk, b2 = (st[k] for k in ("j", "sm", "smk", "tb", "tk", "zz", "zk", "b2"))
            pz, qz = pzb[b2], qzb[b2]
            pzk, qzk = "s_pz%d" % b2, "s_qz%d" % b2
            add("gpsimd", lambda e: e.tensor_tensor(out=pz, in0=zz, in1=tb[:, 0, :], op=ALU.mult), reads=[zk, tk],
                writes=[pzk])
            add("gpsimd", lambda e: e.tensor_tensor(out=qz, in0=zz, in1=tb[:, 1, :], op=ALU.mult), reads=[zk, tk],
                writes=[qzk])
            add("tensor", lambda e: e.matmul(yps, lhsT=sm[:, 256:384], rhs=pz, start=(j == 0), stop=False),
                reads=[smk, pzk], writes=[yk])
            add("tensor", lambda e: e.matmul(yps, lhsT=sm[:, 384:512], rhs=qz, start=False, stop=(j == 7)),
                reads=[smk, qzk], writes=[yk])

        def u_consume(ch, ps, pk):
            s = ch % 2
            add("scalar", lambda e: e.activation(out=uf[:, 0, :], in_=ps, func=AF.Identity), reads=[pk],
                writes=["s_uf"])
            add("vector", lambda e: e.tensor_copy(out=ub[:, s, :], in_=uf[:, 0, :]), reads=["s_uf"],
                writes=["s_ub%d" % s])
            yps, yk = self.long_ps()
            sts = []
            for j in range(8 + 2):
                if j < 8:
                    sts.append(stage_a(ch, j, s))
                if 0 <= j - 1 < 8:
                    stage_a2(sts[j - 1])
                if 0 <= j - 2 < 8:
                    stage_b(sts[j - 2], yps, yk)
            add("vector", lambda e: e.scalar_tensor_tensor(out=ytmp, in0=uf[:, 0, :], scalar=self.ppc("l1_D", ch),
                                                           in1=yps, op0=ALU.mult, op1=ALU.add),
                reads=[yk, "s_uf", "pp"], writes=["s_y"])
            add("scalar", lambda e: e.activation(out=zbf[:, ch, :], in_=ytmp, func=AF.Gelu), reads=["s_y"],
                writes=[("s_zbf", ch)])

        self.proj("l1_w_in", 16, [(128 * ch, 128) for ch in range(6)], self.xbf, xkeys, u_consume)
        if self.stop in ("s5a", "s5b"):
            return
        zkeys = [("s_zbf", k) for k in range(6)]

        def glu_consume(n, ps, pk):
            add("scalar", lambda e: e.activation(out=ytmp, in_=ps, func=AF.Sigmoid, bias=self.ppc("l1_glu_b", n)),
                reads=[pk, "pp"], writes=["s_y"])
            add("vector", lambda e: e.tensor_tensor(out=self.mix[:, n, :], in0=zbf[:, n, :], in1=ytmp, op=ALU.mult),
                reads=["s_y", ("s_zbf", n)], writes=[("mix", n)])
        self.proj("l1_glu_w", 6, [(128 * n, 128) for n in range(6)], zbf, zkeys, glu_consume)

    def layer1_attn(self, c, t0):
        add = self.add
        par = c % 2
        xkeys = [("xbf", n) for n in range(16)]
        R = self.regC
        o = [0]

        def carve(n):
            a = R[:, o[0]:o[0] + n]
            o[0] += n
            return a
        q1 = carve(3072).bitcast(BF16).rearrange("p (h t) -> p h t", h=12)
        self.att_carve(carve)
        v16l = carve(2048).bitcast(BF16).rearrange("p (r n) -> p r n", r=16)
        ob = [carve(512) for _ in range(3)]
        lb = [carve(512) for _ in range(3)]
        tA = carve(512)
        lbcs = [carve(64) for _ in range(2)]
        assert o[0] <= 11264, o[0]
        wv, wvk = self.wload("l1_w_in", 0, 16, 2560, 256)
        for qb in range(4):
            ps, pk = self.next_ps()
            for k in range(16):
                add("tensor", lambda e, k=k, ps=ps, qb=qb: e.matmul(ps[:, 0:256], lhsT=self.xbf[:, k, 128 * qb:128 * qb + 128],
                                                                    rhs=wv[:, k, :], start=(k == 0), stop=(k == 15)),
                    reads=[wvk] + xkeys, writes=[pk])
            add("scalar", lambda e, ps=ps, qb=qb: e.activation(out=self.V1[:, 4 * par + qb, :], in_=ps[:, 0:256],
                                                               func=AF.Identity), reads=[pk], writes=[("V1", 4 * par + qb)])
        for rho in range(4):
            ps, pk = self.next_ps()
            for k in range(16):
                add("tensor", lambda e, k=k, ps=ps, rho=rho: e.matmul(ps[:, 0:256], lhsT=self.xbf[:, k, rho:T:4],
                                                                      rhs=wv[:, k, :], start=(k == 0), stop=(k == 15)),
                    reads=[wvk] + xkeys, writes=[pk])
            add("vector", lambda e, ps=ps, rho=rho: e.tensor_copy(out=self.V4[:, 4 * par + rho, :], in_=ps[:, 0:256]),
                reads=[pk], writes=[("V4", 4 * par + rho)])
        vb = 32 * c if c < 3 else 0
        vdst = self.V16 if c < 3 else v16l
        for r2 in range(8):
            ps, pk = self.next_ps()
            for h2 in range(2):
                rho = 2 * r2 + h2
                for k in range(16):
                    add("tensor", lambda e, k=k, ps=ps, rho=rho, h2=h2: e.matmul(
                        ps[vb:vb + 32, 256 * h2:256 * h2 + 256], lhsT=self.xbf[:, k, rho:T:16], rhs=wv[:, k, :],
                        start=(k == 0), stop=(k == 15)), reads=[wvk] + xkeys, writes=[pk])
            add("scalar", lambda e, ps=ps, r2=r2: e.activation(
                out=vdst[vb:vb + 32, 2 * r2:2 * r2 + 2, :].rearrange("p a b -> p (a b)"), in_=ps[vb:vb + 32, :],
                func=AF.Identity), reads=[pk], writes=[("V16", c, r2)])
        v16keys = [("V16", cc, r2) for cc in range(c + 1) for r2 in range(8)]
        def k_consume(gi, ps, pk):
            kb = (gi % 2) * 64
            add("scalar", lambda e: e.activation(out=self.Kc[kb:kb + 64, gi // 2, t0:t0 + T], in_=ps[kb:kb + 64, :],
                                                 func=AF.Identity), reads=[pk], writes=[("Kc", gi, c)])
        self.proj("l1_w_in", 16, [(2304 + 64 * h, 64, (h % 2) * 64) for h in range(4)], self.xbf, xkeys, k_consume)

        lctr = [0]

        def lse_rows(st, stk, nq, dst_ps, dst_key, targets):
            li = lctr[0] % 2
            lctr[0] += 1
            lbc = lbcs[li]
            lk = "a_lbc%d" % li
            add("gpsimd", lambda e: e.tensor_copy(out=lbc[0:nq, 0:64], in_=st[:, 6:7].to_broadcast([nq, 64])),
                reads=[stk], writes=[lk])
            for (pb, c0, ncol, r0) in targets:
                add("tensor", lambda e, pb=pb, c0=c0, ncol=ncol, r0=r0: e.matmul(
                    dst_ps[pb:pb + 64, c0:c0 + ncol], lhsT=lbc[0:nq, 0:64], rhs=self.identf[0:nq, r0:r0 + ncol],
                    start=True, stop=True), reads=[lk, "identf"], writes=[dst_key])

        for kvp in range(2):
            cols = []
            for r in range(3):
                for kvl in range(2):
                    for g in range(2):
                        cols.append((768 + 256 * (2 * r + kvp) + 128 * kvl + 64 * g, 64, kvl * 64))

            def q_consume(gi, ps, pk):
                kvl = (gi // 2) % 2
                qb_ = kvl * 64
                add("scalar", lambda e: e.activation(out=q1[qb_:qb_ + 64, gi, :], in_=ps[qb_:qb_ + 64, :],
                                                     func=AF.Identity, scale=0.125), reads=[pk], writes=[("q1", gi)])
            self.proj("l1_w_in", 16, cols, self.xbf, xkeys, q_consume)
            for kvl in range(2):
                kv = 2 * kvp + kvl
                kb = kvl * 64
                kvi = kv // 2
                kkeys = [("Kc", kv, cc) for cc in range(c + 1)]
                units = []

                def qslot(r, g, kvl=kvl):
                    return (r * 2 + kvl) * 2 + g

                def evac_post(po, pok, pl, plk, i):
                    def post():
                        add("scalar", lambda e: e.activation(out=ob[i], in_=po, func=AF.Identity), reads=[pok],
                            writes=["a_o%d" % i])
                        add("vector", lambda e: e.tensor_copy(out=lb[i], in_=pl), reads=[plk], writes=["a_l%d" % i])
                    return post
                po, pok = self.long_ps()
                pl, plk = self.long_ps()
                N = 32 * (c + 1)
                for rho in range(16):
                    qparts = [(q1[kb:kb + 64, qslot(2, g), rho:T:16], 32 * g) for g in range(2)]
                    qk = [("q1", qslot(2, 0)), ("q1", qslot(2, 1))]
                    mrow = self.masks[0:64, 2 + c // 2, (c % 2) * 128:(c % 2) * 128 + 128]
                    if c < 3:
                        kvs = [(self.Kc[kb:kb + 64, kvi, rho:T * (c + 1):16], self.V16[0:N, rho, 64 * kv:64 * kv + 64])]
                        masks = [mrow[:, 0:N]]
                    else:
                        kvs = [(self.Kc[kb:kb + 64, kvi, rho:1536:16], self.V16[0:96, rho, 64 * kv:64 * kv + 64]),
                               (self.Kc[kb:kb + 64, kvi, 1536 + rho:2048:16], v16l[0:32, rho, 64 * kv:64 * kv + 64])]
                        masks = [mrow[:, 0:96], mrow[:, 96:128]]

                    def pv(u, po=po, pok=pok, pl=pl, plk=plk, rho=rho):
                        res = u["res"]
                        for g in range(2):
                            for bi, (pt, v_ap) in enumerate(res):
                                add("tensor", lambda e, pt=pt, v_ap=v_ap, bi=bi, nb=len(res), g=g: e.matmul(
                                    po[64 * g:64 * g + 64, 32 * rho:32 * rho + 32], lhsT=v_ap,
                                    rhs=pt[:, 32 * g:32 * g + 32], start=(bi == 0), stop=(bi == nb - 1)),
                                    reads=[u["ptsk"]] + v16keys, writes=[pok])
                        lse_rows(u["st"], u["stk"], 64, pl, plk, [(64 * g, 32 * rho, 32, 32 * g) for g in range(2)])
                    units.append(dict(q=qparts, qkeys=qk, kvs=kvs, kvkeys=kkeys + v16keys, masks=masks, nq=64, pv=pv))
                units[-1]["post"] = evac_post(po, pok, pl, plk, 2)
                po, pok = self.long_ps()
                pl, plk = self.long_ps()
                for g in range(2):
                    for qb in range(4):
                        kvs, vkeys, masks = [], [], []
                        tq = t0 + 128 * qb
                        if not (c == 0 and qb == 0):
                            if qb > 0:
                                vp = self.V1[:, 4 * par + qb - 1, 64 * kv:64 * kv + 64]
                                vkeys.append(("V1", 4 * par + qb - 1))
                            else:
                                vp = self.V1[:, 4 * (1 - par) + 3, 64 * kv:64 * kv + 64]
                                vkeys.append(("V1", 4 * (1 - par) + 3))
                            kvs.append((self.Kc[kb:kb + 64, kvi, tq - 128:tq], vp))
                            masks.append(self.masks[:, 1, 0:128])
                        kvs.append((self.Kc[kb:kb + 64, kvi, tq:tq + 128], self.V1[:, 4 * par + qb, 64 * kv:64 * kv + 64]))
                        vkeys.append(("V1", 4 * par + qb))
                        masks.append(self.masks[:, 1, 128:256])

                        def pv(u, po=po, pok=pok, pl=pl, plk=plk, g=g, qb=qb, vkeys=vkeys):
                            res = u["res"]
                            for bi, (pt, v_ap) in enumerate(res):
                                add("tensor", lambda e, pt=pt, v_ap=v_ap, bi=bi, nb=len(res): e.matmul(
                                    po[64 * g:64 * g + 64, 128 * qb:128 * qb + 128], lhsT=v_ap, rhs=pt, start=(bi == 0),
                                    stop=(bi == nb - 1)), reads=[u["ptsk"]] + vkeys, writes=[pok])
                            lse_rows(u["st"], u["stk"], 128, pl, plk, [(64 * g, 128 * qb, 128, 0)])
                        units.append(dict(q=q1[kb:kb + 64, qslot(0, g), 128 * qb:128 * qb + 128],
                                          qkeys=[("q1", qslot(0, g))], kvs=kvs, kvkeys=kkeys + vkeys, masks=masks,
                                          nq=128, pv=pv, mask_all=(self.masks[:, 1, :] if len(kvs) == 2 else None)))
                units[-1]["post"] = evac_post(po, pok, pl, plk, 0)
                po, pok = self.long_ps()
                pl, plk = self.long_ps()
                for g in range(2):
                    for rho in range(4):
                        kvs, vkeys, masks = [], [], []
                        if c > 0:
                            kvs.append((self.Kc[kb:kb + 64, kvi, t0 - T + rho:t0:4],
                                        self.V4[:, 4 * (1 - par) + rho, 64 * kv:64 * kv + 64]))
                            vkeys.append(("V4", 4 * (1 - par) + rho))
                            masks.append(self.masks[:, 1, 0:128])
                        kvs.append((self.Kc[kb:kb + 64, kvi, t0 + rho:t0 + T:4],
                                    self.V4[:, 4 * par + rho, 64 * kv:64 * kv + 64]))
                        vkeys.append(("V4", 4 * par + rho))
                        masks.append(self.masks[:, 1, 128:256])

                        def pv(u, po=po, pok=pok, pl=pl, plk=plk, g=g, rho=rho, vkeys=vkeys):
                            res = u["res"]
                            for bi, (pt, v_ap) in enumerate(res):
                                add("tensor", lambda e, pt=pt, v_ap=v_ap, bi=bi, nb=len(res): e.matmul(
                                    po[64 * g:64 * g + 64, 128 * rho:128 * rho + 128], lhsT=v_ap, rhs=pt,
                                    start=(bi == 0), stop=(bi == nb - 1)), reads=[u["ptsk"]] + vkeys, writes=[pok])
                            lse_rows(u["st"], u["stk"], 128, pl, plk, [(64 * g, 128 * rho, 128, 0)])
                        units.append(dict(q=q1[kb:kb + 64, qslot(1, g), rho:T:4], qkeys=[("q1", qslot(1, g))], kvs=kvs,
                                          kvkeys=kkeys + vkeys, masks=masks, nq=128, pv=pv,
                                          mask_all=(self.masks[:, 1, :] if len(kvs) == 2 else None)))
                units[-1]["post"] = evac_post(po, pok, pl, plk, 1)
                self.attn_pipeline(units)
                def v4(a):
                    return a.rearrange("p (i r) -> p i r", r=4)

                def s4(a):
                    return a.rearrange("p (r i) -> p i r", r=4)

                def v16(a):
                    return a.rearrange("p (q r) -> p q r", r=16)

                def s16(a):
                    return a.rearrange("p (r q) -> p q r", r=16)
                add("vector", lambda e: e.tensor_tensor(out=v4(tA), in0=v4(lb[0]), in1=s4(lb[1]), op=ALU.max),
                    reads=["a_l0", "a_l1"], writes=["a_tA"])
                add("vector", lambda e: e.tensor_tensor(out=v16(tA), in0=v16(tA), in1=s16(lb[2]), op=ALU.max),
                    reads=["a_tA", "a_l2"], writes=["a_tA"])
                add("gpsimd", lambda e: e.tensor_tensor(out=lb[0], in0=lb[0], in1=tA, op=ALU.subtract),
                    reads=["a_l0", "a_tA"], writes=["a_l0"])
                add("gpsimd", lambda e: e.tensor_tensor(out=s4(lb[1]), in0=s4(lb[1]), in1=v4(tA), op=ALU.subtract),
                    reads=["a_l1", "a_tA"], writes=["a_l1"])
                add("vector", lambda e: e.tensor_tensor(out=s16(lb[2]), in0=s16(lb[2]), in1=v16(tA), op=ALU.subtract),
                    reads=["a_l2", "a_tA"], writes=["a_l2"])
                for i in range(3):
                    add("scalar", lambda e, i=i: e.activation(out=lb[i], in_=lb[i], func=AF.Exp), reads=["a_l%d" % i],
                        writes=["a_l%d" % i])
                    add("gpsimd" if i < 2 else "vector", lambda e, i=i: e.tensor_tensor(out=ob[i], in0=ob[i], in1=lb[i],
                                                                                      op=ALU.mult),
                        reads=["a_l%d" % i, "a_o%d" % i], writes=["a_o%d" % i])
                add("vector", lambda e: e.tensor_tensor(out=v4(tA), in0=v4(lb[0]), in1=s4(lb[1]), op=ALU.add),
                    reads=["a_l0", "a_l1", "a_tA"], writes=["a_tA"])
                add("vector", lambda e: e.tensor_tensor(out=v16(tA), in0=v16(tA), in1=s16(lb[2]), op=ALU.add),
                    reads=["a_tA", "a_l2"], writes=["a_tA"])
                add("vector", lambda e: e.reciprocal(out=tA, in_=tA), reads=["a_tA"], writes=["a_tA"])
                add("gpsimd", lambda e: e.tensor_tensor(out=v4(ob[0]), in0=v4(ob[0]), in1=s4(ob[1]), op=ALU.add),
                    reads=["a_o0", "a_o1"], writes=["a_o0"])
                add("gpsimd", lambda e: e.tensor_tensor(out=v16(ob[0]), in0=v16(ob[0]), in1=s16(ob[2]), op=ALU.add),
                    reads=["a_o0", "a_o2"], writes=["a_o0"])
                add("vector", lambda e, kv=kv: e.tensor_tensor(out=self.mix[:, 6 + kv, :], in0=ob[0], in1=tA,
                                                               op=ALU.mult), reads=["a_o0", "a_tA"],
                    writes=[("mix", 6 + kv)])

    def w_out_ln(self, wname, kc, gname, bname):
        add = self.add
        mkeys = [("mix", n) for n in range(kc)]

        def consume(n, ps, pk):
            add("vector", lambda e: e.scalar_tensor_tensor(out=self.xres[:, n, :], in0=self.xres[:, n, :], scalar=ALPHA,
                                                           in1=ps, op0=ALU.mult, op1=ALU.add),
                reads=[pk, ("xres", n)], writes=[("xres", n)])
        self.proj(wname, kc, [(128 * n, 128) for n in range(16)], self.mix, mkeys, consume)
        self.layer_norm(gname, bname)

    def ffn_carve(self):
        if not hasattr(self, "ffn_tmp"):
            pass

    def tile(self, s, c):
        add = self.add
        t0 = c * T
        src = self.din["xT"][s, :, t0:t0 + T].rearrange("(k p) t -> p k t", p=128)
        add("scalar", lambda e: e.dma_start(out=self.xres, in_=src), writes=[("xres", n) for n in range(16)], ndma=1,
            lane="xres")
        for n in range(16):
            add("vector" if n % 2 else "scalar",
                (lambda e, n=n: e.tensor_copy(out=self.xbf[:, n, :], in_=self.xres[:, n, :])) if n % 2 else
                (lambda e, n=n: e.activation(out=self.xbf[:, n, :], in_=self.xres[:, n, :], func=AF.Identity)),
                reads=[("xres", n)], writes=[("xbf", n)])
        if s == 0 and c == 0 and self.layers == 2:
            for nm, _, _ in BIGW:
                if nm.startswith("l1"):
                    self.convert(nm)
        if c == 0:
            add("gpsimd", lambda e: e.memset(self.xa_halo, 0.0), writes=[("xa_halo", j) for j in range(8)])
            add("gpsimd", lambda e: e.memset(self.lru_h, 0.0), writes=[("lru_h", j) for j in range(8)])
            for l in range(2):
                add("gpsimd", lambda e, l=l: e.memset(self.fhalo[l], 0.0),
                    writes=[("fhalo", l, cc) for cc in range(88)])
        self.barrier()
        self.layer0_mixer(c)
        if self.dbg and s == 0 and c == 0:
            self.debug_store("dbg_mix0", self.mix, [("mix", n) for n in range(16)])
        self.w_out_ln("l0_w_out", 16, "l0_ln1_g", "l0_ln1_b")
        if self.dbg and s == 0 and c == 0:
            self.debug_store("dbg_x1_0", self.xres, [("xres", n) for n in range(16)])
        self.barrier()
        if s == 0 and self.layers == 2:
            tg = self.table_gen(c, t0)

            def tg_hook():
                next(tg, None)
                next(tg, None)
            self.ffn(0, hook=tg_hook)
            for _ in tg:
                pass
        else:
            self.ffn(0)
        last = (self.layers == 1) or self.stop == "prologue"
        self.layer_norm("l0_ln2_g", "l0_ln2_b", store_out=(self.store_chunk(s, t0) if last else None))
        if last:
            return
        if self.dbg and s == 0 and c == 0:
            self.debug_store("dbg_x2_0", self.xres, [("xres", n) for n in range(16)])
        if c == 0:
            add("gpsimd", lambda e: e.memset(self.s5_state, 0.0), writes=[("s5_state", g) for g in range(48)])
        self.barrier()
        if self.stop != "s5z":
            self.layer1_s5(c, t0)
        if self.stop in ("s5", "s5a", "s5z", "s5b"):
            self.debug_store("dbg_mix1", self.mix[:, 0:6, :], [("mix", n) for n in range(6)])
            for n in range(16):
                self.store_chunk(s, t0)(n)
            return
        self.barrier()
        self.layer1_attn(c, t0)
        if self.dbg and s == 0 and c == 0:
            self.debug_store("dbg_mix1", self.mix[:, 0:10, :], [("mix", n) for n in range(10)])
        self.w_out_ln("l1_w_out", 10, "l1_ln1_g", "l1_ln1_b")
        if self.dbg and s == 0 and c == 0:
            self.debug_store("dbg_x1_1", self.xres, [("xres", n) for n in range(16)])
        self.barrier()
        self.ffn(1)
        self.layer_norm("l1_ln2_g", "l1_ln2_b", store_out=self.store_chunk(s, t0))

    def store_chunk(self, s, t0):
        def f(n):
            dst = self.outT[s, 128 * n:128 * n + 128, t0:t0 + T]
            self.finals.append(self.add("gpsimd", lambda e: e.dma_start(out=dst, in_=self.xres[:, n, :]),
                                        reads=[("xres", n)], writes=[("out", n)], ndma=1, lane="out"))
        return f

    def build(self):
        self.setup()
        if self.layers == 2:
            self.s5_prologue()
        self.barrier()
        for s in range(self.nseq):
            for c in range(self.nt):
                self.tile(s, c)
        nsem = self.S.emit(final_waits=self.finals)
        return nsem


_CACHE = {}


def kernel(**inputs):
    hp = host_prepare(inputs)
    x = np.asarray(inputs["x"], np.float32)
    ncores = 8
    nseq = x.shape[0] // ncores
    key = "full"
    if key not in _CACHE:
        b = Builder(nseq=nseq, nt=4, layers=2)
        b.build()
        _CACHE[key] = b
    b = _CACHE[key]
    shared = {nm: np.ascontiguousarray(np.asarray(inputs[nm], np.float32)) for nm, _, _ in BIGW}
    shared.update(hp)
    in_maps = []
    for i in range(ncores):
        m = dict(shared)
        m["xT"] = np.ascontiguousarray(x[i * nseq:(i + 1) * nseq].transpose(0, 2, 1))
        in_maps.append(m)
    res = run_bass_kernel_spmd(b.nc, in_maps, core_ids=list(range(ncores)))
    out = np.empty_like(x)
    for i in range(ncores):
        out[i * nseq:(i + 1) * nseq] = res.results[i]["outT"].transpose(0, 2, 1)
    return out
```
